# Optimizing a Trainium2 kernel written in Bass

```python
import jax, jax.numpy as jnp
from jax import lax
import numpy as np

D_MODEL = 1024
BATCH = 4
SEQ = 4096
DEPTH = 4
DEC_BATCH = 32
DEC_SEQ = 1
PAST_LEN = 8192
PAGE_SIZE = 128

HEAD_DIM = 64
HEADS_PER_GROUP = D_MODEL // HEAD_DIM
WINDOWS = (128, 512, 2048)
DILATIONS = (1, 4, 16)
N_GROUPS = len(WINDOWS)
SPAN = WINDOWS[0] // DILATIONS[0]
BLOCK = SPAN
N_HEADS_TOTAL = N_GROUPS * HEADS_PER_GROUP
N_BUCKETS = 32
MAX_DISTANCE = WINDOWS[-1]
POOL_WINDOWS = (2, 4, 8, 16)
POOL_GROUP_WIDTH = D_MODEL // len(POOL_WINDOWS)
POOL_STATE = max(POOL_WINDOWS) - 1
D_FF = 11 * D_MODEL // 4
CONV_WIDTH = 3
N_ATTN_LAYERS = (DEPTH + 1) // 2
N_POOL_LAYERS = DEPTH // 2
EPS = 1e-6
NEG_INF = -1e30
SCALE = HEAD_DIM ** -0.5

kernel_name = "dilated_attn_pool_convglu_hybrid_step"


def rmsnorm(x, g):
    xf = x.astype(jnp.float32)
    y = xf * lax.rsqrt(jnp.mean(xf * xf, axis=-1, keepdims=True) + EPS)
    return (y * g.astype(jnp.float32)).astype(x.dtype)


def t5_buckets(dist):
    max_exact = N_BUCKETS // 2
    n = np.maximum(dist, 1).astype(np.float32)
    large = max_exact + (np.log(n / max_exact) / np.log(MAX_DISTANCE / max_exact)
                         * (N_BUCKETS - max_exact)).astype(np.int32)
    large = np.minimum(large, N_BUCKETS - 1)
    return np.where(dist < max_exact, dist, large).astype(np.int32)


def group_bias(rel_bias, g):
    dist = np.arange(SPAN + 1) * DILATIONS[g]
    b = rel_bias[t5_buckets(dist)]
    return b[:, g * HEADS_PER_GROUP:(g + 1) * HEADS_PER_GROUP].T.astype(jnp.float32)


def banded_attention(q, k, v, bias):
    n, L, H, hd = q.shape
    nb = L // BLOCK
    qb = q.reshape(n, nb, BLOCK, H, hd)

    def band(a):
        a = jnp.concatenate([jnp.zeros((n, BLOCK, H, hd), a.dtype), a], axis=1)
        a = a.reshape(n, nb + 1, BLOCK, H, hd)
        return jnp.concatenate([a[:, :-1], a[:, 1:]], axis=2)

    kb, vb = band(k), band(v)
    qi = np.arange(BLOCK)[:, None]
    kj = np.arange(2 * BLOCK)[None, :]
    rel = qi - kj + BLOCK
    key_pos = np.arange(nb)[:, None, None] * BLOCK - BLOCK + kj[None]
    valid = (rel >= 0) & (rel <= SPAN) & (key_pos >= 0)
    s = jnp.einsum("nbqhd,nbkhd->nbhqk", qb, kb).astype(jnp.float32) * SCALE
    s = s + bias[:, np.clip(rel, 0, SPAN)]
    s = jnp.where(valid[None, :, None], s, NEG_INF)
    m = jnp.max(s, axis=-1)
    p = jnp.exp(s - m[..., None])
    l = jnp.sum(p, axis=-1)
    o = jnp.einsum("nbhqk,nbkhd->nbqhd", p.astype(v.dtype), vb).astype(jnp.float32)
    m = jnp.swapaxes(m, 2, 3)
    l = jnp.swapaxes(l, 2, 3)
    o = o / l[..., None]
    return o.reshape(n, L, H, hd), m.reshape(n, L, H), l.reshape(n, L, H)


def dilated_prompt(q, k, v, dil, bias):
    B, T, H, hd = q.shape
    tc = T // dil
    lp = -(-tc // BLOCK) * BLOCK

    def to_stream(a):
        a = a.reshape(B, tc, dil, H, hd).transpose(0, 2, 1, 3, 4).reshape(B * dil, tc, H, hd)
        return jnp.pad(a, ((0, 0), (0, lp - tc), (0, 0), (0, 0)))

    def from_stream(a):
        a = a[:, :tc].reshape((B, dil, tc) + a.shape[2:])
        return jnp.swapaxes(a, 1, 2).reshape((B, T) + a.shape[3:])

    o, m, l = banded_attention(to_stream(q), to_stream(k), to_stream(v), bias)
    return from_stream(o), from_stream(m), from_stream(l)


def dilated_sample(q, k_new, v_new, k_cache, v_cache, dil, bias):
    lw = k_cache.shape[1]
    S = q.shape[1]
    kc = jnp.concatenate([k_cache, k_new], axis=1)
    vc = jnp.concatenate([v_cache, v_new], axis=1)
    idx = lw + np.arange(S)[:, None] - np.arange(SPAN + 1)[None, :] * dil
    valid = idx >= 0
    idx_c = np.maximum(idx, 0)
    kg = kc[:, idx_c]
    vg = vc[:, idx_c]
    s = jnp.einsum("bshd,bsjhd->bshj", q, kg).astype(jnp.float32) * SCALE + bias
    s = jnp.where(valid[None, :, None, :], s, NEG_INF)
    m = jnp.max(s, axis=-1)
    p = jnp.exp(s - m[..., None])
    l = jnp.sum(p, axis=-1)
    o = jnp.einsum("bshj,bsjhd->bshd", p.astype(vg.dtype), vg).astype(jnp.float32) / l[..., None]
    return o, m, l


def merge_groups(outs):
    o = jnp.stack([t[0] for t in outs])
    m = jnp.stack([t[1] for t in outs])
    l = jnp.stack([t[2] for t in outs])
    w = l * jnp.exp(m - jnp.max(m, axis=0))
    w = w / jnp.sum(w, axis=0)
    return jnp.einsum("gbthd,gbth->bthd", o, w)


def pool_mix(h_ext, n_prefix, w_pool, scale):
    hf = h_ext.astype(jnp.float32)
    c = jnp.pad(jnp.cumsum(hf, axis=1), ((0, 0), (1, 0), (0, 0)))
    r = np.arange(n_prefix, h_ext.shape[1])
    outs = []
    for g, w in enumerate(POOL_WINDOWS):
        sl = slice(g * POOL_GROUP_WIDTH, (g + 1) * POOL_GROUP_WIDTH)
        lo = np.maximum(r + 1 - w, 0)
        cnt = (r + 1 - lo).astype(np.float32)
        mean = (c[:, r + 1, sl] - c[:, lo, sl]) / cnt[None, :, None]
        z = (mean - hf[:, r, sl]).astype(h_ext.dtype)
        outs.append(jnp.einsum("bsc,ce->bse", z, w_pool[g]))
    return jnp.concatenate(outs, axis=-1) * scale


def conv_glu(h, gate_prefix, w_in, conv_w, conv_b, w_out):
    u = h @ w_in
    gate, val = u[..., :D_FF], u[..., D_FF:]
    ext = jnp.concatenate([gate_prefix, gate], axis=1)
    S = h.shape[1]
    conv = conv_b + ext[:, 0:S] * conv_w[0]
    for j in range(1, CONV_WIDTH):
        conv = conv + ext[:, j:j + S] * conv_w[j]
    y = (jax.nn.silu(conv) * val) @ w_out
    return y, ext[:, ext.shape[1] - (CONV_WIDTH - 1):]


def setup_inputs(seed: int = 0) -> dict:
    key = jax.random.key(seed)
    ks = jax.random.split(key, 24)
    f32 = jnp.float32

    def nrm(k, shape, s=1.0):
        return jax.random.normal(k, shape, f32) * s

    def kv_shape(w):
        return (N_ATTN_LAYERS, DEC_BATCH, min(w, PAST_LEN), HEADS_PER_GROUP, HEAD_DIM)

    return {
        "x_prompt": nrm(ks[0], (BATCH, SEQ, D_MODEL)),
        "x_sample": nrm(ks[1], (DEC_BATCH, DEC_SEQ, D_MODEL)),
        "cache_k_w128": nrm(ks[2], kv_shape(WINDOWS[0])),
        "cache_v_w128": nrm(ks[3], kv_shape(WINDOWS[0])),
        "cache_k_w512": nrm(ks[4], kv_shape(WINDOWS[1])),
        "cache_v_w512": nrm(ks[5], kv_shape(WINDOWS[1])),
        "cache_k_w2048": nrm(ks[6], kv_shape(WINDOWS[2])),
        "cache_v_w2048": nrm(ks[7], kv_shape(WINDOWS[2])),
        "state_pool": nrm(ks[8], (N_POOL_LAYERS, DEC_BATCH, POOL_STATE, D_MODEL)),
        "state_conv": nrm(ks[9], (DEPTH, DEC_BATCH, CONV_WIDTH - 1, D_FF)),
        "rel_bias": nrm(ks[10], (N_BUCKETS, N_HEADS_TOTAL), 0.2),
        "norm_mix": 1.0 + nrm(ks[11], (DEPTH, D_MODEL), 0.02),
        "w_qkv": nrm(ks[12], (N_ATTN_LAYERS, D_MODEL, N_GROUPS * 3 * HEADS_PER_GROUP * HEAD_DIM), D_MODEL ** -0.5),
        "w_o": nrm(ks[13], (N_ATTN_LAYERS, HEADS_PER_GROUP * HEAD_DIM, D_MODEL), D_MODEL ** -0.5),
        "w_pool": nrm(ks[14], (N_POOL_LAYERS, len(POOL_WINDOWS), POOL_GROUP_WIDTH, POOL_GROUP_WIDTH), POOL_GROUP_WIDTH ** -0.5),
        "pool_scale": 1.0 + nrm(ks[15], (N_POOL_LAYERS, D_MODEL), 0.02),
        "norm_ffn": 1.0 + nrm(ks[16], (DEPTH, D_MODEL), 0.02),
        "w_in": nrm(ks[17], (DEPTH, D_MODEL, 2 * D_FF), D_MODEL ** -0.5),
        "conv_w": nrm(ks[18], (DEPTH, CONV_WIDTH, D_FF), CONV_WIDTH ** -0.5),
        "conv_b": nrm(ks[19], (DEPTH, D_FF), 0.01),
        "w_out": nrm(ks[20], (DEPTH, D_FF, D_MODEL), D_FF ** -0.5),
        "norm_final": 1.0 + nrm(ks[21], (D_MODEL,), 0.02),
    }


def reference(x_prompt, x_sample, cache_k_w128, cache_v_w128, cache_k_w512, cache_v_w512,
              cache_k_w2048, cache_v_w2048, state_pool, state_conv, rel_bias, norm_mix,
              w_qkv, w_o, w_pool, pool_scale, norm_ffn, w_in, conv_w, conv_b, w_out, norm_final):
    cache_k = (cache_k_w128, cache_k_w512, cache_k_w2048)
    cache_v = (cache_v_w128, cache_v_w512, cache_v_w2048)
    biases = [group_bias(rel_bias, g) for g in range(N_GROUPS)]
    bp, T, _ = x_prompt.shape
    bs, S, _ = x_sample.shape
    nk_p = [[] for _ in range(N_GROUPS)]
    nv_p = [[] for _ in range(N_GROUPS)]
    nk_s = [[] for _ in range(N_GROUPS)]
    nv_s = [[] for _ in range(N_GROUPS)]
    pool_p, pool_s, conv_p, conv_s = [], [], [], []
    xp, xs = x_prompt, x_sample
    for i in range(DEPTH):
        if i % 2 == 0:
            a = i // 2
            qkv_p = (rmsnorm(xp, norm_mix[i]) @ w_qkv[a]).reshape(bp, T, N_GROUPS, 3, HEADS_PER_GROUP, HEAD_DIM)
            qkv_s = (rmsnorm(xs, norm_mix[i]) @ w_qkv[a]).reshape(bs, S, N_GROUPS, 3, HEADS_PER_GROUP, HEAD_DIM)
            outs_p, outs_s = [], []
            for g in range(N_GROUPS):
                qp, kp, vp = qkv_p[:, :, g, 0], qkv_p[:, :, g, 1], qkv_p[:, :, g, 2]
                qs, ksn, vsn = qkv_s[:, :, g, 0], qkv_s[:, :, g, 1], qkv_s[:, :, g, 2]
                outs_p.append(dilated_prompt(qp, kp, vp, DILATIONS[g], biases[g]))
                outs_s.append(dilated_sample(qs, ksn, vsn, cache_k[g][a], cache_v[g][a], DILATIONS[g], biases[g]))
                keep = min(WINDOWS[g], T)
                nk_p[g].append(kp[:, T - keep:])
                nv_p[g].append(vp[:, T - keep:])
                nk_s[g].append(ksn)
                nv_s[g].append(vsn)
            xp = xp + merge_groups(outs_p).astype(xp.dtype).reshape(bp, T, D_MODEL) @ w_o[a]
            xs = xs + merge_groups(outs_s).astype(xs.dtype).reshape(bs, S, D_MODEL) @ w_o[a]
        else:
            b = i // 2
            hp = rmsnorm(xp, norm_mix[i])
            hs = rmsnorm(xs, norm_mix[i])
            hs_ext = jnp.concatenate([state_pool[b].astype(hs.dtype), hs], axis=1)
            xp = xp + pool_mix(hp, 0, w_pool[b], pool_scale[b])
            xs = xs + pool_mix(hs_ext, POOL_STATE, w_pool[b], pool_scale[b])
            pool_p.append(hp[:, T - POOL_STATE:])
            pool_s.append(hs_ext[:, hs_ext.shape[1] - POOL_STATE:])
        hp = rmsnorm(xp, norm_ffn[i])
        hs = rmsnorm(xs, norm_ffn[i])
        yp_i, cp = conv_glu(hp, jnp.zeros((bp, CONV_WIDTH - 1, D_FF), hp.dtype), w_in[i], conv_w[i], conv_b[i], w_out[i])
        ys_i, cs = conv_glu(hs, state_conv[i].astype(hs.dtype), w_in[i], conv_w[i], conv_b[i], w_out[i])
        xp = xp + yp_i
        xs = xs + ys_i
        conv_p.append(cp)
        conv_s.append(cs)
    y_prompt = rmsnorm(xp, norm_final)
    y_sample = rmsnorm(xs, norm_final)
    new_k_w128_prompt = jnp.stack(nk_p[0])
    new_v_w128_prompt = jnp.stack(nv_p[0])
    new_k_w512_prompt = jnp.stack(nk_p[1])
    new_v_w512_prompt = jnp.stack(nv_p[1])
    new_k_w2048_prompt = jnp.stack(nk_p[2])
    new_v_w2048_prompt = jnp.stack(nv_p[2])
    new_k_w128_sample = jnp.stack(nk_s[0])
    new_v_w128_sample = jnp.stack(nv_s[0])
    new_k_w512_sample = jnp.stack(nk_s[1])
    new_v_w512_sample = jnp.stack(nv_s[1])
    new_k_w2048_sample = jnp.stack(nk_s[2])
    new_v_w2048_sample = jnp.stack(nv_s[2])
    new_state_pool_prompt = jnp.stack(pool_p)
    new_state_pool_sample = jnp.stack(pool_s)
    new_state_conv_prompt = jnp.stack(conv_p)
    new_state_conv_sample = jnp.stack(conv_s)
    return (y_prompt, y_sample,
            new_k_w128_prompt, new_v_w128_prompt, new_k_w512_prompt, new_v_w512_prompt,
            new_k_w2048_prompt, new_v_w2048_prompt,
            new_k_w128_sample, new_v_w128_sample, new_k_w512_sample, new_v_w512_sample,
            new_k_w2048_sample, new_v_w2048_sample,
            new_state_pool_prompt, new_state_pool_sample,
            new_state_conv_prompt, new_state_conv_sample)
```

```python
from contextlib import ExitStack
import numpy as np
import concourse.bass as bass
import concourse.mybir as mybir
from concourse.bass_utils import run_bass_kernel_spmd

F32 = mybir.dt.float32
BF16 = mybir.dt.bfloat16
ALU = mybir.AluOpType
AF = mybir.ActivationFunctionType
AX = mybir.AxisListType

D = 1024
KC = 8
NT = 4096
ST = 2048
NST = NT // ST
NS = 4
DFF = 2816
FC = 22
FT = 256
VW = 144
WINS = (128, 512, 2048)
DILS = (1, 4, 16)
EPS = 1e-6
NEG = -30000.0
NPV = 440
PV_NM, PV_NF, PV_FIN, PV_PSC, PV_CW, PV_CB = 0, 32, 64, 72, 88, 352


DEBUG_STOP = None


class _Stop(Exception):
    pass


class Trk:
    def __init__(self, nc):
        self.nc = nc
        self.E = dict(pe=nc.tensor, act=nc.scalar, dve=nc.vector, pool=nc.gpsimd, sp=nc.sync)
        self.sem = {}
        self.cnt = {}
        for n in ["pe", "act", "dve", "pool", "q_sp", "q_act", "q_pool"]:
            self.sem[n] = nc.alloc_semaphore("s_" + n)
            self.cnt[n] = 0
        self.waited = {}
        self.lastw = {}
        self.readers = {}

    def _wait(self, eng, deps):
        best = {}
        for (s, v) in deps:
            if eng == "pe" and s == "pe":
                continue
            if best.get(s, 0) < v:
                best[s] = v
        for s, v in best.items():
            if s.startswith("q_"):
                v = self.cnt[s]
            if self.waited.get((eng, s), 0) < v:
                self.E[eng].wait_ge(self.sem[s], v)
                self.waited[(eng, s)] = v

    def op(self, eng, fn, reads=(), writes=(), dma=False):
        reads = list(dict.fromkeys(reads))
        writes = list(dict.fromkeys(list(writes) + [r for r in reads if isinstance(r, str) and r[0] == "P" and r[1:].isdigit()]))
        deps = set()
        for r in reads:
            if r in self.lastw:
                deps.add(self.lastw[r])
        for w in writes:
            if w in self.lastw:
                deps.add(self.lastw[w])
            for t in self.readers.get(w, ()):
                deps.add(t)
        self._wait(eng, deps)
        ins = fn()
        if dma:
            s = "q_" + eng
            self.cnt[s] += 16
            ins.then_inc(self.sem[s], 16)
        else:
            s = eng
            self.cnt[s] += 1
            ins.then_inc(self.sem[s], 1)
        tok = (s, self.cnt[s])
        for w in writes:
            self.lastw[w] = tok
            self.readers[w] = []
        for r in reads:
            if r in writes:
                continue
            lst = self.readers.setdefault(r, [])
            lst.append(tok)
            if len(lst) > 48:
                best = {}
                for (s2, v2) in lst:
                    if best.get(s2, 0) < v2:
                        best[s2] = v2
                self.readers[r] = list(best.items())
        return ins

    def drain(self, eng="sp"):
        for s, v in self.cnt.items():
            if v > 0 and self.waited.get((eng, s), 0) < v:
                self.E[eng].wait_ge(self.sem[s], v)
                self.waited[(eng, s)] = v


def t5_buckets(dist):
    n_b, max_d = 32, 2048
    max_exact = n_b // 2
    n = np.maximum(dist, 1).astype(np.float32)
    large = max_exact + (np.log(n / max_exact) / np.log(max_d / max_exact) * (n_b - max_exact)).astype(np.int32)
    large = np.minimum(large, n_b - 1)
    return np.where(dist < max_exact, dist, large).astype(np.int32)


def host_consts():
    c = {}
    ohpad = np.zeros((3, 32, 384), np.float32)
    ohs = np.zeros((3, 32, 128), np.float32)
    for g in range(3):
        bk = t5_buckets(np.arange(129) * DILS[g])
        for rel in range(129):
            ohpad[g, bk[rel], 127 + rel] = 1.0
        for p in range(128):
            ohs[g, bk[128 - p], p] = 1.0
    maskpad = np.full((16, 384), NEG, np.float32)
    maskpad[:, 127:256] = 0.0
    selh = np.zeros((16, 16, 128), np.float32)
    for h in range(16):
        selh[h, h, :] = 1.0
    oh0 = np.zeros((32, NS), np.float32)
    oh0[0, :] = 1.0
    sel4 = np.zeros((NS, NS, 128), np.float32)
    for b in range(NS):
        sel4[b, b, :] = 1.0
    c["ohpad"] = ohpad
    c["ohs"] = ohs
    c["maskpad"] = maskpad
    c["selh"] = selh.reshape(16, 16 * 128)
    c["oh0"] = oh0
    c["sel4"] = sel4.reshape(NS, NS * 128)
    c["ident"] = np.eye(128, dtype=np.float32)
    pc = np.ones((3, 128, 16), np.float32)
    for gi, w in enumerate((2, 4, 8)):
        pass
    pcorr = np.ones((4, 128, 16), np.float32)
    for gi, w in enumerate((2, 4, 8, 16)):
        for t in range(16):
            pcorr[gi, :, t] = w / min(w, t + 1)
    c["pcorr"] = np.ascontiguousarray(pcorr.transpose(1, 0, 2)).reshape(128, 64)
    return c


def build():
    nc = bass.Bass("TRN2", target_bir_lowering=False)
    T = Trk(nc)
    _uid = [0]

    def chk(name):
        if DEBUG_STOP is not None and name == DEBUG_STOP:
            raise _Stop()

    def un(name):
        _uid[0] += 1
        return f"sb_{name}_{_uid[0]}"

    def din(name, shape):
        return nc.dram_tensor(name, list(shape), F32, kind="ExternalInput")

    def dout(name, shape):
        return nc.dram_tensor(name, list(shape), F32, kind="ExternalOutput")

    xT_in = din("xT", [D, NT]).ap()
    xsT_in = din("xsT", [D, NS]).ap()
    ck = [din(f"ck{g}", [2, NS, 128, D]).ap() for g in range(3)]
    cv = [din(f"cv{g}", [2, NS, 128, D]).ap() for g in range(3)]
    spoolT_in = din("spoolT", [2, 128, KC, NS, 15]).ap()
    spool_in = din("spool", [2, NS, 15, D]).ap()
    sconvT_in = din("sconvT", [4, 128, 2, FC, NS]).ap()
    sconv_in = din("sconv", [4, NS, 2, DFF]).ap()
    relb_in = din("rel_bias", [32, 48]).ap()
    pvec_in = din("pvec", [128, NPV]).ap()
    wqkv_in = din("w_qkv", [2, D, 9 * D]).ap()
    wo_in = din("w_o", [2, D, D]).ap()
    wpool_in = din("w_pool", [2, 4, 256, 256]).ap()
    win_in = din("w_in", [4, D, 2 * DFF]).ap()
    wout_in = din("w_out", [4, DFF, D]).ap()
    ohpad_in = din("ohpad", [3, 32, 384]).ap()
    ohs_in = din("ohs", [3, 32, 128]).ap()
    maskpad_in = din("maskpad", [16, 384]).ap()
    selh_in = din("selh", [16, 16 * 128]).ap()
    oh0_in = din("oh0", [32, NS]).ap()
    sel4_in = din("sel4", [NS, NS * 128]).ap()
    ident_in = din("ident", [128, 128]).ap()
    pcorr_in = din("pcorr", [128, 64]).ap()

    yT = dout("yT", [D, NT]).ap()
    ysT = dout("ysT", [D, NS]).ap()
    kpT = [dout(f"kp{g}T", [2, D, WINS[g]]).ap() for g in range(3)]
    vp = [dout(f"vp{g}", [2, WINS[g], D]).ap() for g in range(3)]
    qkvs_out = dout("qkvs", [2, NS, 9 * D]).ap()
    poolpT = dout("poolpT", [2, D, 15]).ap()
    poolsT = dout("poolsT", [2, D, NS]).ap()
    pools14 = dout("pools14", [2, NS, 14, D]).ap()
    convpT = dout("convpT", [4, DFF, 2]).ap()
    convsT = dout("convsT", [4, DFF, NS]).ap()
    convs0 = dout("convs0", [4, NS, DFF]).ap()

    xres = nc.dram_tensor("xres", [D, NT], F32).ap()
    scrb_h = nc.dram_tensor("scrb", [48, 128, 384], F32)
    scrb = scrb_h.ap()
    kh_scr = nc.dram_tensor("kh_scr", [24, 128, 16 * 128], BF16).ap()
    vh_scr = nc.dram_tensor("vh_scr", [24, 128, 16 * VW], BF16).ap()
    qkvs_scr = nc.dram_tensor("qkvs_scr", [NS, 9 * D], F32).ap()

    ps = [nc.alloc_psum_tensor(f"ps{i}", [128, 512], F32) for i in range(4)]
    psY = nc.alloc_psum_tensor("psY", [128, 2048], F32)

    def P(bank, lo=0, hi=512):
        return [f"P{bank}"]

    def PY(lo=0, hi=2048):
        out = []
        for bk in range(4):
            l2, h2 = max(lo, bk * 512), min(hi, (bk + 1) * 512)
            if l2 < h2:
                out += P(4 + bk, l2 - bk * 512, h2 - bk * 512)
        return out

    pv = nc.alloc_sbuf_tensor(un("pv"), [128, NPV], F32)
    meanm = nc.alloc_sbuf_tensor(un("meanm"), [128, 128], BF16)
    ones32 = nc.alloc_sbuf_tensor(un("ones32"), [128, 128], F32)
    identb = nc.alloc_sbuf_tensor(un("identb"), [128, 128], BF16)
    xs = nc.alloc_sbuf_tensor(un("xs"), [128, KC, NS], F32)
    SB = nc.alloc_sbuf_tensor(un("SB"), [128, 48], F32)
    B0 = nc.alloc_sbuf_tensor(un("B0"), [NS, 48], F32)
    sel4 = nc.alloc_sbuf_tensor(un("sel4"), [NS, NS * 128], F32)
    epsc = nc.alloc_sbuf_tensor(un("epsc"), [128, 1], F32)

    V = nc.vector
    A = nc.scalar
    G = nc.gpsimd
    PE = nc.tensor
    SP = nc.sync

    def dma(eng, out, in_, reads=(), writes=(), nonc=False):
        e = {"sp": SP, "pool": G, "act": A}[eng]
        if nonc:
            return T.op(eng, lambda: e.dma_start(out=out, in_=in_, allow_slow_non_contiguous=True), reads=reads, writes=writes, dma=True)
        return T.op(eng, lambda: e.dma_start(out=out, in_=in_), reads=reads, writes=writes, dma=True)

    def mm(out, pairs, reads, writes, first=True, last=True):
        def fn():
            ins = None
            n = len(pairs)
            for i, (l, r) in enumerate(pairs):
                ins = PE.matmul(out, lhsT=l, rhs=r, start=(first and i == 0), stop=(last and i == n - 1))
            return ins
        return T.op("pe", fn, reads=reads, writes=writes)

    dma("sp", pv[:], pvec_in[:, :], writes=["pv"])
    dma("sp", xs[:], xsT_in.rearrange("(kc p) n -> p kc n", p=128), writes=["xs"])
    dma("sp", sel4[:], sel4_in[:, :], writes=["sel4"])
    dma("pool", identb[:], ident_in[:, :], writes=["identb"])
    T.op("dve", lambda: V.memset(meanm[:], 1.0 / D), writes=["meanm"])
    T.op("dve", lambda: V.memset(ones32[:], 1.0), writes=["ones32"])
    T.op("dve", lambda: V.memset(epsc[:], EPS), writes=["epsc"])

    with ExitStack() as es:
        rb = es.enter_context(nc.sbuf_tensor(un("rb"), [32, 48], F32))
        ohp = es.enter_context(nc.sbuf_tensor(un("ohp"), [32, 3, 384], F32))
        ohs = es.enter_context(nc.sbuf_tensor(un("ohs"), [32, 3, 128], F32))
        oh0 = es.enter_context(nc.sbuf_tensor(un("oh0"), [32, NS], F32))
        mpad = es.enter_context(nc.sbuf_tensor(un("mpad"), [16, 384], F32))
        selh = es.enter_context(nc.sbuf_tensor(un("selh"), [16, 16 * 128], F32))
        fpad = es.enter_context(nc.sbuf_tensor(un("fpad"), [16, 384], F32))
        rep = [es.enter_context(nc.sbuf_tensor(un(f"rep{i}"), [128, 384], F32)) for i in range(2)]
        dma("sp", rb[:], relb_in[:, :], writes=["rb"])
        dma("sp", ohp[:], ohpad_in.rearrange("g k x -> k g x"), writes=["ohp"])
        dma("sp", ohs[:], ohs_in.rearrange("g k x -> k g x"), writes=["ohs"])
        dma("sp", oh0[:], oh0_in[:, :], writes=["oh0"])
        dma("sp", mpad[:], maskpad_in[:, :], writes=["mpad"])
        dma("sp", selh[:], selh_in[:, :], writes=["selh"])
        mm(ps[0][0:NS, 0:48], [(oh0[:], rb[:])], ["oh0", "rb"], P(0))
        T.op("dve", lambda: V.tensor_copy(out=B0[:], in_=ps[0][0:NS, 0:48]), reads=P(0), writes=["B0"])
        for g in range(3):
            mm(ps[1][:, g * 16:(g + 1) * 16], [(ohs[:, g, :], rb[:, g * 16:(g + 1) * 16])], ["ohs", "rb"], P(1))
        T.op("dve", lambda: V.tensor_copy(out=SB[:], in_=ps[1][:, 0:48]), reads=P(1), writes=["SB"])
        for g in range(3):
            mm(ps[2][0:16, 0:384], [(rb[:, g * 16:(g + 1) * 16], ohp[:, g, :])], ["rb", "ohp"], P(2))
            T.op("dve", lambda: V.scalar_tensor_tensor(out=fpad[:], in0=ps[2][0:16, 0:384], scalar=8.0, in1=mpad[:],
                                                       op0=ALU.mult, op1=ALU.add), reads=P(2) + ["mpad"], writes=["fpad"])
            for h in range(16):
                i = h % 2
                mm(ps[i][:, 0:384], [(selh[:, h * 128:(h + 1) * 128], fpad[:])], ["selh", "fpad"], P(i))
                T.op("act", lambda: A.copy(out=rep[i][:], in_=ps[i][:, 0:384]), reads=P(i), writes=[f"rep{i}"])
                dma("sp", scrb[g * 16 + h], rep[i][:], reads=[f"rep{i}"], writes=["scrb"])
    T.drain("sp")
    nc.all_engine_barrier()
    try:
        chk("setup")
    except _Stop:
        return nc

    def src_x(l):
        return xT_in if l == 0 else xres

    def x_view(ap, t0, n):
        return ap.rearrange("(kc p) t -> p kc t", p=128)[:, :, t0:t0 + n]

    def rms_rstd(xt_ap, n, sq, rstd, psm, xname, tag):
        pn_ = P(0, 0, n)
        T.op("act", lambda: A.activation(out=sq, in_=xt_ap, func=AF.Square), reads=[xname], writes=["sq" + tag])
        mm(psm, [(meanm[:], sq[:, kc, :]) for kc in range(KC)], ["meanm", "sq" + tag], pn_)
        T.op("act", lambda: A.activation(out=rstd, in_=psm, func=AF.Sqrt, bias=epsc[:, 0:1]), reads=pn_ + ["epsc"], writes=["rstd" + tag])
        T.op("dve", lambda: V.reciprocal(out=rstd, in_=rstd), reads=["rstd" + tag], writes=["rstd" + tag])

    def attention_pass(l):
        a = l // 2
        src = src_x(l)
        with ExitStack() as es:
            sb = lambda name, shape, dt: es.enter_context(nc.sbuf_tensor(un(name), shape, dt))
            hT = sb("hT", [128, KC, ST], BF16)
            attnT = sb("attnT", [128, KC, ST], BF16)
            BT = sb("BT", [128, 48, 256], BF16)
            hsT = sb("hsT", [128, KC, NS], BF16)
            sqs = sb("sqs", [128, KC, NS], BF16)
            rstds = sb("rstds", [128, NS], F32)

            for g in range(3):
                skew = bass.AP(scrb_h, 127 + g * 16 * 128 * 384, [[383, 128], [128 * 384, 16], [1, 256]])
                dma("pool", BT[:, g * 16:(g + 1) * 16, :], skew, reads=["scrb"], writes=["BT"])

            rms_rstd(xs[:], NS, sqs[:], rstds[:], ps[0][:, 0:NS], "xs", "s")
            for kc in range(KC):
                T.op("dve", lambda: V.scalar_tensor_tensor(out=hsT[:, kc, :], in0=xs[:, kc, :], scalar=pv[:, PV_NM + l * 8 + kc:PV_NM + l * 8 + kc + 1],
                                                           in1=rstds[:], op0=ALU.mult, op1=ALU.mult), reads=["xs", "pv", "rstds"], writes=["hsT"])
            it = 0
            for st in range(NST):
                base = st * ST
                with ExitStack() as esA:
                    sbA = lambda name, shape, dt: esA.enter_context(nc.sbuf_tensor(un(name), shape, dt))
                    xt = sbA("xt_a", [128, KC, 512], F32)
                    sq = sbA("sq_a", [128, KC, 512], BF16)
                    rstd = sbA("rstd_a", [128, 512], F32)
                    for tt in range(4):
                        t0 = base + tt * 512
                        dma("sp", xt[:], x_view(src, t0, 512), reads=[f"xr{t0 // 256}", f"xr{t0 // 256 + 1}"], writes=["xt_a"])
                        rms_rstd(xt[:], 512, sq[:], rstd[:], ps[0][:, :], "xt_a", "a")
                        for kc in range(KC):
                            T.op("dve", lambda: V.scalar_tensor_tensor(out=hT[:, kc, tt * 512:(tt + 1) * 512], in0=xt[:, kc, :],
                                                                       scalar=pv[:, PV_NM + l * 8 + kc:PV_NM + l * 8 + kc + 1], in1=rstd[:],
                                                                       op0=ALU.mult, op1=ALU.mult),
                                 reads=["xt_a", "pv", "rstda"], writes=[f"hT{tt}"])
                    T.drain("sp")
                    nc.all_engine_barrier()
                chk(f"A{l}{st}")
                hT_all = [f"hT{tt}" for tt in range(4)]
                with ExitStack() as esB:
                    sbB = lambda name, shape, dt: esB.enter_context(nc.sbuf_tensor(un(name), shape, dt))
                    wq = [sbB(f"wq{i}", [128, KC, 128], BF16) for i in range(2)]
                    wk = [sbB(f"wk{i}", [128, KC, 128], BF16) for i in range(2)]
                    wv = [sbB(f"wv{i}", [128, KC, 128], BF16) for i in range(2)]
                    Qs = [sbB(f"Qs{i}", [128, 2, ST], BF16) for i in range(2)]
                    Ks = [sbB(f"Ks{i}", [128, 16 * 128 + ST], BF16) for i in range(2)]
                    Vx = [sbB(f"Vx{i}", [128, 32, VW], BF16) for i in range(2)]
                    pt = [sbB("pt0", [128, 17, 256], BF16), sbB("pt1", [128, 8, 256], BF16)]
                    accn = sbB("accn", [128, ST], F32)
                    accd = sbB("accd", [1, 2, ST], F32)
                    kst = [sbB("kst0", [128, 512], F32)] * 2
                    vst = [sbB("vst0", [128, 512], F32)] * 2
                    sst = [sbB(f"sst{i}", [NS, 384], F32) for i in range(2)]
                    for i in range(2):
                        T.op("pool", lambda: G.memset(Qs[i][:], 0.0), writes=[f"Qs{i}"])
                    for hp in range(8):
                        for g in range(3):
                            d = DILS[g]
                            W = WINS[g]
                            S = ST // d
                            nb = S // 128
                            par = it % 2
                            it += 1
                            slot = g * 8 + hp
                            wts = (wq[par], wk[par], wv[par])
                            wn = (f"wq{par}", f"wk{par}", f"wv{par}")
                            for s in range(3):
                                c0 = (g * 3 + s) * D + hp * 128
                                dma("pool", wts[s][:], wqkv_in[a].rearrange("(kc p) n -> p kc n", p=128)[:, :, c0:c0 + 128], writes=[wn[s]])
                            Qn, Kn, Vn = f"Qs{par}", f"Ks{par}", f"Vx{par}"
                            Ksv = Ks[par][:, 0:d * (128 + S)].rearrange("p (r s) -> p r s", r=d)
                            Qsv = [Qs[par][:, e_, :].rearrange("p (r s) -> p r s", r=d) for e_ in range(2)]
                            NH = d
                            if st == 0:
                                T.op("pool", lambda: G.memset(Ksv[:, :, 0:128], 0.0), writes=[Kn])
                                T.op("pool", lambda: G.memset(Vx[par][:, 0:NH, :], 0.0), writes=[Vn])
                            else:
                                dma("sp", Ksv[:, :, 0:128], kh_scr[slot, :, 0:d * 128].rearrange("p (r s) -> p r s", r=d), reads=[f"kh{slot}"], writes=[Kn])
                                dma("sp", Vx[par][:, 0:NH, :], vh_scr[slot, :, 0:d * VW].rearrange("p (r s) -> p r s", r=d), reads=[f"vh{slot}"], writes=[Vn])
                            T.op("pool", lambda: G.memset(Vx[par][:, NH:NH + 16, 128:VW], 1.0), writes=[Vn])
                            chk(f"p1{l}{st}{hp}{g}")
                            if st == 0:
                                sp_ = slot % 2
                                for s in range(3):
                                    mm(ps[3][0:NS, s * 128:(s + 1) * 128], [(hsT[:, kc, :], wts[s][:, kc, :]) for kc in range(KC)], ["hsT", wn[s]], P(3))
                                T.op("act", lambda: A.copy(out=sst[sp_][:], in_=ps[3][0:NS, 0:384]), reads=P(3), writes=[f"sst{sp_}"])
                                for s in range(3):
                                    c0 = (g * 3 + s) * D + hp * 128
                                    dma("sp", qkvs_scr[:, c0:c0 + 128], sst[sp_][:, s * 128:(s + 1) * 128], reads=[f"sst{sp_}"], writes=["qkvs_scr"])
                                    dma("sp", qkvs_out[a, :, c0:c0 + 128], sst[sp_][:, s * 128:(s + 1) * 128], reads=[f"sst{sp_}"])
                            chk(f"p2{l}{st}{hp}{g}")
                            for tt in range(4):
                                t0 = base + tt * 512
                                sl = 512 // d
                                pq, pk = ps[0], ps[1]
                                mm(pq[:, :], [(wq[par][:, kc, :], hT[:, kc, tt * 512:(tt + 1) * 512]) for kc in range(KC)], [wn[0], f"hT{tt}"], P(0))
                                for e_ in range(2):
                                    T.op("act", lambda: A.copy(out=Qsv[e_][64 * e_:64 * e_ + 64, :, tt * sl:(tt + 1) * sl],
                                                               in_=pq[64 * e_:64 * e_ + 64, :].rearrange("p (s r) -> p r s", r=d)),
                                         reads=P(0), writes=[Qn])
                                mm(pk[:, :], [(wk[par][:, kc, :], hT[:, kc, tt * 512:(tt + 1) * 512]) for kc in range(KC)], [wn[1], f"hT{tt}"], P(1))
                                T.op("dve", lambda: V.tensor_copy(out=Ksv[:, :, 128 + tt * sl:128 + (tt + 1) * sl], in_=pk[:, :].rearrange("p (s r) -> p r s", r=d)),
                                     reads=P(1), writes=[Kn])
                                klo = max(t0, NT - W)
                                if klo < t0 + 512:
                                    kp_ = 0
                                    T.op("act", lambda: A.copy(out=kst[kp_][:], in_=pk[:, :]), reads=P(1), writes=[f"kst{kp_}"])
                                    dma("sp", kpT[g][a, hp * 128:(hp + 1) * 128, klo - (NT - W):t0 + 512 - (NT - W)], kst[kp_][:, klo - t0:512],
                                        reads=[f"kst{kp_}"])
                            chk(f"p3{l}{st}{hp}{g}")
                            blocks = [(r, j) for r in range(d) for j in range(nb)]
                            for b4 in range(4):
                                pvb = ps[2 + (b4 % 2)]
                                pvn = P(2 + (b4 % 2))
                                for q in range(4):
                                    r, j = blocks[b4 * 4 + q]
                                    off = 128 * j * d + r
                                    mm(pvb[:, q * 128:(q + 1) * 128],
                                       [(hT[:, kc, off:off + 127 * d + 1:d], wv[par][:, kc, :]) for kc in range(KC)], [wn[2]] + hT_all, pvn)
                                T.op("act", lambda: A.copy(out=Vx[par][:, NH + b4 * 4:NH + b4 * 4 + 4, 0:128], in_=pvb[:, :].rearrange("p (q c) -> p q c", q=4)),
                                     reads=pvn, writes=[Vn])
                                keep = [q for q in range(4) if base + 128 * blocks[b4 * 4 + q][1] * d + blocks[b4 * 4 + q][0] >= NT - W]
                                if keep:
                                    vp_ = 0
                                    T.op("dve", lambda: V.tensor_copy(out=vst[vp_][:], in_=pvb[:, :]), reads=pvn, writes=[f"vst{vp_}"])
                                    for q in keep:
                                        r, j = blocks[b4 * 4 + q]
                                        row0 = base + 128 * j * d + r - (NT - W)
                                        dst = vp[g][a, row0:row0 + 127 * d + 1:d, hp * 128:(hp + 1) * 128]
                                        dma("sp", dst, vst[vp_][:, q * 128:(q + 1) * 128], reads=[f"vst{vp_}"])
                            chk(f"p4{l}{st}{hp}{g}")
                            if st + 1 < NST:
                                if True:
                                    dma("sp", kh_scr[slot, :, 0:d * 128].rearrange("p (r s) -> p r s", r=d), Ksv[:, :, S:S + 128], reads=[Kn], writes=[f"kh{slot}"])
                                if True:
                                  dma("sp", vh_scr[slot, :, 0:d * VW].rearrange("p (r s) -> p r s", r=d),
                                    Vx[par][:, NH:NH + 16, :].rearrange("p (r j) c -> p r j c", j=nb)[:, :, nb - 1, :], reads=[Vn], writes=[f"vh{slot}"])
                            chk(f"proj{l}{st}{hp}{g}")
                            for e in range(2):
                                h = 2 * hp + e
                                pe0 = 64 * e
                                if g == 0:
                                    bundles = [[(0, qb) for qb in range(4 * m, 4 * m + 4)] for m in range(4)]
                                elif g == 1:
                                    bundles = [[(r, qb) for qb in range(4)] for r in range(4)]
                                else:
                                    bundles = [[(r, 0) for r in range(4 * m, 4 * m + 4)] for m in range(4)]
                                done_streams = {}
                                sslot = 0
                                for bi, bun in enumerate(bundles):
                                    for (r, qb) in bun:
                                        if r in done_streams:
                                            continue
                                        if g == 0:
                                            ptp, sbase = 0, 0
                                        elif g == 1:
                                            ptp, sbase = (r % 2), 0
                                            if ptp == 0:
                                                sbase = 0
                                        else:
                                            ptp, sbase = 1, (r % 4) * 2
                                        if g == 1:
                                            pass
                                        done_streams[r] = (ptp, sbase)
                                        for kb in range(-1, nb):
                                            q_lo = max(kb, 0)
                                            q_hi = min(kb + 2, nb)
                                            N = (q_hi - q_lo) * 128
                                            bc0 = 128 if kb == -1 else 0
                                            ss = sslot % 4
                                            sslot += 1
                                            pso = psY[:, ss * 512:ss * 512 + N]
                                            pname = PY(ss * 512, ss * 512 + N)

                                            def sc_fn():
                                                PE.matmul(pso, lhsT=Ksv[:, r, 128 * (kb + 1):128 * (kb + 2)],
                                                          rhs=Qsv[e][:, r, 128 * q_lo:128 * q_hi], start=True, stop=False)
                                                return PE.matmul(pso, lhsT=identb[:], rhs=BT[:, g * 16 + h, bc0:bc0 + N], start=False, stop=True)
                                            T.op("pe", sc_fn, reads=[Kn, Qn, "identb", "BT"], writes=pname)
                                            T.op("act", lambda: A.activation(out=pt[ptp][:, sbase + kb + 1, 0:N], in_=pso, func=AF.Exp, scale=0.125),
                                                 reads=pname, writes=[f"pt{ptp}"])
                                    chk(f"s1{l}{st}{hp}{g}")
                                    pnum = ps[0 + 2 * (bi % 2)]
                                    pden = ps[1 + 2 * (bi % 2)]
                                    PN = P(0 + 2 * (bi % 2))
                                    PD = P(1 + 2 * (bi % 2))
                                    ptnames = set()
                                    for q, (r, qb) in enumerate(bun):
                                        ptp, sbase = done_streams[r]
                                        ptnames.add(f"pt{ptp}")
                                        blk_prev = r if qb == 0 else NH + r * nb + qb - 1
                                        blk_cur = NH + r * nb + qb
                                        c_prev = 0 if qb == 0 else 128
                                        rhs_prev = pt[ptp][:, sbase + qb, c_prev:c_prev + 128]
                                        rhs_cur = pt[ptp][:, sbase + qb + 1, 0:128]

                                        def pv_fn():
                                            PE.matmul(pnum[:, q * 128:(q + 1) * 128], lhsT=Vx[par][:, blk_prev, 0:128], rhs=rhs_prev, start=True, stop=False)
                                            PE.matmul(pnum[:, q * 128:(q + 1) * 128], lhsT=Vx[par][:, blk_cur, 0:128], rhs=rhs_cur, start=False, stop=True)
                                            PE.matmul(pden[0:1, q * 128:(q + 1) * 128], lhsT=Vx[par][:, blk_prev, 128:129], rhs=rhs_prev, start=True, stop=False)
                                            return PE.matmul(pden[0:1, q * 128:(q + 1) * 128], lhsT=Vx[par][:, blk_cur, 128:129], rhs=rhs_cur, start=False, stop=True)
                                        T.op("pe", pv_fn, reads=[Vn, f"pt{ptp}"], writes=PN + PD)
                                    chk(f"s2{l}{st}{hp}{g}")
                                    if g == 0:
                                        an = accn[pe0:pe0 + 64, bi * 512:(bi + 1) * 512]
                                        ad = accd[0:1, e, bi * 512:(bi + 1) * 512]
                                        pn = pnum[pe0:pe0 + 64, :]
                                        pd = pden[0:1, :]
                                    elif g == 1:
                                        r = bun[0][0]
                                        an = accn[pe0:pe0 + 64, :].rearrange("p (qb i r) -> p r qb i", r=4, i=128)[:, r]
                                        ad = accd[0:1, e, :].rearrange("p (qb i r) -> p r qb i", r=4, i=128)[:, r]
                                        pn = pnum[pe0:pe0 + 64, :].rearrange("p (qb i) -> p qb i", i=128)
                                        pd = pden[0:1, :].rearrange("p (qb i) -> p qb i", i=128)
                                    else:
                                        r0 = bun[0][0]
                                        an = accn[pe0:pe0 + 64, :].rearrange("p (i r) -> p r i", r=16)[:, r0:r0 + 4]
                                        ad = accd[0:1, e, :].rearrange("p (i r) -> p r i", r=16)[:, r0:r0 + 4]
                                        pn = pnum[pe0:pe0 + 64, :].rearrange("p (q i) -> p q i", i=128)
                                        pd = pden[0:1, :].rearrange("p (q i) -> p q i", i=128)
                                    if g == 0:
                                        T.op("act", lambda: A.copy(out=an, in_=pn), reads=PN, writes=["accn"])
                                        T.op("dve", lambda: V.tensor_copy(out=ad, in_=pd), reads=PD, writes=["accd"])
                                    else:
                                        T.op("dve", lambda: V.tensor_tensor(out=an, in0=an, in1=pn, op=ALU.add), reads=PN + ["accn"], writes=["accn"])
                                        T.op("dve", lambda: V.tensor_tensor(out=ad, in0=ad, in1=pd, op=ALU.add), reads=PD + ["accd"], writes=["accd"])
                            chk(f"att{l}{st}{hp}{g}")
                        T.op("dve", lambda: V.reciprocal(out=accd[:], in_=accd[:]), reads=["accd"], writes=["accd"])
                        for e in range(2):
                            pe0 = 64 * e
                            for tt in range(4):
                                bk = ps[2 + tt % 2]
                                bn = P(2 + tt % 2)
                                mm(bk[:, :], [(ones32[0:1, :], accd[0:1, e, tt * 512:(tt + 1) * 512])], ["ones32", "accd"], bn)
                                T.op("dve", lambda: V.tensor_tensor(out=attnT[pe0:pe0 + 64, hp, tt * 512:(tt + 1) * 512], in0=accn[pe0:pe0 + 64, tt * 512:(tt + 1) * 512],
                                                                    in1=bk[pe0:pe0 + 64, :], op=ALU.mult), reads=bn + ["accn"], writes=["attnT"])
                    T.drain("sp")
                    nc.all_engine_barrier()
                chk(f"B{l}{st}")
                with ExitStack() as esC:
                    sbC = lambda name, shape, dt: esC.enter_context(nc.sbuf_tensor(un(name), shape, dt))
                    xt = sbC("xt_c", [128, KC, 512], F32)
                    wo = sbC("wo", [128, KC, D], BF16)
                    for c in range(KC):
                        dma("pool", wo[:, c, :], wo_in[a, c * 128:(c + 1) * 128, :], writes=["wo"])
                    for tt in range(4):
                        t0 = base + tt * 512
                        xr = [f"xr{t0 // 256}", f"xr{t0 // 256 + 1}"]
                        dma("sp", xt[:], x_view(src, t0, 512), reads=xr, writes=["xt_c"])
                        for ec in range(KC):
                            bk = ps[ec % 2]
                            bn = P(ec % 2)
                            mm(bk[:, :], [(wo[:, c, ec * 128:(ec + 1) * 128], attnT[:, c, tt * 512:(tt + 1) * 512]) for c in range(KC)], ["wo", "attnT"], bn)
                            T.op("dve", lambda: V.tensor_tensor(out=xt[:, ec, :], in0=xt[:, ec, :], in1=bk[:, :], op=ALU.add), reads=bn + ["xt_c"], writes=["xt_c"])
                        dma("sp", x_view(xres, t0, 512), xt[:], reads=["xt_c"], writes=xr)
                    T.drain("sp")
                    nc.all_engine_barrier()
        T.drain("sp")
        nc.all_engine_barrier()

    def sample_attention(l):
        a = l // 2
        with ExitStack() as es:
            sb = lambda name, shape, dt: es.enter_context(nc.sbuf_tensor(un(name), shape, dt))
            qs = sb("qs", [NS, 9 * D], F32)
            prod0 = sb("prod0", [NS, D], F32)
            sc0 = sb("sc0", [NS, 16], F32)
            pv0 = [sb(f"pv0_{g}", [NS, D + 16], F32) for g in range(3)]
            Kc = [sb(f"Kc{i}", [128, D], F32) for i in range(2)]
            Vc = [sb(f"Vc{i}", [128, D], F32) for i in range(2)]
            prod = sb("prod", [128, D], F32)
            sc = sb("sc", [128, 16], F32)
            pvc = [sb(f"pvc{i}", [128, D + 16], F32) for i in range(2)]
            rec = sb("rec", [1, 16], F32)
            arow = sb("arow", [1, D], F32)
            asT = sb("asT", [128, KC, NS], BF16)
            wo = sb("wo_s", [128, KC, D], BF16)
            for c in range(KC):
                dma("pool", wo[:, c, :], wo_in[a, c * 128:(c + 1) * 128, :], writes=["wo_s"])
            dma("sp", qs[:], qkvs_scr[:, :], reads=["qkvs_scr"], writes=["qs"])
            for g in range(3):
                qg = qs[:, (g * 3) * D:(g * 3 + 1) * D]
                kg = qs[:, (g * 3 + 1) * D:(g * 3 + 2) * D]
                vg = qs[:, (g * 3 + 2) * D:(g * 3 + 3) * D]
                T.op("dve", lambda: V.tensor_tensor(out=prod0[:], in0=qg, in1=kg, op=ALU.mult), reads=["qs"], writes=["prod0"])
                T.op("dve", lambda: V.tensor_reduce(out=sc0[:], in_=prod0[:].rearrange("p (h d) -> p h d", d=64), axis=AX.X, op=ALU.add),
                     reads=["prod0"], writes=["sc0"])
                T.op("dve", lambda: V.scalar_tensor_tensor(out=sc0[:], in0=sc0[:], scalar=0.125, in1=B0[:, g * 16:(g + 1) * 16], op0=ALU.mult, op1=ALU.add),
                     reads=["sc0", "B0"], writes=["sc0"])
                T.op("act", lambda: A.activation(out=pv0[g][:, D:D + 16], in_=sc0[:], func=AF.Exp), reads=["sc0"], writes=[f"pv0_{g}p"])
                T.op("dve", lambda: V.tensor_tensor(out=pv0[g][:, 0:D].rearrange("p (h d) -> p h d", d=64), in0=vg.rearrange("p (h d) -> p h d", d=64),
                                                    in1=pv0[g][:, D:D + 16].unsqueeze(2).broadcast_to([NS, 16, 64]), op=ALU.mult),
                     reads=["qs", f"pv0_{g}p"], writes=[f"pv0_{g}"])
            ci = 0
            for b in range(NS):
                pieces = [(0, 512), (512, 1024), (1024, 1040)]
                pouts = [ps[2][0:1, 0:512], ps[3][0:1, 0:512], psY[0:1, 0:16]]
                pnames = [P(2), P(3), PY(0, 16)]
                for g in range(3):
                    par = ci % 2
                    ci += 1
                    dma("sp", Kc[par][:], ck[g][a, b], writes=[f"Kc{par}"])
                    dma("sp", Vc[par][:], cv[g][a, b], writes=[f"Vc{par}"])
                    qg = qs[:, (g * 3) * D:(g * 3 + 1) * D]
                    mm(ps[0][:, :], [(sel4[:, b * 128:(b + 1) * 128], qg[:, 0:512])], ["sel4", "qs"], P(0))
                    mm(ps[1][:, :], [(sel4[:, b * 128:(b + 1) * 128], qg[:, 512:1024])], ["sel4", "qs"], P(1))
                    T.op("dve", lambda: V.tensor_tensor(out=prod[:, 0:512], in0=Kc[par][:, 0:512], in1=ps[0][:, :], op=ALU.mult), reads=[f"Kc{par}"] + P(0), writes=["prodA"])
                    T.op("dve", lambda: V.tensor_tensor(out=prod[:, 512:1024], in0=Kc[par][:, 512:1024], in1=ps[1][:, :], op=ALU.mult), reads=[f"Kc{par}"] + P(1), writes=["prodB"])
                    T.op("dve", lambda: V.tensor_reduce(out=sc[:], in_=prod[:].rearrange("p (h d) -> p h d", d=64), axis=AX.X, op=ALU.add),
                         reads=["prodA", "prodB"], writes=["sc"])
                    T.op("dve", lambda: V.scalar_tensor_tensor(out=sc[:], in0=sc[:], scalar=0.125, in1=SB[:, g * 16:(g + 1) * 16], op0=ALU.mult, op1=ALU.add),
                         reads=["sc", "SB"], writes=["sc"])
                    T.op("act", lambda: A.activation(out=pvc[par][:, D:D + 16], in_=sc[:], func=AF.Exp), reads=["sc"], writes=[f"pvc{par}p"])
                    T.op("dve", lambda: V.tensor_tensor(out=pvc[par][:, 0:D].rearrange("p (h d) -> p h d", d=64), in0=Vc[par][:].rearrange("p (h d) -> p h d", d=64),
                                                        in1=pvc[par][:, D:D + 16].unsqueeze(2).broadcast_to([128, 16, 64]), op=ALU.mult),
                         reads=[f"Vc{par}", f"pvc{par}p"], writes=[f"pvc{par}"])
                    for pi, (c0, c1) in enumerate(pieces):
                        def nd_fn():
                            PE.matmul(pouts[pi][:, 0:c1 - c0], lhsT=ones32[:, 0:1], rhs=pvc[par][:, c0:c1], start=(g == 0), stop=False)
                            return PE.matmul(pouts[pi][:, 0:c1 - c0], lhsT=sel4[:, b * 128:b * 128 + 1], rhs=pv0[g][:, c0:c1], start=False, stop=(g == 2))
                        T.op("pe", nd_fn, reads=["ones32", "sel4", f"pvc{par}", f"pvc{par}p", f"pv0_{g}", f"pv0_{g}p"], writes=pnames[pi])
                T.op("dve", lambda: V.reciprocal(out=rec[:], in_=psY[0:1, 0:16]), reads=PY(0, 16), writes=["rec"])
                for pi in range(2):
                    T.op("dve", lambda: V.tensor_tensor(out=arow[:, pi * 512:(pi + 1) * 512].rearrange("p (h d) -> p h d", d=64),
                                                        in0=pouts[pi].rearrange("p (h d) -> p h d", d=64),
                                                        in1=rec[:, pi * 8:(pi + 1) * 8].unsqueeze(2).broadcast_to([1, 8, 64]), op=ALU.mult),
                         reads=pnames[pi] + ["rec"], writes=["arow%d" % pi])
                for c in range(KC):
                    mm(psY[:, 512 + c:512 + c + 1], [(arow[0:1, c * 128:(c + 1) * 128], ones32[0:1, 0:1])], ["arow0", "arow1", "ones32"], PY(512, 520))
                T.op("dve", lambda: V.tensor_copy(out=asT[:, :, b], in_=psY[:, 512:512 + KC]), reads=PY(512, 520), writes=["asT"])
            for ec in range(KC):
                mm(psY[:, 1024 + ec * NS:1024 + (ec + 1) * NS], [(wo[:, c, ec * 128:(ec + 1) * 128], asT[:, c, :]) for c in range(KC)], ["wo_s", "asT"], PY(1024, 1056))
            T.op("dve", lambda: V.tensor_tensor(out=xs[:], in0=xs[:], in1=psY[:, 1024:1024 + KC * NS].rearrange("p (c n) -> p c n", n=NS), op=ALU.add),
                 reads=PY(1024, 1056) + ["xs"], writes=["xs"])
        T.drain("sp")
        nc.all_engine_barrier()

    def tile_pass(l):
        odd = (l % 2 == 1)
        pb = l // 2
        last = (l == 3)
        with ExitStack() as es:
            sb = lambda name, shape, dt: es.enter_context(nc.sbuf_tensor(un(name), shape, dt))
            win = sb("win", [128, KC, 2 * DFF], BF16)
            wout = sb("wout", [128, FC, D], BF16)
            xt = [sb(f"xt{i}", [128, KC, FT], F32) for i in range(2)]
            sq = sb("sq", [128, KC, FT], BF16)
            rstd = sb("rstd", [128, FT], F32)
            hb = sb("hb", [128, KC, FT], BF16)
            gext = [sb(f"gext{i}", [128, FT + 2], F32) for i in range(4)]
            cvt = [sb(f"cvt{i}", [128, FT], F32) for i in range(4)]
            actc = [sb(f"actc{i}", [128, FT], BF16) for i in range(4)]
            ghalo = sb("ghalo", [128, FC, 2], F32)
            pre = sb("pre", [128, 2, FC, NS], F32)
            gs = sb("gs", [128, FC, NS], F32)
            if odd:
                wpl = sb("wpl", [128, 2, 4, 256], BF16)
                hx = sb("hx", [128, KC, 16 + FT], F32)
                hxs = sb("hxs", [128, KC, NS, 16], F32)
                hso = sb("hso", [128, KC, NS], F32)
                s1 = sb("s1", [128, 16 + FT], F32)
                s2 = sb("s2", [128, 16 + FT], F32)
                zb = sb("zb", [128, KC, FT], BF16)
                pcorr = sb("pcorr", [128, 4, 16], F32)
                dma("sp", pcorr[:], pcorr_in.rearrange("p (g t) -> p g t", g=4), writes=["pcorr"])
                for g in range(4):
                    dma("pool", wpl[:, :, g, :], wpool_in[pb, g].rearrange("(kc p) n -> p kc n", p=128), writes=["wpl"])
                dma("sp", hxs[:, :, :, 0:15], spoolT_in[pb], writes=["hxs"])
                dma("sp", pools14[pb], spool_in[pb, :, 1:15, :], reads=[])
            for kc in range(KC):
                dma("pool", win[:, kc, :], win_in[l, kc * 128:(kc + 1) * 128, :], writes=["win"])
            for f in range(FC):
                dma("pool", wout[:, f, :], wout_in[l, f * 128:(f + 1) * 128, :], writes=["wout"])
            dma("sp", pre[:], sconvT_in[l], writes=["pre"])
            dma("sp", convs0[l], sconv_in[l, :, 1, :])
            T.op("pool", lambda: G.memset(ghalo[:], 0.0), writes=["ghalo"])
            if odd:
                T.op("pool", lambda: G.memset(hx[:, :, 0:16], 0.0), writes=["hx"])

            ntile = NT // FT
            fi = 0
            for ti in range(ntile + 1):
                sample = (ti == ntile)
                n = NS if sample else FT
                t0 = ti * FT
                if sample:
                    x = xs
                    xn = "xs"
                    xa = xs[:]
                else:
                    par = ti % 2
                    x = xt[par]
                    xn = f"xt{par}"
                    xa = x[:]
                    dma("sp", xa, x_view(xres, t0, FT), reads=[f"xr{ti}"], writes=[xn])
                if odd:
                    rms_rstd(xa, n, sq[:, :, 0:n], rstd[:, 0:n], ps[0][:, 0:n], xn, "t")
                    if not sample:
                        hv = lambda c, lo, hi: hx[:, c, lo:hi]
                        for c in range(KC):
                            T.op("dve", lambda: V.scalar_tensor_tensor(out=hx[:, c, 16:16 + n], in0=x[:, c, :], scalar=pv[:, PV_NM + l * 8 + c:PV_NM + l * 8 + c + 1],
                                                                       in1=rstd[:, 0:n], op0=ALU.mult, op1=ALU.mult), reads=[xn, "pv", "rstdt"], writes=["hx"])
                        if ti == ntile - 1:
                            dma("sp", poolpT[pb].rearrange("(kc p) t -> p kc t", p=128), hx[:, :, 16 + FT - 15:16 + FT], reads=["hx"], nonc=True)
                    else:
                        for c in range(KC):
                            T.op("dve", lambda: V.scalar_tensor_tensor(out=hxs[:, c, :, 15], in0=x[:, c, :], scalar=pv[:, PV_NM + l * 8 + c:PV_NM + l * 8 + c + 1],
                                                                       in1=rstd[:, 0:n], op0=ALU.mult, op1=ALU.mult), reads=[xn, "pv", "rstdt"], writes=["hxs"])
                        T.op("pool", lambda: G.tensor_copy(out=hso[:], in_=hxs[:, :, :, 15]), reads=["hxs"], writes=["hso"])
                        dma("sp", poolsT[pb].rearrange("(kc p) n -> p kc n", p=128), hso[:], reads=["hso"], nonc=True)
                    for c in range(KC):
                        grp = c // 2
                        w = (2, 4, 8, 16)[grp]
                        steps = grp + 1
                        if not sample:
                            L = 16 + FT
                            cur = lambda lo, hi: hx[:, c, lo:hi]
                            bufs = [lambda lo, hi: s1[:, lo:hi], lambda lo, hi: s2[:, lo:hi]]
                            fin = lambda vw: vw(16, L)
                            hcur = hx[:, c, 16:L]
                            zo = zb[:, c, 0:n]
                        else:
                            L = 16
                            cur = lambda lo, hi: hxs[:, c, :, lo:hi]
                            s1v = s1[:, 0:NS * 16].rearrange("p (b t) -> p b t", t=16)
                            s2v = s2[:, 0:NS * 16].rearrange("p (b t) -> p b t", t=16)
                            bufs = [lambda lo, hi: s1v[:, :, lo:hi], lambda lo, hi: s2v[:, :, lo:hi]]
                            fin = lambda vw: vw(15, 16)
                            hcur = hxs[:, c, :, 15:16]
                            zo = zb[:, c, 0:n].unsqueeze(2)
                        srcv = cur
                        srcn = "hxs" if sample else "hx"
                        sh = 1
                        for k in range(steps):
                            dst = bufs[k % 2]
                            dn = "s1" if k % 2 == 0 else "s2"
                            lo = 2 * sh - 1
                            sv, sn_ = srcv, srcn
                            T.op("dve", lambda: V.tensor_tensor(out=dst(lo, L), in0=sv(lo, L), in1=sv(lo - sh, L - sh), op=ALU.add), reads=[sn_], writes=[dn])
                            srcv, srcn = dst, dn
                            sh *= 2
                        fv = fin(srcv)
                        if (not sample) and ti == 0:
                            sv = srcv
                            T.op("dve", lambda: V.tensor_tensor(out=sv(16, 32), in0=sv(16, 32), in1=pcorr[:, grp, :], op=ALU.mult), reads=[srcn, "pcorr"], writes=[srcn])
                        T.op("dve", lambda: V.scalar_tensor_tensor(out=zo, in0=fv, scalar=1.0 / w, in1=hcur, op0=ALU.mult, op1=ALU.subtract),
                             reads=[srcn, "hxs" if sample else "hx"], writes=["zb"])
                    for grp in range(4):
                        for eh in range(2):
                            ec = grp * 2 + eh
                            bk = ps[1 + ec % 2]
                            bn = P(1 + ec % 2, 0, n)
                            mm(bk[:, 0:n], [(wpl[:, kc2, grp, eh * 128:(eh + 1) * 128], zb[:, grp * 2 + kc2, 0:n]) for kc2 in range(2)], ["wpl", "zb"], bn)
                            T.op("dve", lambda: V.scalar_tensor_tensor(out=x[:, ec, :], in0=bk[:, 0:n], scalar=pv[:, PV_PSC + pb * 8 + ec:PV_PSC + pb * 8 + ec + 1],
                                                                       in1=x[:, ec, :], op0=ALU.mult, op1=ALU.add), reads=bn + ["pv", xn], writes=[xn])
                    if not sample:
                        T.op("pool", lambda: G.tensor_copy(out=hx[:, :, 0:16], in_=hx[:, :, FT:FT + 16]), reads=["hx"], writes=["hx"])
                rms_rstd(xa, n, sq[:, :, 0:n], rstd[:, 0:n], ps[0][:, 0:n], xn, "t")
                for kc in range(KC):
                    T.op("dve", lambda: V.scalar_tensor_tensor(out=hb[:, kc, 0:n], in0=x[:, kc, :], scalar=pv[:, PV_NF + l * 8 + kc:PV_NF + l * 8 + kc + 1],
                                                               in1=rstd[:, 0:n], op0=ALU.mult, op1=ALU.mult), reads=[xn, "pv", "rstdt"], writes=["hb"])
                for f in range(FC):
                    p2 = fi % 4
                    fi += 1
                    bk = ps[p2]
                    bng = P(p2)
                    bnv = P(p2)

                    def gv_fn():
                        for kc in range(KC):
                            PE.matmul(bk[:, 0:n], lhsT=win[:, kc, f * 128:(f + 1) * 128], rhs=hb[:, kc, 0:n], start=(kc == 0), stop=(kc == KC - 1))
                        ins = None
                        for kc in range(KC):
                            ins = PE.matmul(bk[:, 256:256 + n], lhsT=win[:, kc, DFF + f * 128:DFF + (f + 1) * 128], rhs=hb[:, kc, 0:n], start=(kc == 0), stop=(kc == KC - 1))
                        return ins
                    T.op("pe", gv_fn, reads=["win", "hb"], writes=bng)
                    cw = lambda j: pv[:, PV_CW + (l * 3 + j) * FC + f:PV_CW + (l * 3 + j) * FC + f + 1]
                    cb = pv[:, PV_CB + l * FC + f:PV_CB + l * FC + f + 1]
                    cvn = f"cvt{p2}"
                    T.op("act", lambda: A.activation(out=cvt[p2][:, 0:n], in_=bk[:, 0:n], func=AF.Identity, bias=cb, scale=cw(2)), reads=bng + ["pv"], writes=[cvn])
                    if not sample:
                        gn = f"gext{p2}"
                        T.op("pool", lambda: G.tensor_copy(out=gext[p2][:, 0:2], in_=ghalo[:, f, :]), reads=["ghalo"], writes=[gn + "h"])
                        T.op("act", lambda: A.copy(out=gext[p2][:, 2:2 + n], in_=bk[:, 0:n]), reads=bng, writes=[gn])
                        T.op("pool", lambda: G.tensor_copy(out=ghalo[:, f, :], in_=gext[p2][:, n:n + 2]), reads=[gn], writes=["ghalo"])
                        T.op("dve", lambda: V.scalar_tensor_tensor(out=cvt[p2][:, 0:n], in0=gext[p2][:, 1:1 + n], scalar=cw(1), in1=cvt[p2][:, 0:n], op0=ALU.mult, op1=ALU.add),
                             reads=[gn, gn + "h", "pv", cvn], writes=[cvn])
                        T.op("dve", lambda: V.scalar_tensor_tensor(out=cvt[p2][:, 0:n], in0=gext[p2][:, 0:n], scalar=cw(0), in1=cvt[p2][:, 0:n], op0=ALU.mult, op1=ALU.add),
                             reads=[gn, gn + "h", "pv", cvn], writes=[cvn])
                    else:
                        T.op("act", lambda: A.copy(out=gs[:, f, :], in_=bk[:, 0:n]), reads=bng, writes=["gs"])
                        T.op("dve", lambda: V.scalar_tensor_tensor(out=cvt[p2][:, 0:n], in0=pre[:, 1, f, :], scalar=cw(1), in1=cvt[p2][:, 0:n], op0=ALU.mult, op1=ALU.add),
                             reads=["pre", "pv", cvn], writes=[cvn])
                        T.op("dve", lambda: V.scalar_tensor_tensor(out=cvt[p2][:, 0:n], in0=pre[:, 0, f, :], scalar=cw(0), in1=cvt[p2][:, 0:n], op0=ALU.mult, op1=ALU.add),
                             reads=["pre", "pv", cvn], writes=[cvn])
                    T.op("act", lambda: A.activation(out=cvt[p2][:, 0:n], in_=cvt[p2][:, 0:n], func=AF.Silu), reads=[cvn], writes=[cvn])
                    an = f"actc{p2}"
                    T.op("dve", lambda: V.tensor_tensor(out=actc[p2][:, 0:n], in0=cvt[p2][:, 0:n], in1=bk[:, 256:256 + n], op=ALU.mult), reads=[cvn] + bnv, writes=[an])

                    def y_fn():
                        ins = None
                        for ec in range(KC):
                            ins = PE.matmul(psY[:, ec * 256:ec * 256 + n], lhsT=wout[:, f, ec * 128:(ec + 1) * 128], rhs=actc[p2][:, 0:n], start=(f == 0 and ec % 2 == 0), stop=(f == FC - 1))
                        return ins
                    T.op("pe", y_fn, reads=["wout", an], writes=PY())
                yv = psY[:, :].rearrange("p (c t) -> p c t", t=256)[:, :, 0:n]
                T.op("dve", lambda: V.tensor_tensor(out=xa, in0=xa, in1=yv, op=ALU.add), reads=PY() + [xn], writes=[xn])
                if not last:
                    if not sample:
                        dma("sp", x_view(xres, t0, FT), xa, reads=[xn], writes=[f"xr{ti}"])
                else:
                    rms_rstd(xa, n, sq[:, :, 0:n], rstd[:, 0:n], ps[0][:, 0:n], xn, "t")
                    for kc in range(KC):
                        T.op("dve", lambda: V.scalar_tensor_tensor(out=x[:, kc, :], in0=x[:, kc, :], scalar=pv[:, PV_FIN + kc:PV_FIN + kc + 1],
                                                                   in1=rstd[:, 0:n], op0=ALU.mult, op1=ALU.mult), reads=[xn, "pv", "rstdt"], writes=[xn])
                    if not sample:
                        dma("sp", x_view(yT, t0, FT), xa, reads=[xn])
                    else:
                        dma("sp", ysT.rearrange("(kc p) n -> p kc n", p=128), xa, reads=[xn], nonc=True)
                if (not sample) and ti == ntile - 1:
                    dma("sp", convpT[l].rearrange("(f p) j -> p f j", p=128), ghalo[:], reads=["ghalo"], nonc=True)
                if sample:
                    dma("sp", convsT[l].rearrange("(f p) n -> p f n", p=128), gs[:], reads=["gs"], nonc=True)
        T.drain("sp")
        nc.all_engine_barrier()

    try:
        for l in range(4):
            if l % 2 == 0:
                attention_pass(l)
                chk(f"attn{l}")
                sample_attention(l)
                chk(f"sattn{l}")
            tile_pass(l)
            chk(f"tile{l}")
    except _Stop:
        pass
    T.drain("sp")
    T.drain("act")
    return nc


_NC = None


def kernel(**inp):
    global _NC
    f32 = np.float32
    g = lambda k: np.asarray(inp[k], dtype=f32)
    x_prompt, x_sample = g("x_prompt"), g("x_sample")
    caches_k = [g("cache_k_w128"), g("cache_k_w512"), g("cache_k_w2048")]
    caches_v = [g("cache_v_w128"), g("cache_v_w512"), g("cache_v_w2048")]
    state_pool, state_conv = g("state_pool"), g("state_conv")
    norm_mix, norm_ffn, norm_final = g("norm_mix"), g("norm_ffn"), g("norm_final")
    pool_scale, conv_w, conv_b = g("pool_scale"), g("conv_w"), g("conv_b")

    fm = lambda v: np.ascontiguousarray(v.reshape(-1, 128).T)
    pvec = np.zeros((128, NPV), f32)
    for l in range(4):
        pvec[:, PV_NM + l * 8:PV_NM + (l + 1) * 8] = fm(norm_mix[l])
        pvec[:, PV_NF + l * 8:PV_NF + (l + 1) * 8] = fm(norm_ffn[l])
        pvec[:, PV_CB + l * FC:PV_CB + (l + 1) * FC] = fm(conv_b[l])
        for j in range(3):
            pvec[:, PV_CW + (l * 3 + j) * FC:PV_CW + (l * 3 + j + 1) * FC] = fm(conv_w[l, j])
    pvec[:, PV_FIN:PV_FIN + 8] = fm(norm_final)
    for b in range(2):
        pvec[:, PV_PSC + b * 8:PV_PSC + (b + 1) * 8] = fm(pool_scale[b])
    consts = host_consts()
    shared = dict(rel_bias=g("rel_bias"), pvec=pvec, w_qkv=g("w_qkv"), w_o=g("w_o"), w_pool=g("w_pool"),
                  w_in=g("w_in"), w_out=g("w_out"), **consts)
    in_maps = []
    for c in range(8):
        b = c % 4
        sl = slice(NS * c, NS * c + NS)
        m = dict(shared)
        m["xT"] = np.ascontiguousarray(x_prompt[b].T)
        m["xsT"] = np.ascontiguousarray(x_sample[sl, 0, :].T)
        for gi in range(3):
            dil = DILS[gi]
            m[f"ck{gi}"] = np.ascontiguousarray(caches_k[gi][:, sl, ::dil]).reshape(2, NS, 128, D)
            m[f"cv{gi}"] = np.ascontiguousarray(caches_v[gi][:, sl, ::dil]).reshape(2, NS, 128, D)
        sp = state_pool[:, sl]
        m["spool"] = np.ascontiguousarray(sp)
        m["spoolT"] = np.ascontiguousarray(sp.reshape(2, NS, 15, KC, 128).transpose(0, 4, 3, 1, 2))
        scv = state_conv[:, sl]
        m["sconv"] = np.ascontiguousarray(scv)
        m["sconvT"] = np.ascontiguousarray(scv.reshape(4, NS, 2, FC, 128).transpose(0, 4, 2, 3, 1))
        in_maps.append(m)
    if _NC is None:
        _NC = build()
    res = run_bass_kernel_spmd(_NC, in_maps, core_ids=list(range(8)))
    R = res.results
    B = 4
    y_prompt = np.stack([R[b]["yT"].T for b in range(B)]).astype(f32)
    y_sample = np.concatenate([R[c]["ysT"].T for c in range(8)], 0).reshape(32, 1, D).astype(f32)
    outs = [y_prompt, y_sample]
    for gi in range(3):
        kp = np.stack([R[b][f"kp{gi}T"].transpose(0, 2, 1) for b in range(B)], 1)
        vpp = np.stack([R[b][f"vp{gi}"] for b in range(B)], 1)
        outs.append(np.ascontiguousarray(kp).reshape(2, B, WINS[gi], 16, 64).astype(f32))
        outs.append(np.ascontiguousarray(vpp).reshape(2, B, WINS[gi], 16, 64).astype(f32))
    qk = np.concatenate([R[c]["qkvs"] for c in range(8)], 1)
    for gi in range(3):
        outs.append(np.ascontiguousarray(qk[:, :, (gi * 3 + 1) * D:(gi * 3 + 2) * D]).reshape(2, 32, 1, 16, 64).astype(f32))
        outs.append(np.ascontiguousarray(qk[:, :, (gi * 3 + 2) * D:(gi * 3 + 3) * D]).reshape(2, 32, 1, 16, 64).astype(f32))
    outs.append(np.stack([R[b]["poolpT"].transpose(0, 2, 1) for b in range(B)], 1).astype(f32))
    ps_ = np.zeros((2, 32, 15, D), f32)
    for c in range(8):
        ps_[:, NS * c:NS * c + NS, 0:14] = R[c]["pools14"]
        ps_[:, NS * c:NS * c + NS, 14] = R[c]["poolsT"].transpose(0, 2, 1)
    outs.append(ps_)
    outs.append(np.stack([R[b]["convpT"].transpose(0, 2, 1) for b in range(B)], 1).astype(f32))
    cs_ = np.zeros((4, 32, 2, DFF), f32)
    for c in range(8):
        cs_[:, NS * c:NS * c + NS, 0] = R[c]["convs0"]
        cs_[:, NS * c:NS * c + NS, 1] = R[c]["convsT"].transpose(0, 2, 1)
    outs.append(cs_)
    return tuple(outs)
```

```python
from contextlib import ExitStack
import numpy as np
import concourse.bass as bass
import concourse.mybir as mybir
from concourse.bass_utils import run_bass_kernel_spmd

F32 = mybir.dt.float32
BF16 = mybir.dt.bfloat16
ALU = mybir.AluOpType
AF = mybir.ActivationFunctionType
AX = mybir.AxisListType

D = 1024
KC = 8
NT = 4096
ST = 2048
NST = NT // ST
NS = 4
DFF = 2816
FC = 22
FT = 256
VW = 144
WINS = (128, 512, 2048)
DILS = (1, 4, 16)
EPS = 1e-6
NEG = -30000.0
NPV = 440
PV_NM, PV_NF, PV_FIN, PV_PSC, PV_CW, PV_CB = 0, 32, 64, 72, 88, 352


DEBUG_STOP = None


class _Stop(Exception):
    pass


class Trk:
    def __init__(self, nc):
        self.nc = nc
        self.E = dict(pe=nc.tensor, act=nc.scalar, dve=nc.vector, pool=nc.gpsimd, sp=nc.sync)
        self.sem = {}
        self.cnt = {}
        for n in ["pe", "act", "dve", "pool", "q_sp", "q_act", "q_pool"]:
            self.sem[n] = nc.alloc_semaphore("s_" + n)
            self.cnt[n] = 0
        self.waited = {}
        self.lastw = {}
        self.readers = {}

    def _wait(self, eng, deps):
        best = {}
        for (s, v) in deps:
            if eng == "pe" and s == "pe":
                continue
            if best.get(s, 0) < v:
                best[s] = v
        for s, v in best.items():
            if s.startswith("q_"):
                v = self.cnt[s]
            if self.waited.get((eng, s), 0) < v:
                self.E[eng].wait_ge(self.sem[s], v)
                self.waited[(eng, s)] = v

    def op(self, eng, fn, reads=(), writes=(), dma=False):
        reads = list(dict.fromkeys(reads))
        writes = list(dict.fromkeys(list(writes) + [r for r in reads if isinstance(r, str) and r[0] == "P" and r[1:].isdigit()]))
        deps = set()
        for r in reads:
            if r in self.lastw:
                deps.add(self.lastw[r])
        for w in writes:
            if w in self.lastw:
                deps.add(self.lastw[w])
            for t in self.readers.get(w, ()):
                deps.add(t)
        self._wait(eng, deps)
        ins = fn()
        if dma:
            s = "q_" + eng
            self.cnt[s] += 16
            ins.then_inc(self.sem[s], 16)
        else:
            s = eng
            self.cnt[s] += 1
            ins.then_inc(self.sem[s], 1)
        tok = (s, self.cnt[s])
        for w in writes:
            self.lastw[w] = tok
            self.readers[w] = []
        for r in reads:
            if r in writes:
                continue
            lst = self.readers.setdefault(r, [])
            lst.append(tok)
            if len(lst) > 48:
                best = {}
                for (s2, v2) in lst:
                    if best.get(s2, 0) < v2:
                        best[s2] = v2
                self.readers[r] = list(best.items())
        return ins

    def drain(self, eng="sp"):
        for s, v in self.cnt.items():
            if v > 0 and self.waited.get((eng, s), 0) < v:
                self.E[eng].wait_ge(self.sem[s], v)
                self.waited[(eng, s)] = v


def t5_buckets(dist):
    n_b, max_d = 32, 2048
    max_exact = n_b // 2
    n = np.maximum(dist, 1).astype(np.float32)
    large = max_exact + (np.log(n / max_exact) / np.log(max_d / max_exact) * (n_b - max_exact)).astype(np.int32)
    large = np.minimum(large, n_b - 1)
    return np.where(dist < max_exact, dist, large).astype(np.int32)


def host_consts():
    c = {}
    ohpad = np.zeros((3, 32, 384), np.float32)
    ohs = np.zeros((3, 32, 128), np.float32)
    for g in range(3):
        bk = t5_buckets(np.arange(129) * DILS[g])
        for rel in range(129):
            ohpad[g, bk[rel], 127 + rel] = 1.0
        for p in range(128):
            ohs[g, bk[128 - p], p] = 1.0
    maskpad = np.full((16, 384), NEG, np.float32)
    maskpad[:, 127:256] = 0.0
    selh = np.zeros((16, 16, 128), np.float32)
    for h in range(16):
        selh[h, h, :] = 1.0
    oh0 = np.zeros((32, NS), np.float32)
    oh0[0, :] = 1.0
    sel4 = np.zeros((NS, NS, 128), np.float32)
    for b in range(NS):
        sel4[b, b, :] = 1.0
    c["ohpad"] = ohpad
    c["ohs"] = ohs
    c["maskpad"] = maskpad
    c["selh"] = selh.reshape(16, 16 * 128)
    c["oh0"] = oh0
    c["sel4"] = sel4.reshape(NS, NS * 128)
    c["ident"] = np.eye(128, dtype=np.float32)
    pc = np.ones((3, 128, 16), np.float32)
    for gi, w in enumerate((2, 4, 8)):
        pass
    pcorr = np.ones((4, 128, 16), np.float32)
    for gi, w in enumerate((2, 4, 8, 16)):
        for t in range(16):
            pcorr[gi, :, t] = w / min(w, t + 1)
    c["pcorr"] = np.ascontiguousarray(pcorr.transpose(1, 0, 2)).reshape(128, 64)
    return c


def build():
    nc = bass.Bass("TRN2", target_bir_lowering=False)
    T = Trk(nc)
    _uid = [0]

    def chk(name):
        if DEBUG_STOP is not None and name == DEBUG_STOP:
            raise _Stop()

    def un(name):
        _uid[0] += 1
        return f"sb_{name}_{_uid[0]}"

    def din(name, shape):
        return nc.dram_tensor(name, list(shape), F32, kind="ExternalInput")

    def dout(name, shape):
        return nc.dram_tensor(name, list(shape), F32, kind="ExternalOutput")

    xT_in = din("xT", [D, NT]).ap()
    xsT_in = din("xsT", [D, NS]).ap()
    ck = [din(f"ck{g}", [2, NS, 128, D]).ap() for g in range(3)]
    cv = [din(f"cv{g}", [2, NS, 128, D]).ap() for g in range(3)]
    spoolT_in = din("spoolT", [2, 128, KC, NS, 15]).ap()
    spool_in = din("spool", [2, NS, 15, D]).ap()
    sconvT_in = din("sconvT", [4, 128, 2, FC, NS]).ap()
    sconv_in = din("sconv", [4, NS, 2, DFF]).ap()
    relb_in = din("rel_bias", [32, 48]).ap()
    pvec_in = din("pvec", [128, NPV]).ap()
    wqkv_in = din("w_qkv", [2, D, 9 * D]).ap()
    wo_in = din("w_o", [2, D, D]).ap()
    wpool_in = din("w_pool", [2, 4, 256, 256]).ap()
    win_in = din("w_in", [4, D, 2 * DFF]).ap()
    wout_in = din("w_out", [4, DFF, D]).ap()
    ohpad_in = din("ohpad", [3, 32, 384]).ap()
    ohs_in = din("ohs", [3, 32, 128]).ap()
    maskpad_in = din("maskpad", [16, 384]).ap()
    selh_in = din("selh", [16, 16 * 128]).ap()
    oh0_in = din("oh0", [32, NS]).ap()
    sel4_in = din("sel4", [NS, NS * 128]).ap()
    ident_in = din("ident", [128, 128]).ap()
    pcorr_in = din("pcorr", [128, 64]).ap()

    yT = dout("yT", [D, NT]).ap()
    ysT = dout("ysT", [D, NS]).ap()
    kpT = [dout(f"kp{g}T", [2, D, WINS[g]]).ap() for g in range(3)]
    vp = [dout(f"vp{g}", [2, WINS[g], D]).ap() for g in range(3)]
    qkvs_out = dout("qkvs", [2, NS, 9 * D]).ap()
    poolpT = dout("poolpT", [2, D, 15]).ap()
    poolsT = dout("poolsT", [2, D, NS]).ap()
    pools14 = dout("pools14", [2, NS, 14, D]).ap()
    convpT = dout("convpT", [4, DFF, 2]).ap()
    convsT = dout("convsT", [4, DFF, NS]).ap()
    convs0 = dout("convs0", [4, NS, DFF]).ap()

    xres = nc.dram_tensor("xres", [D, NT], F32).ap()
    scrb_h = nc.dram_tensor("scrb", [48, 128, 384], F32)
    scrb = scrb_h.ap()
    kh_scr = nc.dram_tensor("kh_scr", [24, 128, 16 * 128], BF16).ap()
    vh_scr = nc.dram_tensor("vh_scr", [24, 128, 16 * VW], BF16).ap()
    qkvs_scr = nc.dram_tensor("qkvs_scr", [NS, 9 * D], F32).ap()

    ps = [nc.alloc_psum_tensor(f"ps{i}", [128, 512], F32) for i in range(4)]
    psY = nc.alloc_psum_tensor("psY", [128, 2048], F32)

    def P(bank, lo=0, hi=512):
        return [f"P{bank}"]

    def PY(lo=0, hi=2048):
        out = []
        for bk in range(4):
            l2, h2 = max(lo, bk * 512), min(hi, (bk + 1) * 512)
            if l2 < h2:
                out += P(4 + bk, l2 - bk * 512, h2 - bk * 512)
        return out

    pv = nc.alloc_sbuf_tensor(un("pv"), [128, NPV], F32)
    meanm = nc.alloc_sbuf_tensor(un("meanm"), [128, 128], BF16)
    ones32 = nc.alloc_sbuf_tensor(un("ones32"), [128, 128], F32)
    identb = nc.alloc_sbuf_tensor(un("identb"), [128, 128], BF16)
    xs = nc.alloc_sbuf_tensor(un("xs"), [128, KC, NS], F32)
    SB = nc.alloc_sbuf_tensor(un("SB"), [128, 48], F32)
    B0 = nc.alloc_sbuf_tensor(un("B0"), [NS, 48], F32)
    sel4 = nc.alloc_sbuf_tensor(un("sel4"), [NS, NS * 128], F32)
    epsc = nc.alloc_sbuf_tensor(un("epsc"), [128, 1], F32)

    V = nc.vector
    A = nc.scalar
    G = nc.gpsimd
    PE = nc.tensor
    SP = nc.sync

    def dma(eng, out, in_, reads=(), writes=(), nonc=False):
        e = {"sp": SP, "pool": G, "act": A}[eng]
        if nonc:
            return T.op(eng, lambda: e.dma_start(out=out, in_=in_, allow_slow_non_contiguous=True), reads=reads, writes=writes, dma=True)
        return T.op(eng, lambda: e.dma_start(out=out, in_=in_), reads=reads, writes=writes, dma=True)

    def mm(out, pairs, reads, writes, first=True, last=True):
        def fn():
            ins = None
            n = len(pairs)
            for i, (l, r) in enumerate(pairs):
                ins = PE.matmul(out, lhsT=l, rhs=r, start=(first and i == 0), stop=(last and i == n - 1))
            return ins
        return T.op("pe", fn, reads=reads, writes=writes)

    dma("sp", pv[:], pvec_in[:, :], writes=["pv"])
    dma("sp", xs[:], xsT_in.rearrange("(kc p) n -> p kc n", p=128), writes=["xs"])
    dma("sp", sel4[:], sel4_in[:, :], writes=["sel4"])
    dma("pool", identb[:], ident_in[:, :], writes=["identb"])
    T.op("dve", lambda: V.memset(meanm[:], 1.0 / D), writes=["meanm"])
    T.op("dve", lambda: V.memset(ones32[:], 1.0), writes=["ones32"])
    T.op("dve", lambda: V.memset(epsc[:], EPS), writes=["epsc"])

    with ExitStack() as es:
        rb = es.enter_context(nc.sbuf_tensor(un("rb"), [32, 48], F32))
        ohp = es.enter_context(nc.sbuf_tensor(un("ohp"), [32, 3, 384], F32))
        ohs = es.enter_context(nc.sbuf_tensor(un("ohs"), [32, 3, 128], F32))
        oh0 = es.enter_context(nc.sbuf_tensor(un("oh0"), [32, NS], F32))
        mpad = es.enter_context(nc.sbuf_tensor(un("mpad"), [16, 384], F32))
        selh = es.enter_context(nc.sbuf_tensor(un("selh"), [16, 16 * 128], F32))
        fpad = es.enter_context(nc.sbuf_tensor(un("fpad"), [16, 384], F32))
        rep = [es.enter_context(nc.sbuf_tensor(un(f"rep{i}"), [128, 384], F32)) for i in range(2)]
        dma("sp", rb[:], relb_in[:, :], writes=["rb"])
        dma("sp", ohp[:], ohpad_in.rearrange("g k x -> k g x"), writes=["ohp"])
        dma("sp", ohs[:], ohs_in.rearrange("g k x -> k g x"), writes=["ohs"])
        dma("sp", oh0[:], oh0_in[:, :], writes=["oh0"])
        dma("sp", mpad[:], maskpad_in[:, :], writes=["mpad"])
        dma("sp", selh[:], selh_in[:, :], writes=["selh"])
        mm(ps[0][0:NS, 0:48], [(oh0[:], rb[:])], ["oh0", "rb"], P(0))
        T.op("dve", lambda: V.tensor_copy(out=B0[:], in_=ps[0][0:NS, 0:48]), reads=P(0), writes=["B0"])
        for g in range(3):
            mm(ps[1][:, g * 16:(g + 1) * 16], [(ohs[:, g, :], rb[:, g * 16:(g + 1) * 16])], ["ohs", "rb"], P(1))
        T.op("dve", lambda: V.tensor_copy(out=SB[:], in_=ps[1][:, 0:48]), reads=P(1), writes=["SB"])
        for g in range(3):
            mm(ps[2][0:16, 0:384], [(rb[:, g * 16:(g + 1) * 16], ohp[:, g, :])], ["rb", "ohp"], P(2))
            T.op("dve", lambda: V.scalar_tensor_tensor(out=fpad[:], in0=ps[2][0:16, 0:384], scalar=8.0, in1=mpad[:],
                                                       op0=ALU.mult, op1=ALU.add), reads=P(2) + ["mpad"], writes=["fpad"])
            for h in range(16):
                i = h % 2
                mm(ps[i][:, 0:384], [(selh[:, h * 128:(h + 1) * 128], fpad[:])], ["selh", "fpad"], P(i))
                T.op("act", lambda: A.copy(out=rep[i][:], in_=ps[i][:, 0:384]), reads=P(i), writes=[f"rep{i}"])
                dma("sp", scrb[g * 16 + h], rep[i][:], reads=[f"rep{i}"], writes=["scrb"])
    T.drain("sp")
    nc.all_engine_barrier()
    try:
        chk("setup")
    except _Stop:
        return nc

    def src_x(l):
        return xT_in if l == 0 else xres

    def x_view(ap, t0, n):
        return ap.rearrange("(kc p) t -> p kc t", p=128)[:, :, t0:t0 + n]

    def rms_rstd(xt_ap, n, sq, rstd, psm, xname, tag):
        pn_ = P(0, 0, n)
        T.op("act", lambda: A.activation(out=sq, in_=xt_ap, func=AF.Square), reads=[xname], writes=["sq" + tag])
        mm(psm, [(meanm[:], sq[:, kc, :]) for kc in range(KC)], ["meanm", "sq" + tag], pn_)
        T.op("act", lambda: A.activation(out=rstd, in_=psm, func=AF.Sqrt, bias=epsc[:, 0:1]), reads=pn_ + ["epsc"], writes=["rstd" + tag])
        T.op("dve", lambda: V.reciprocal(out=rstd, in_=rstd), reads=["rstd" + tag], writes=["rstd" + tag])

    def attention_pass(l):
        a = l // 2
        src = src_x(l)
        with ExitStack() as es:
            sb = lambda name, shape, dt: es.enter_context(nc.sbuf_tensor(un(name), shape, dt))
            hT = sb("hT", [128, KC, ST], BF16)
            attnT = sb("attnT", [128, KC, ST], BF16)
            BT = sb("BT", [128, 48, 256], BF16)
            hsT = sb("hsT", [128, KC, NS], BF16)
            sqs = sb("sqs", [128, KC, NS], BF16)
            rstds = sb("rstds", [128, NS], F32)

            for g in range(3):
                skew = bass.AP(scrb_h, 127 + g * 16 * 128 * 384, [[383, 128], [128 * 384, 16], [1, 256]])
                dma("pool", BT[:, g * 16:(g + 1) * 16, :], skew, reads=["scrb"], writes=["BT"])

            rms_rstd(xs[:], NS, sqs[:], rstds[:], ps[0][:, 0:NS], "xs", "s")
            for kc in range(KC):
                T.op("dve", lambda: V.scalar_tensor_tensor(out=hsT[:, kc, :], in0=xs[:, kc, :], scalar=pv[:, PV_NM + l * 8 + kc:PV_NM + l * 8 + kc + 1],
                                                           in1=rstds[:], op0=ALU.mult, op1=ALU.mult), reads=["xs", "pv", "rstds"], writes=["hsT"])
            it = 0
            for st in range(NST):
                base = st * ST
                with ExitStack() as esA:
                    sbA = lambda name, shape, dt: esA.enter_context(nc.sbuf_tensor(un(name), shape, dt))
                    xt = sbA("xt_a", [128, KC, 512], F32)
                    sq = sbA("sq_a", [128, KC, 512], BF16)
                    rstd = sbA("rstd_a", [128, 512], F32)
                    for tt in range(4):
                        t0 = base + tt * 512
                        dma("sp", xt[:], x_view(src, t0, 512), reads=[f"xr{t0 // 256}", f"xr{t0 // 256 + 1}"], writes=["xt_a"])
                        rms_rstd(xt[:], 512, sq[:], rstd[:], ps[0][:, :], "xt_a", "a")
                        for kc in range(KC):
                            T.op("dve", lambda: V.scalar_tensor_tensor(out=hT[:, kc, tt * 512:(tt + 1) * 512], in0=xt[:, kc, :],
                                                                       scalar=pv[:, PV_NM + l * 8 + kc:PV_NM + l * 8 + kc + 1], in1=rstd[:],
                                                                       op0=ALU.mult, op1=ALU.mult),
                                 reads=["xt_a", "pv", "rstda"], writes=[f"hT{tt}"])
                    T.drain("sp")
                    nc.all_engine_barrier()
                chk(f"A{l}{st}")
                hT_all = [f"hT{tt}" for tt in range(4)]
                with ExitStack() as esB:
                    sbB = lambda name, shape, dt: esB.enter_context(nc.sbuf_tensor(un(name), shape, dt))
                    wq = [sbB(f"wq{i}", [128, KC, 128], BF16) for i in range(2)]
                    wk = [sbB(f"wk{i}", [128, KC, 128], BF16) for i in range(2)]
                    wv = [sbB(f"wv{i}", [128, KC, 128], BF16) for i in range(2)]
                    Qs = [sbB(f"Qs{i}", [128, 2, ST], BF16) for i in range(2)]
                    Ks = [sbB(f"Ks{i}", [128, 16 * 128 + ST], BF16) for i in range(2)]
                    Vx = [sbB(f"Vx{i}", [128, 32, VW], BF16) for i in range(2)]
                    pt = [sbB("pt0", [128, 17, 256], BF16), sbB("pt1", [128, 8, 256], BF16)]
                    accn = sbB("accn", [128, ST], F32)
                    accd = sbB("accd", [1, 2, ST], F32)
                    kst = [sbB("kst0", [128, 512], F32)] * 2
                    vst = [sbB("vst0", [128, 512], F32)] * 2
                    sst = [sbB(f"sst{i}", [NS, 384], F32) for i in range(2)]
                    for i in range(2):
                        T.op("pool", lambda: G.memset(Qs[i][:], 0.0), writes=[f"Qs{i}"])
                    for hp in range(8):
                        for g in range(3):
                            d = DILS[g]
                            W = WINS[g]
                            S = ST // d
                            nb = S // 128
                            par = it % 2
                            it += 1
                            slot = g * 8 + hp
                            wts = (wq[par], wk[par], wv[par])
                            wn = (f"wq{par}", f"wk{par}", f"wv{par}")
                            for s in range(3):
                                c0 = (g * 3 + s) * D + hp * 128
                                dma("pool", wts[s][:], wqkv_in[a].rearrange("(kc p) n -> p kc n", p=128)[:, :, c0:c0 + 128], writes=[wn[s]])
                            Qn, Kn, Vn = f"Qs{par}", f"Ks{par}", f"Vx{par}"
                            Ksv = Ks[par][:, 0:d * (128 + S)].rearrange("p (r s) -> p r s", r=d)
                            Qsv = [Qs[par][:, e_, :].rearrange("p (r s) -> p r s", r=d) for e_ in range(2)]
                            NH = d
                            if st == 0:
                                T.op("pool", lambda: G.memset(Ksv[:, :, 0:128], 0.0), writes=[Kn])
                                T.op("pool", lambda: G.memset(Vx[par][:, 0:NH, :], 0.0), writes=[Vn])
                            else:
                                dma("sp", Ksv[:, :, 0:128], kh_scr[slot, :, 0:d * 128].rearrange("p (r s) -> p r s", r=d), reads=[f"kh{slot}"], writes=[Kn])
                                dma("sp", Vx[par][:, 0:NH, :], vh_scr[slot, :, 0:d * VW].rearrange("p (r s) -> p r s", r=d), reads=[f"vh{slot}"], writes=[Vn])
                            T.op("pool", lambda: G.memset(Vx[par][:, NH:NH + 16, 128:VW], 1.0), writes=[Vn])
                            chk(f"p1{l}{st}{hp}{g}")
                            if st == 0:
                                sp_ = slot % 2
                                for s in range(3):
                                    mm(ps[3][0:NS, s * 128:(s + 1) * 128], [(hsT[:, kc, :], wts[s][:, kc, :]) for kc in range(KC)], ["hsT", wn[s]], P(3))
                                T.op("act", lambda: A.copy(out=sst[sp_][:], in_=ps[3][0:NS, 0:384]), reads=P(3), writes=[f"sst{sp_}"])
                                for s in range(3):
                                    c0 = (g * 3 + s) * D + hp * 128
                                    dma("sp", qkvs_scr[:, c0:c0 + 128], sst[sp_][:, s * 128:(s + 1) * 128], reads=[f"sst{sp_}"], writes=["qkvs_scr"])
                                    dma("sp", qkvs_out[a, :, c0:c0 + 128], sst[sp_][:, s * 128:(s + 1) * 128], reads=[f"sst{sp_}"])
                            chk(f"p2{l}{st}{hp}{g}")
                            for tt in range(4):
                                t0 = base + tt * 512
                                sl = 512 // d
                                pq, pk = ps[0], ps[1]
                                mm(pq[:, :], [(wq[par][:, kc, :], hT[:, kc, tt * 512:(tt + 1) * 512]) for kc in range(KC)], [wn[0], f"hT{tt}"], P(0))
                                for e_ in range(2):
                                    T.op("act", lambda: A.copy(out=Qsv[e_][64 * e_:64 * e_ + 64, :, tt * sl:(tt + 1) * sl],
                                                               in_=pq[64 * e_:64 * e_ + 64, :].rearrange("p (s r) -> p r s", r=d)),
                                         reads=P(0), writes=[Qn])
                                mm(pk[:, :], [(wk[par][:, kc, :], hT[:, kc, tt * 512:(tt + 1) * 512]) for kc in range(KC)], [wn[1], f"hT{tt}"], P(1))
                                T.op("dve", lambda: V.tensor_copy(out=Ksv[:, :, 128 + tt * sl:128 + (tt + 1) * sl], in_=pk[:, :].rearrange("p (s r) -> p r s", r=d)),
                                     reads=P(1), writes=[Kn])
                                klo = max(t0, NT - W)
                                if klo < t0 + 512:
                                    kp_ = 0
                                    T.op("act", lambda: A.copy(out=kst[kp_][:], in_=pk[:, :]), reads=P(1), writes=[f"kst{kp_}"])
                                    dma("sp", kpT[g][a, hp * 128:(hp + 1) * 128, klo - (NT - W):t0 + 512 - (NT - W)], kst[kp_][:, klo - t0:512],
                                        reads=[f"kst{kp_}"])
                            chk(f"p3{l}{st}{hp}{g}")
                            blocks = [(r, j) for r in range(d) for j in range(nb)]
                            for b4 in range(4):
                                pvb = ps[2 + (b4 % 2)]
                                pvn = P(2 + (b4 % 2))
                                for q in range(4):
                                    r, j = blocks[b4 * 4 + q]
                                    off = 128 * j * d + r
                                    mm(pvb[:, q * 128:(q + 1) * 128],
                                       [(hT[:, kc, off:off + 127 * d + 1:d], wv[par][:, kc, :]) for kc in range(KC)], [wn[2]] + hT_all, pvn)
                                T.op("act", lambda: A.copy(out=Vx[par][:, NH + b4 * 4:NH + b4 * 4 + 4, 0:128], in_=pvb[:, :].rearrange("p (q c) -> p q c", q=4)),
                                     reads=pvn, writes=[Vn])
                                keep = [q for q in range(4) if base + 128 * blocks[b4 * 4 + q][1] * d + blocks[b4 * 4 + q][0] >= NT - W]
                                if keep:
                                    vp_ = 0
                                    T.op("dve", lambda: V.tensor_copy(out=vst[vp_][:], in_=pvb[:, :]), reads=pvn, writes=[f"vst{vp_}"])
                                    for q in keep:
                                        r, j = blocks[b4 * 4 + q]
                                        row0 = base + 128 * j * d + r - (NT - W)
                                        dst = vp[g][a, row0:row0 + 127 * d + 1:d, hp * 128:(hp + 1) * 128]
                                        dma("sp", dst, vst[vp_][:, q * 128:(q + 1) * 128], reads=[f"vst{vp_}"])
                            chk(f"p4{l}{st}{hp}{g}")
                            if st + 1 < NST:
                                if True:
                                    dma("sp", kh_scr[slot, :, 0:d * 128].rearrange("p (r s) -> p r s", r=d), Ksv[:, :, S:S + 128], reads=[Kn], writes=[f"kh{slot}"])
                                if True:
                                  dma("sp", vh_scr[slot, :, 0:d * VW].rearrange("p (r s) -> p r s", r=d),
                                    Vx[par][:, NH:NH + 16, :].rearrange("p (r j) c -> p r j c", j=nb)[:, :, nb - 1, :], reads=[Vn], writes=[f"vh{slot}"])
                            chk(f"proj{l}{st}{hp}{g}")
                            for e in range(2):
                                h = 2 * hp + e
                                pe0 = 64 * e
                                if g == 0:
                                    bundles = [[(0, qb) for qb in range(4 * m, 4 * m + 4)] for m in range(4)]
                                elif g == 1:
                                    bundles = [[(r, qb) for qb in range(4)] for r in range(4)]
                                else:
                                    bundles = [[(r, 0) for r in range(4 * m, 4 * m + 4)] for m in range(4)]
                                done_streams = {}
                                sslot = 0
                                for bi, bun in enumerate(bundles):
                                    for (r, qb) in bun:
                                        if r in done_streams:
                                            continue
                                        if g == 0:
                                            ptp, sbase = 0, 0
                                        elif g == 1:
                                            ptp, sbase = (r % 2), 0
                                            if ptp == 0:
                                                sbase = 0
                                        else:
                                            ptp, sbase = 1, (r % 4) * 2
                                        if g == 1:
                                            pass
                                        done_streams[r] = (ptp, sbase)
                                        for kb in range(-1, nb):
                                            q_lo = max(kb, 0)
                                            q_hi = min(kb + 2, nb)
                                            N = (q_hi - q_lo) * 128
                                            bc0 = 128 if kb == -1 else 0
                                            ss = sslot % 4
                                            sslot += 1
                                            pso = psY[:, ss * 512:ss * 512 + N]
                                            pname = PY(ss * 512, ss * 512 + N)

                                            def sc_fn():
                                                PE.matmul(pso, lhsT=Ksv[:, r, 128 * (kb + 1):128 * (kb + 2)],
                                                          rhs=Qsv[e][:, r, 128 * q_lo:128 * q_hi], start=True, stop=False)
                                                return PE.matmul(pso, lhsT=identb[:], rhs=BT[:, g * 16 + h, bc0:bc0 + N], start=False, stop=True)
                                            T.op("pe", sc_fn, reads=[Kn, Qn, "identb", "BT"], writes=pname)
                                            T.op("act", lambda: A.activation(out=pt[ptp][:, sbase + kb + 1, 0:N], in_=pso, func=AF.Exp, scale=0.125),
                                                 reads=pname, writes=[f"pt{ptp}"])
                                    chk(f"s1{l}{st}{hp}{g}")
                                    pnum = ps[0 + 2 * (bi % 2)]
                                    pden = ps[1 + 2 * (bi % 2)]
                                    PN = P(0 + 2 * (bi % 2))
                                    PD = P(1 + 2 * (bi % 2))
                                    ptnames = set()
                                    for q, (r, qb) in enumerate(bun):
                                        ptp, sbase = done_streams[r]
                                        ptnames.add(f"pt{ptp}")
                                        blk_prev = r if qb == 0 else NH + r * nb + qb - 1
                                        blk_cur = NH + r * nb + qb
                                        c_prev = 0 if qb == 0 else 128
                                        rhs_prev = pt[ptp][:, sbase + qb, c_prev:c_prev + 128]
                                        rhs_cur = pt[ptp][:, sbase + qb + 1, 0:128]

                                        def pv_fn():
                                            PE.matmul(pnum[:, q * 128:(q + 1) * 128], lhsT=Vx[par][:, blk_prev, 0:128], rhs=rhs_prev, start=True, stop=False)
                                            PE.matmul(pnum[:, q * 128:(q + 1) * 128], lhsT=Vx[par][:, blk_cur, 0:128], rhs=rhs_cur, start=False, stop=True)
                                            PE.matmul(pden[0:1, q * 128:(q + 1) * 128], lhsT=Vx[par][:, blk_prev, 128:129], rhs=rhs_prev, start=True, stop=False)
                                            return PE.matmul(pden[0:1, q * 128:(q + 1) * 128], lhsT=Vx[par][:, blk_cur, 128:129], rhs=rhs_cur, start=False, stop=True)
                                        T.op("pe", pv_fn, reads=[Vn, f"pt{ptp}"], writes=PN + PD)
                                    chk(f"s2{l}{st}{hp}{g}")
                                    if g == 0:
                                        an = accn[pe0:pe0 + 64, bi * 512:(bi + 1) * 512]
                                        ad = accd[0:1, e, bi * 512:(bi + 1) * 512]
                                        pn = pnum[pe0:pe0 + 64, :]
                                        pd = pden[0:1, :]
                                    elif g == 1:
                                        r = bun[0][0]
                                        an = accn[pe0:pe0 + 64, :].rearrange("p (qb i r) -> p r qb i", r=4, i=128)[:, r]
                                        ad = accd[0:1, e, :].rearrange("p (qb i r) -> p r qb i", r=4, i=128)[:, r]
                                        pn = pnum[pe0:pe0 + 64, :].rearrange("p (qb i) -> p qb i", i=128)
                                        pd = pden[0:1, :].rearrange("p (qb i) -> p qb i", i=128)
                                    else:
                                        r0 = bun[0][0]
                                        an = accn[pe0:pe0 + 64, :].rearrange("p (i r) -> p r i", r=16)[:, r0:r0 + 4]
                                        ad = accd[0:1, e, :].rearrange("p (i r) -> p r i", r=16)[:, r0:r0 + 4]
                                        pn = pnum[pe0:pe0 + 64, :].rearrange("p (q i) -> p q i", i=128)
                                        pd = pden[0:1, :].rearrange("p (q i) -> p q i", i=128)
                                    if g == 0:
                                        T.op("act", lambda: A.copy(out=an, in_=pn), reads=PN, writes=["accn"])
                                        T.op("dve", lambda: V.tensor_copy(out=ad, in_=pd), reads=PD, writes=["accd"])
                                    else:
                                        T.op("dve", lambda: V.tensor_tensor(out=an, in0=an, in1=pn, op=ALU.add), reads=PN + ["accn"], writes=["accn"])
                                        T.op("dve", lambda: V.tensor_tensor(out=ad, in0=ad, in1=pd, op=ALU.add), reads=PD + ["accd"], writes=["accd"])
                            chk(f"att{l}{st}{hp}{g}")
                        T.op("dve", lambda: V.reciprocal(out=accd[:], in_=accd[:]), reads=["accd"], writes=["accd"])
                        for e in range(2):
                            pe0 = 64 * e
                            for tt in range(4):
                                bk = ps[2 + tt % 2]
                                bn = P(2 + tt % 2)
                                mm(bk[:, :], [(ones32[0:1, :], accd[0:1, e, tt * 512:(tt + 1) * 512])], ["ones32", "accd"], bn)
                                T.op("dve", lambda: V.tensor_tensor(out=attnT[pe0:pe0 + 64, hp, tt * 512:(tt + 1) * 512], in0=accn[pe0:pe0 + 64, tt * 512:(tt + 1) * 512],
                                                                    in1=bk[pe0:pe0 + 64, :], op=ALU.mult), reads=bn + ["accn"], writes=["attnT"])
                    T.drain("sp")
                    nc.all_engine_barrier()
                chk(f"B{l}{st}")
                with ExitStack() as esC:
                    sbC = lambda name, shape, dt: esC.enter_context(nc.sbuf_tensor(un(name), shape, dt))
                    xt = sbC("xt_c", [128, KC, 512], F32)
                    wo = sbC("wo", [128, KC, D], BF16)
                    for c in range(KC):
                        dma("pool", wo[:, c, :], wo_in[a, c * 128:(c + 1) * 128, :], writes=["wo"])
                    for tt in range(4):
                        t0 = base + tt * 512
                        xr = [f"xr{t0 // 256}", f"xr{t0 // 256 + 1}"]
                        dma("sp", xt[:], x_view(src, t0, 512), reads=xr, writes=["xt_c"])
                        for ec in range(KC):
                            bk = ps[ec % 2]
                            bn = P(ec % 2)
                            mm(bk[:, :], [(wo[:, c, ec * 128:(ec + 1) * 128], attnT[:, c, tt * 512:(tt + 1) * 512]) for c in range(KC)], ["wo", "attnT"], bn)
                            T.op("dve", lambda: V.tensor_tensor(out=xt[:, ec, :], in0=xt[:, ec, :], in1=bk[:, :], op=ALU.add), reads=bn + ["xt_c"], writes=["xt_c"])
                        dma("sp", x_view(xres, t0, 512), xt[:], reads=["xt_c"], writes=xr)
                    T.drain("sp")
                    nc.all_engine_barrier()
        T.drain("sp")
        nc.all_engine_barrier()

    def sample_attention(l):
        a = l // 2
        with ExitStack() as es:
            sb = lambda name, shape, dt: es.enter_context(nc.sbuf_tensor(un(name), shape, dt))
            qs = sb("qs", [NS, 9 * D], F32)
            prod0 = sb("prod0", [NS, D], F32)
            sc0 = sb("sc0", [NS, 16], F32)
            pv0 = [sb(f"pv0_{g}", [NS, D + 16], F32) for g in range(3)]
            Kc = [sb(f"Kc{i}", [128, D], F32) for i in range(2)]
            Vc = [sb(f"Vc{i}", [128, D], F32) for i in range(2)]
            prod = sb("prod", [128, D], F32)
            sc = sb("sc", [128, 16], F32)
            pvc = [sb(f"pvc{i}", [128, D + 16], F32) for i in range(2)]
            rec = sb("rec", [1, 16], F32)
            arow = sb("arow", [1, D], F32)
            asT = sb("asT", [128, KC, NS], BF16)
            wo = sb("wo_s", [128, KC, D], BF16)
            for c in range(KC):
                dma("pool", wo[:, c, :], wo_in[a, c * 128:(c + 1) * 128, :], writes=["wo_s"])
            dma("sp", qs[:], qkvs_scr[:, :], reads=["qkvs_scr"], writes=["qs"])
            for g in range(3):
                qg = qs[:, (g * 3) * D:(g * 3 + 1) * D]
                kg = qs[:, (g * 3 + 1) * D:(g * 3 + 2) * D]
                vg = qs[:, (g * 3 + 2) * D:(g * 3 + 3) * D]
                T.op("dve", lambda: V.tensor_tensor(out=prod0[:], in0=qg, in1=kg, op=ALU.mult), reads=["qs"], writes=["prod0"])
                T.op("dve", lambda: V.tensor_reduce(out=sc0[:], in_=prod0[:].rearrange("p (h d) -> p h d", d=64), axis=AX.X, op=ALU.add),
                     reads=["prod0"], writes=["sc0"])
                T.op("dve", lambda: V.scalar_tensor_tensor(out=sc0[:], in0=sc0[:], scalar=0.125, in1=B0[:, g * 16:(g + 1) * 16], op0=ALU.mult, op1=ALU.add),
                     reads=["sc0", "B0"], writes=["sc0"])
                T.op("act", lambda: A.activation(out=pv0[g][:, D:D + 16], in_=sc0[:], func=AF.Exp), reads=["sc0"], writes=[f"pv0_{g}p"])
                T.op("dve", lambda: V.tensor_tensor(out=pv0[g][:, 0:D].rearrange("p (h d) -> p h d", d=64), in0=vg.rearrange("p (h d) -> p h d", d=64),
                                                    in1=pv0[g][:, D:D + 16].unsqueeze(2).broadcast_to([NS, 16, 64]), op=ALU.mult),
                     reads=["qs", f"pv0_{g}p"], writes=[f"pv0_{g}"])
            ci = 0
            for b in range(NS):
                pieces = [(0, 512), (512, 1024), (1024, 1040)]
                pouts = [ps[2][0:1, 0:512], ps[3][0:1, 0:512], psY[0:1, 0:16]]
                pnames = [P(2), P(3), PY(0, 16)]
                for g in range(3):
                    par = ci % 2
                    ci += 1
                    dma("sp", Kc[par][:], ck[g][a, b], writes=[f"Kc{par}"])
                    dma("sp", Vc[par][:], cv[g][a, b], writes=[f"Vc{par}"])
                    qg = qs[:, (g * 3) * D:(g * 3 + 1) * D]
                    mm(ps[0][:, :], [(sel4[:, b * 128:(b + 1) * 128], qg[:, 0:512])], ["sel4", "qs"], P(0))
                    mm(ps[1][:, :], [(sel4[:, b * 128:(b + 1) * 128], qg[:, 512:1024])], ["sel4", "qs"], P(1))
                    T.op("dve", lambda: V.tensor_tensor(out=prod[:, 0:512], in0=Kc[par][:, 0:512], in1=ps[0][:, :], op=ALU.mult), reads=[f"Kc{par}"] + P(0), writes=["prodA"])
                    T.op("dve", lambda: V.tensor_tensor(out=prod[:, 512:1024], in0=Kc[par][:, 512:1024], in1=ps[1][:, :], op=ALU.mult), reads=[f"Kc{par}"] + P(1), writes=["prodB"])
                    T.op("dve", lambda: V.tensor_reduce(out=sc[:], in_=prod[:].rearrange("p (h d) -> p h d", d=64), axis=AX.X, op=ALU.add),
                         reads=["prodA", "prodB"], writes=["sc"])
                    T.op("dve", lambda: V.scalar_tensor_tensor(out=sc[:], in0=sc[:], scalar=0.125, in1=SB[:, g * 16:(g + 1) * 16], op0=ALU.mult, op1=ALU.add),
                         reads=["sc", "SB"], writes=["sc"])
                    T.op("act", lambda: A.activation(out=pvc[par][:, D:D + 16], in_=sc[:], func=AF.Exp), reads=["sc"], writes=[f"pvc{par}p"])
                    T.op("dve", lambda: V.tensor_tensor(out=pvc[par][:, 0:D].rearrange("p (h d) -> p h d", d=64), in0=Vc[par][:].rearrange("p (h d) -> p h d", d=64),
                                                        in1=pvc[par][:, D:D + 16].unsqueeze(2).broadcast_to([128, 16, 64]), op=ALU.mult),
                         reads=[f"Vc{par}", f"pvc{par}p"], writes=[f"pvc{par}"])
                    for pi, (c0, c1) in enumerate(pieces):
                        def nd_fn():
                            PE.matmul(pouts[pi][:, 0:c1 - c0], lhsT=ones32[:, 0:1], rhs=pvc[par][:, c0:c1], start=(g == 0), stop=False)
                            return PE.matmul(pouts[pi][:, 0:c1 - c0], lhsT=sel4[:, b * 128:b * 128 + 1], rhs=pv0[g][:, c0:c1], start=False, stop=(g == 2))
                        T.op("pe", nd_fn, reads=["ones32", "sel4", f"pvc{par}", f"pvc{par}p", f"pv0_{g}", f"pv0_{g}p"], writes=pnames[pi])
                T.op("dve", lambda: V.reciprocal(out=rec[:], in_=psY[0:1, 0:16]), reads=PY(0, 16), writes=["rec"])
                for pi in range(2):
                    T.op("dve", lambda: V.tensor_tensor(out=arow[:, pi * 512:(pi + 1) * 512].rearrange("p (h d) -> p h d", d=64),
                                                        in0=pouts[pi].rearrange("p (h d) -> p h d", d=64),
                                                        in1=rec[:, pi * 8:(pi + 1) * 8].unsqueeze(2).broadcast_to([1, 8, 64]), op=ALU.mult),
                         reads=pnames[pi] + ["rec"], writes=["arow%d" % pi])
                for c in range(KC):
                    mm(psY[:, 512 + c:512 + c + 1], [(arow[0:1, c * 128:(c + 1) * 128], ones32[0:1, 0:1])], ["arow0", "arow1", "ones32"], PY(512, 520))
                T.op("dve", lambda: V.tensor_copy(out=asT[:, :, b], in_=psY[:, 512:512 + KC]), reads=PY(512, 520), writes=["asT"])
            for ec in range(KC):
                mm(psY[:, 1024 + ec * NS:1024 + (ec + 1) * NS], [(wo[:, c, ec * 128:(ec + 1) * 128], asT[:, c, :]) for c in range(KC)], ["wo_s", "asT"], PY(1024, 1056))
            T.op("dve", lambda: V.tensor_tensor(out=xs[:], in0=xs[:], in1=psY[:, 1024:1024 + KC * NS].rearrange("p (c n) -> p c n", n=NS), op=ALU.add),
                 reads=PY(1024, 1056) + ["xs"], writes=["xs"])
        T.drain("sp")
        nc.all_engine_barrier()

    def tile_pass(l):
        odd = (l % 2 == 1)
        pb = l // 2
        last = (l == 3)
        with ExitStack() as es:
            sb = lambda name, shape, dt: es.enter_context(nc.sbuf_tensor(un(name), shape, dt))
            win = sb("win", [128, KC, 2 * DFF], BF16)
            wout = sb("wout", [128, FC, D], BF16)
            xt = [sb(f"xt{i}", [128, KC, FT], F32) for i in range(2)]
            sq = sb("sq", [128, KC, FT], BF16)
            rstd = sb("rstd", [128, FT], F32)
            hb = sb("hb", [128, KC, FT], BF16)
            gext = [sb(f"gext{i}", [128, FT + 2], F32) for i in range(4)]
            cvt = [sb(f"cvt{i}", [128, FT], F32) for i in range(4)]
            actc = [sb(f"actc{i}", [128, FT], BF16) for i in range(4)]
            ghalo = sb("ghalo", [128, FC, 2], F32)
            pre = sb("pre", [128, 2, FC, NS], F32)
            gs = sb("gs", [128, FC, NS], F32)
            if odd:
                wpl = sb("wpl", [128, 2, 4, 256], BF16)
                hx = sb("hx", [128, KC, 16 + FT], F32)
                hxs = sb("hxs", [128, KC, NS, 16], F32)
                hso = sb("hso", [128, KC, NS], F32)
                s1 = sb("s1", [128, 16 + FT], F32)
                s2 = sb("s2", [128, 16 + FT], F32)
                zb = sb("zb", [128, KC, FT], BF16)
                pcorr = sb("pcorr", [128, 4, 16], F32)
                dma("sp", pcorr[:], pcorr_in.rearrange("p (g t) -> p g t", g=4), writes=["pcorr"])
                for g in range(4):
                    dma("pool", wpl[:, :, g, :], wpool_in[pb, g].rearrange("(kc p) n -> p kc n", p=128), writes=["wpl"])
                dma("sp", hxs[:, :, :, 0:15], spoolT_in[pb], writes=["hxs"])
                dma("sp", pools14[pb], spool_in[pb, :, 1:15, :], reads=[])
            for kc in range(KC):
                dma("pool", win[:, kc, :], win_in[l, kc * 128:(kc + 1) * 128, :], writes=["win"])
            for f in range(FC):
                dma("pool", wout[:, f, :], wout_in[l, f * 128:(f + 1) * 128, :], writes=["wout"])
            dma("sp", pre[:], sconvT_in[l], writes=["pre"])
            dma("sp", convs0[l], sconv_in[l, :, 1, :])
            T.op("pool", lambda: G.memset(ghalo[:], 0.0), writes=["ghalo"])
            if odd:
                T.op("pool", lambda: G.memset(hx[:, :, 0:16], 0.0), writes=["hx"])

            ntile = NT // FT
            fi = 0
            for ti in range(ntile + 1):
                sample = (ti == ntile)
                n = NS if sample else FT
                t0 = ti * FT
                if sample:
                    x = xs
                    xn = "xs"
                    xa = xs[:]
                else:
                    par = ti % 2
                    x = xt[par]
                    xn = f"xt{par}"
                    xa = x[:]
                    dma("sp", xa, x_view(xres, t0, FT), reads=[f"xr{ti}"], writes=[xn])
                if odd:
                    rms_rstd(xa, n, sq[:, :, 0:n], rstd[:, 0:n], ps[0][:, 0:n], xn, "t")
                    if not sample:
                        hv = lambda c, lo, hi: hx[:, c, lo:hi]
                        for c in range(KC):
                            T.op("dve", lambda: V.scalar_tensor_tensor(out=hx[:, c, 16:16 + n], in0=x[:, c, :], scalar=pv[:, PV_NM + l * 8 + c:PV_NM + l * 8 + c + 1],
                                                                       in1=rstd[:, 0:n], op0=ALU.mult, op1=ALU.mult), reads=[xn, "pv", "rstdt"], writes=["hx"])
                        if ti == ntile - 1:
                            dma("sp", poolpT[pb].rearrange("(kc p) t -> p kc t", p=128), hx[:, :, 16 + FT - 15:16 + FT], reads=["hx"], nonc=True)
                    else:
                        for c in range(KC):
                            T.op("dve", lambda: V.scalar_tensor_tensor(out=hxs[:, c, :, 15], in0=x[:, c, :], scalar=pv[:, PV_NM + l * 8 + c:PV_NM + l * 8 + c + 1],
                                                                       in1=rstd[:, 0:n], op0=ALU.mult, op1=ALU.mult), reads=[xn, "pv", "rstdt"], writes=["hxs"])
                        T.op("pool", lambda: G.tensor_copy(out=hso[:], in_=hxs[:, :, :, 15]), reads=["hxs"], writes=["hso"])
                        dma("sp", poolsT[pb].rearrange("(kc p) n -> p kc n", p=128), hso[:], reads=["hso"], nonc=True)
                    for c in range(KC):
                        grp = c // 2
                        w = (2, 4, 8, 16)[grp]
                        steps = grp + 1
                        if not sample:
                            L = 16 + FT
                            cur = lambda lo, hi: hx[:, c, lo:hi]
                            bufs = [lambda lo, hi: s1[:, lo:hi], lambda lo, hi: s2[:, lo:hi]]
                            fin = lambda vw: vw(16, L)
                            hcur = hx[:, c, 16:L]
                            zo = zb[:, c, 0:n]
                        else:
                            L = 16
                            cur = lambda lo, hi: hxs[:, c, :, lo:hi]
                            s1v = s1[:, 0:NS * 16].rearrange("p (b t) -> p b t", t=16)
                            s2v = s2[:, 0:NS * 16].rearrange("p (b t) -> p b t", t=16)
                            bufs = [lambda lo, hi: s1v[:, :, lo:hi], lambda lo, hi: s2v[:, :, lo:hi]]
                            fin = lambda vw: vw(15, 16)
                            hcur = hxs[:, c, :, 15:16]
                            zo = zb[:, c, 0:n].unsqueeze(2)
                        srcv = cur
                        srcn = "hxs" if sample else "hx"
                        sh = 1
                        for k in range(steps):
                            dst = bufs[k % 2]
                            dn = "s1" if k % 2 == 0 else "s2"
                            lo = 2 * sh - 1
                            sv, sn_ = srcv, srcn
                            T.op("dve", lambda: V.tensor_tensor(out=dst(lo, L), in0=sv(lo, L), in1=sv(lo - sh, L - sh), op=ALU.add), reads=[sn_], writes=[dn])
                            srcv, srcn = dst, dn
                            sh *= 2
                        fv = fin(srcv)
                        if (not sample) and ti == 0:
                            sv = srcv
                            T.op("dve", lambda: V.tensor_tensor(out=sv(16, 32), in0=sv(16, 32), in1=pcorr[:, grp, :], op=ALU.mult), reads=[srcn, "pcorr"], writes=[srcn])
                        T.op("dve", lambda: V.scalar_tensor_tensor(out=zo, in0=fv, scalar=1.0 / w, in1=hcur, op0=ALU.mult, op1=ALU.subtract),
                             reads=[srcn, "hxs" if sample else "hx"], writes=["zb"])
                    for grp in range(4):
                        for eh in range(2):
                            ec = grp * 2 + eh
                            bk = ps[1 + ec % 2]
                            bn = P(1 + ec % 2, 0, n)
                            mm(bk[:, 0:n], [(wpl[:, kc2, grp, eh * 128:(eh + 1) * 128], zb[:, grp * 2 + kc2, 0:n]) for kc2 in range(2)], ["wpl", "zb"], bn)
                            T.op("dve", lambda: V.scalar_tensor_tensor(out=x[:, ec, :], in0=bk[:, 0:n], scalar=pv[:, PV_PSC + pb * 8 + ec:PV_PSC + pb * 8 + ec + 1],
                                                                       in1=x[:, ec, :], op0=ALU.mult, op1=ALU.add), reads=bn + ["pv", xn], writes=[xn])
                    if not sample:
                        T.op("pool", lambda: G.tensor_copy(out=hx[:, :, 0:16], in_=hx[:, :, FT:FT + 16]), reads=["hx"], writes=["hx"])
                rms_rstd(xa, n, sq[:, :, 0:n], rstd[:, 0:n], ps[0][:, 0:n], xn, "t")
                for kc in range(KC):
                    T.op("dve", lambda: V.scalar_tensor_tensor(out=hb[:, kc, 0:n], in0=x[:, kc, :], scalar=pv[:, PV_NF + l * 8 + kc:PV_NF + l * 8 + kc + 1],
                                                               in1=rstd[:, 0:n], op0=ALU.mult, op1=ALU.mult), reads=[xn, "pv", "rstdt"], writes=["hb"])
                def emit_y(f, p2, an):
                    def y_fn():
                        ins = None
                        for ec in range(KC):
                            ins = PE.matmul(psY[:, ec * 256:ec * 256 + n], lhsT=wout[:, f, ec * 128:(ec + 1) * 128], rhs=actc[p2][:, 0:n], start=(f == 0 and ec % 2 == 0), stop=(f == FC - 1))
                        return ins
                    T.op("pe", y_fn, reads=["wout", an], writes=PY())
                pend = []
                for f in range(FC):
                    p2 = fi % 4
                    fi += 1
                    bk = ps[p2]
                    bng = P(p2)
                    bnv = P(p2)

                    def gv_fn():
                        for kc in range(KC):
                            PE.matmul(bk[:, 0:n], lhsT=win[:, kc, f * 128:(f + 1) * 128], rhs=hb[:, kc, 0:n], start=(kc == 0), stop=(kc == KC - 1))
                        ins = None
                        for kc in range(KC):
                            ins = PE.matmul(bk[:, 256:256 + n], lhsT=win[:, kc, DFF + f * 128:DFF + (f + 1) * 128], rhs=hb[:, kc, 0:n], start=(kc == 0), stop=(kc == KC - 1))
                        return ins
                    T.op("pe", gv_fn, reads=["win", "hb"], writes=bng)
                    cw = lambda j: pv[:, PV_CW + (l * 3 + j) * FC + f:PV_CW + (l * 3 + j) * FC + f + 1]
                    cb = pv[:, PV_CB + l * FC + f:PV_CB + l * FC + f + 1]
                    cvn = f"cvt{p2}"
                    T.op("act", lambda: A.activation(out=cvt[p2][:, 0:n], in_=bk[:, 0:n], func=AF.Identity, bias=cb, scale=cw(2)), reads=bng + ["pv"], writes=[cvn])
                    if not sample:
                        gn = f"gext{p2}"
                        T.op("pool", lambda: G.tensor_copy(out=gext[p2][:, 0:2], in_=ghalo[:, f, :]), reads=["ghalo"], writes=[gn + "h"])
                        T.op("act", lambda: A.copy(out=gext[p2][:, 2:2 + n], in_=bk[:, 0:n]), reads=bng, writes=[gn])
                        T.op("pool", lambda: G.tensor_copy(out=ghalo[:, f, :], in_=gext[p2][:, n:n + 2]), reads=[gn], writes=["ghalo"])
                        T.op("dve", lambda: V.scalar_tensor_tensor(out=cvt[p2][:, 0:n], in0=gext[p2][:, 1:1 + n], scalar=cw(1), in1=cvt[p2][:, 0:n], op0=ALU.mult, op1=ALU.add),
                             reads=[gn, gn + "h", "pv", cvn], writes=[cvn])
                        T.op("dve", lambda: V.scalar_tensor_tensor(out=cvt[p2][:, 0:n], in0=gext[p2][:, 0:n], scalar=cw(0), in1=cvt[p2][:, 0:n], op0=ALU.mult, op1=ALU.add),
                             reads=[gn, gn + "h", "pv", cvn], writes=[cvn])
                    else:
                        T.op("act", lambda: A.copy(out=gs[:, f, :], in_=bk[:, 0:n]), reads=bng, writes=["gs"])
                        T.op("dve", lambda: V.scalar_tensor_tensor(out=cvt[p2][:, 0:n], in0=pre[:, 1, f, :], scalar=cw(1), in1=cvt[p2][:, 0:n], op0=ALU.mult, op1=ALU.add),
                             reads=["pre", "pv", cvn], writes=[cvn])
                        T.op("dve", lambda: V.scalar_tensor_tensor(out=cvt[p2][:, 0:n], in0=pre[:, 0, f, :], scalar=cw(0), in1=cvt[p2][:, 0:n], op0=ALU.mult, op1=ALU.add),
                             reads=["pre", "pv", cvn], writes=[cvn])
                    T.op("act", lambda: A.activation(out=cvt[p2][:, 0:n], in_=cvt[p2][:, 0:n], func=AF.Silu), reads=[cvn], writes=[cvn])
                    an = f"actc{p2}"
                    T.op("dve", lambda: V.tensor_tensor(out=actc[p2][:, 0:n], in0=cvt[p2][:, 0:n], in1=bk[:, 256:256 + n], op=ALU.mult), reads=[cvn] + bnv, writes=[an])

                    pend.append((f, p2, an))
                    if len(pend) > 2:
                        emit_y(*pend.pop(0))
                while pend:
                    emit_y(*pend.pop(0))
                yv = psY[:, :].rearrange("p (c t) -> p c t", t=256)[:, :, 0:n]
                T.op("dve", lambda: V.tensor_tensor(out=xa, in0=xa, in1=yv, op=ALU.add), reads=PY() + [xn], writes=[xn])
                if not last:
                    if not sample:
                        dma("sp", x_view(xres, t0, FT), xa, reads=[xn], writes=[f"xr{ti}"])
                else:
                    rms_rstd(xa, n, sq[:, :, 0:n], rstd[:, 0:n], ps[0][:, 0:n], xn, "t")
                    for kc in range(KC):
                        T.op("dve", lambda: V.scalar_tensor_tensor(out=x[:, kc, :], in0=x[:, kc, :], scalar=pv[:, PV_FIN + kc:PV_FIN + kc + 1],
                                                                   in1=rstd[:, 0:n], op0=ALU.mult, op1=ALU.mult), reads=[xn, "pv", "rstdt"], writes=[xn])
                    if not sample:
                        dma("sp", x_view(yT, t0, FT), xa, reads=[xn])
                    else:
                        dma("sp", ysT.rearrange("(kc p) n -> p kc n", p=128), xa, reads=[xn], nonc=True)
                if (not sample) and ti == ntile - 1:
                    dma("sp", convpT[l].rearrange("(f p) j -> p f j", p=128), ghalo[:], reads=["ghalo"], nonc=True)
                if sample:
                    dma("sp", convsT[l].rearrange("(f p) n -> p f n", p=128), gs[:], reads=["gs"], nonc=True)
        T.drain("sp")
        nc.all_engine_barrier()

    try:
        for l in range(4):
            if l % 2 == 0:
                attention_pass(l)
                chk(f"attn{l}")
                sample_attention(l)
                chk(f"sattn{l}")
            tile_pass(l)
            chk(f"tile{l}")
    except _Stop:
        pass
    T.drain("sp")
    T.drain("act")
    return nc


_NC = None


def kernel(**inp):
    global _NC
    f32 = np.float32
    g = lambda k: np.asarray(inp[k], dtype=f32)
    x_prompt, x_sample = g("x_prompt"), g("x_sample")
    caches_k = [g("cache_k_w128"), g("cache_k_w512"), g("cache_k_w2048")]
    caches_v = [g("cache_v_w128"), g("cache_v_w512"), g("cache_v_w2048")]
    state_pool, state_conv = g("state_pool"), g("state_conv")
    norm_mix, norm_ffn, norm_final = g("norm_mix"), g("norm_ffn"), g("norm_final")
    pool_scale, conv_w, conv_b = g("pool_scale"), g("conv_w"), g("conv_b")

    fm = lambda v: np.ascontiguousarray(v.reshape(-1, 128).T)
    pvec = np.zeros((128, NPV), f32)
    for l in range(4):
        pvec[:, PV_NM + l * 8:PV_NM + (l + 1) * 8] = fm(norm_mix[l])
        pvec[:, PV_NF + l * 8:PV_NF + (l + 1) * 8] = fm(norm_ffn[l])
        pvec[:, PV_CB + l * FC:PV_CB + (l + 1) * FC] = fm(conv_b[l])
        for j in range(3):
            pvec[:, PV_CW + (l * 3 + j) * FC:PV_CW + (l * 3 + j + 1) * FC] = fm(conv_w[l, j])
    pvec[:, PV_FIN:PV_FIN + 8] = fm(norm_final)
    for b in range(2):
        pvec[:, PV_PSC + b * 8:PV_PSC + (b + 1) * 8] = fm(pool_scale[b])
    consts = host_consts()
    shared = dict(rel_bias=g("rel_bias"), pvec=pvec, w_qkv=g("w_qkv"), w_o=g("w_o"), w_pool=g("w_pool"),
                  w_in=g("w_in"), w_out=g("w_out"), **consts)
    in_maps = []
    for c in range(8):
        b = c % 4
        sl = slice(NS * c, NS * c + NS)
        m = dict(shared)
        m["xT"] = np.ascontiguousarray(x_prompt[b].T)
        m["xsT"] = np.ascontiguousarray(x_sample[sl, 0, :].T)
        for gi in range(3):
            dil = DILS[gi]
            m[f"ck{gi}"] = np.ascontiguousarray(caches_k[gi][:, sl, ::dil]).reshape(2, NS, 128, D)
            m[f"cv{gi}"] = np.ascontiguousarray(caches_v[gi][:, sl, ::dil]).reshape(2, NS, 128, D)
        sp = state_pool[:, sl]
        m["spool"] = np.ascontiguousarray(sp)
        m["spoolT"] = np.ascontiguousarray(sp.reshape(2, NS, 15, KC, 128).transpose(0, 4, 3, 1, 2))
        scv = state_conv[:, sl]
        m["sconv"] = np.ascontiguousarray(scv)
        m["sconvT"] = np.ascontiguousarray(scv.reshape(4, NS, 2, FC, 128).transpose(0, 4, 2, 3, 1))
        in_maps.append(m)
    if _NC is None:
        _NC = build()
    res = run_bass_kernel_spmd(_NC, in_maps, core_ids=list(range(8)))
    R = res.results
    B = 4
    y_prompt = np.stack([R[b]["yT"].T for b in range(B)]).astype(f32)
    y_sample = np.concatenate([R[c]["ysT"].T for c in range(8)], 0).reshape(32, 1, D).astype(f32)
    outs = [y_prompt, y_sample]
    for gi in range(3):
        kp = np.stack([R[b][f"kp{gi}T"].transpose(0, 2, 1) for b in range(B)], 1)
        vpp = np.stack([R[b][f"vp{gi}"] for b in range(B)], 1)
        outs.append(np.ascontiguousarray(kp).reshape(2, B, WINS[gi], 16, 64).astype(f32))
        outs.append(np.ascontiguousarray(vpp).reshape(2, B, WINS[gi], 16, 64).astype(f32))
    qk = np.concatenate([R[c]["qkvs"] for c in range(8)], 1)
    for gi in range(3):
        outs.append(np.ascontiguousarray(qk[:, :, (gi * 3 + 1) * D:(gi * 3 + 2) * D]).reshape(2, 32, 1, 16, 64).astype(f32))
        outs.append(np.ascontiguousarray(qk[:, :, (gi * 3 + 2) * D:(gi * 3 + 3) * D]).reshape(2, 32, 1, 16, 64).astype(f32))
    outs.append(np.stack([R[b]["poolpT"].transpose(0, 2, 1) for b in range(B)], 1).astype(f32))
    ps_ = np.zeros((2, 32, 15, D), f32)
    for c in range(8):
        ps_[:, NS * c:NS * c + NS, 0:14] = R[c]["pools14"]
        ps_[:, NS * c:NS * c + NS, 14] = R[c]["poolsT"].transpose(0, 2, 1)
    outs.append(ps_)
    outs.append(np.stack([R[b]["convpT"].transpose(0, 2, 1) for b in range(B)], 1).astype(f32))
    cs_ = np.zeros((4, 32, 2, DFF), f32)
    for c in range(8):
        cs_[:, NS * c:NS * c + NS, 0] = R[c]["convs0"]
        cs_[:, NS * c:NS * c + NS, 1] = R[c]["convsT"].transpose(0, 2, 1)
    outs.append(cs_)
    return tuple(outs)
```

```python
from contextlib import ExitStack
import numpy as np
import concourse.bass as bass
import concourse.mybir as mybir
from concourse.bass_utils import run_bass_kernel_spmd

F32 = mybir.dt.float32
BF16 = mybir.dt.bfloat16
ALU = mybir.AluOpType
AF = mybir.ActivationFunctionType
AX = mybir.AxisListType

D = 1024
KC = 8
NT = 4096
ST = 2048
NST = NT // ST
NS = 4
DFF = 2816
FC = 22
FT = 256
VW = 144
WINS = (128, 512, 2048)
DILS = (1, 4, 16)
EPS = 1e-6
NEG = -30000.0
NPV = 440
PV_NM, PV_NF, PV_FIN, PV_PSC, PV_CW, PV_CB = 0, 32, 64, 72, 88, 352


DEBUG_STOP = None


class _Stop(Exception):
    pass


class Trk:
    def __init__(self, nc):
        self.nc = nc
        self.E = dict(pe=nc.tensor, act=nc.scalar, dve=nc.vector, pool=nc.gpsimd, sp=nc.sync)
        self.sem = {}
        self.cnt = {}
        for n in ["pe", "act", "dve", "pool", "q_sp", "q_act", "q_pool"]:
            self.sem[n] = nc.alloc_semaphore("s_" + n)
            self.cnt[n] = 0
        self.waited = {}
        self.lastw = {}
        self.readers = {}

    def _wait(self, eng, deps):
        best = {}
        for (s, v) in deps:
            if eng == "pe" and s == "pe":
                continue
            if best.get(s, 0) < v:
                best[s] = v
        for s, v in best.items():
            if s.startswith("q_"):
                v = self.cnt[s]
            if self.waited.get((eng, s), 0) < v:
                self.E[eng].wait_ge(self.sem[s], v)
                self.waited[(eng, s)] = v

    def op(self, eng, fn, reads=(), writes=(), dma=False):
        reads = list(dict.fromkeys(reads))
        writes = list(dict.fromkeys(list(writes) + [r for r in reads if isinstance(r, str) and r[0] == "P" and r[1:].isdigit()]))
        deps = set()
        for r in reads:
            if r in self.lastw:
                deps.add(self.lastw[r])
        for w in writes:
            if w in self.lastw:
                deps.add(self.lastw[w])
            for t in self.readers.get(w, ()):
                deps.add(t)
        self._wait(eng, deps)
        ins = fn()
        if dma:
            s = "q_" + eng
            self.cnt[s] += 16
            ins.then_inc(self.sem[s], 16)
        else:
            s = eng
            self.cnt[s] += 1
            ins.then_inc(self.sem[s], 1)
        tok = (s, self.cnt[s])
        for w in writes:
            self.lastw[w] = tok
            self.readers[w] = []
        for r in reads:
            if r in writes:
                continue
            lst = self.readers.setdefault(r, [])
            lst.append(tok)
            if len(lst) > 48:
                best = {}
                for (s2, v2) in lst:
                    if best.get(s2, 0) < v2:
                        best[s2] = v2
                self.readers[r] = list(best.items())
        return ins

    def drain(self, eng="sp"):
        for s, v in self.cnt.items():
            if v > 0 and self.waited.get((eng, s), 0) < v:
                self.E[eng].wait_ge(self.sem[s], v)
                self.waited[(eng, s)] = v


def t5_buckets(dist):
    n_b, max_d = 32, 2048
    max_exact = n_b // 2
    n = np.maximum(dist, 1).astype(np.float32)
    large = max_exact + (np.log(n / max_exact) / np.log(max_d / max_exact) * (n_b - max_exact)).astype(np.int32)
    large = np.minimum(large, n_b - 1)
    return np.where(dist < max_exact, dist, large).astype(np.int32)


def host_consts():
    c = {}
    ohpad = np.zeros((3, 32, 384), np.float32)
    ohs = np.zeros((3, 32, 128), np.float32)
    for g in range(3):
        bk = t5_buckets(np.arange(129) * DILS[g])
        for rel in range(129):
            ohpad[g, bk[rel], 127 + rel] = 1.0
        for p in range(128):
            ohs[g, bk[128 - p], p] = 1.0
    maskpad = np.full((16, 384), NEG, np.float32)
    maskpad[:, 127:256] = 0.0
    selh = np.zeros((16, 16, 128), np.float32)
    for h in range(16):
        selh[h, h, :] = 1.0
    oh0 = np.zeros((32, NS), np.float32)
    oh0[0, :] = 1.0
    sel4 = np.zeros((NS, NS, 128), np.float32)
    for b in range(NS):
        sel4[b, b, :] = 1.0
    c["ohpad"] = ohpad
    c["ohs"] = ohs
    c["maskpad"] = maskpad
    c["selh"] = selh.reshape(16, 16 * 128)
    c["oh0"] = oh0
    c["sel4"] = sel4.reshape(NS, NS * 128)
    c["ident"] = np.eye(128, dtype=np.float32)
    pc = np.ones((3, 128, 16), np.float32)
    for gi, w in enumerate((2, 4, 8)):
        pass
    pcorr = np.ones((4, 128, 16), np.float32)
    for gi, w in enumerate((2, 4, 8, 16)):
        for t in range(16):
            pcorr[gi, :, t] = w / min(w, t + 1)
    c["pcorr"] = np.ascontiguousarray(pcorr.transpose(1, 0, 2)).reshape(128, 64)
    return c


def build():
    nc = bass.Bass("TRN2", target_bir_lowering=False)
    T = Trk(nc)
    _uid = [0]

    def chk(name):
        if DEBUG_STOP is not None and name == DEBUG_STOP:
            raise _Stop()

    def un(name):
        _uid[0] += 1
        return f"sb_{name}_{_uid[0]}"

    def din(name, shape):
        return nc.dram_tensor(name, list(shape), F32, kind="ExternalInput")

    def dout(name, shape):
        return nc.dram_tensor(name, list(shape), F32, kind="ExternalOutput")

    xT_in = din("xT", [D, NT]).ap()
    xsT_in = din("xsT", [D, NS]).ap()
    ck = [din(f"ck{g}", [2, NS, 128, D]).ap() for g in range(3)]
    cv = [din(f"cv{g}", [2, NS, 128, D]).ap() for g in range(3)]
    spoolT_in = din("spoolT", [2, 128, KC, NS, 15]).ap()
    spool_in = din("spool", [2, NS, 15, D]).ap()
    sconvT_in = din("sconvT", [4, 128, 2, FC, NS]).ap()
    sconv_in = din("sconv", [4, NS, 2, DFF]).ap()
    relb_in = din("rel_bias", [32, 48]).ap()
    pvec_in = din("pvec", [128, NPV]).ap()
    wqkv_in = din("w_qkv", [2, D, 9 * D]).ap()
    wo_in = din("w_o", [2, D, D]).ap()
    wpool_in = din("w_pool", [2, 4, 256, 256]).ap()
    win_in = din("w_in", [4, D, 2 * DFF]).ap()
    wout_in = din("w_out", [4, DFF, D]).ap()
    ohpad_in = din("ohpad", [3, 32, 384]).ap()
    ohs_in = din("ohs", [3, 32, 128]).ap()
    maskpad_in = din("maskpad", [16, 384]).ap()
    selh_in = din("selh", [16, 16 * 128]).ap()
    oh0_in = din("oh0", [32, NS]).ap()
    sel4_in = din("sel4", [NS, NS * 128]).ap()
    ident_in = din("ident", [128, 128]).ap()
    pcorr_in = din("pcorr", [128, 64]).ap()

    yT = dout("yT", [D, NT]).ap()
    ysT = dout("ysT", [D, NS]).ap()
    kpT = [dout(f"kp{g}T", [2, D, WINS[g]]).ap() for g in range(3)]
    vp = [dout(f"vp{g}", [2, WINS[g], D]).ap() for g in range(3)]
    qkvs_out = dout("qkvs", [2, NS, 9 * D]).ap()
    poolpT = dout("poolpT", [2, D, 15]).ap()
    poolsT = dout("poolsT", [2, D, NS]).ap()
    pools14 = dout("pools14", [2, NS, 14, D]).ap()
    convpT = dout("convpT", [4, DFF, 2]).ap()
    convsT = dout("convsT", [4, DFF, NS]).ap()
    convs0 = dout("convs0", [4, NS, DFF]).ap()

    xres = nc.dram_tensor("xres", [D, NT], F32).ap()
    scrb_h = nc.dram_tensor("scrb", [48, 128, 384], F32)
    scrb = scrb_h.ap()
    kh_scr = nc.dram_tensor("kh_scr", [24, 128, 16 * 128], BF16).ap()
    vh_scr = nc.dram_tensor("vh_scr", [24, 128, 16 * VW], BF16).ap()
    qkvs_scr = nc.dram_tensor("qkvs_scr", [NS, 9 * D], F32).ap()

    ps = [nc.alloc_psum_tensor(f"ps{i}", [128, 512], F32) for i in range(4)]
    psY = nc.alloc_psum_tensor("psY", [128, 2048], F32)

    def P(bank, lo=0, hi=512):
        return [f"P{bank}"]

    def PY(lo=0, hi=2048):
        out = []
        for bk in range(4):
            l2, h2 = max(lo, bk * 512), min(hi, (bk + 1) * 512)
            if l2 < h2:
                out += P(4 + bk, l2 - bk * 512, h2 - bk * 512)
        return out

    pv = nc.alloc_sbuf_tensor(un("pv"), [128, NPV], F32)
    meanm = nc.alloc_sbuf_tensor(un("meanm"), [128, 128], BF16)
    ones32 = nc.alloc_sbuf_tensor(un("ones32"), [128, 128], F32)
    identb = nc.alloc_sbuf_tensor(un("identb"), [128, 128], BF16)
    xs = nc.alloc_sbuf_tensor(un("xs"), [128, KC, NS], F32)
    SB = nc.alloc_sbuf_tensor(un("SB"), [128, 48], F32)
    B0 = nc.alloc_sbuf_tensor(un("B0"), [NS, 48], F32)
    sel4 = nc.alloc_sbuf_tensor(un("sel4"), [NS, NS * 128], F32)
    epsc = nc.alloc_sbuf_tensor(un("epsc"), [128, 1], F32)

    V = nc.vector
    A = nc.scalar
    G = nc.gpsimd
    PE = nc.tensor
    SP = nc.sync

    def dma(eng, out, in_, reads=(), writes=(), nonc=False):
        e = {"sp": SP, "pool": G, "act": A}[eng]
        if nonc:
            return T.op(eng, lambda: e.dma_start(out=out, in_=in_, allow_slow_non_contiguous=True), reads=reads, writes=writes, dma=True)
        return T.op(eng, lambda: e.dma_start(out=out, in_=in_), reads=reads, writes=writes, dma=True)

    def mm(out, pairs, reads, writes, first=True, last=True):
        def fn():
            ins = None
            n = len(pairs)
            for i, (l, r) in enumerate(pairs):
                ins = PE.matmul(out, lhsT=l, rhs=r, start=(first and i == 0), stop=(last and i == n - 1))
            return ins
        return T.op("pe", fn, reads=reads, writes=writes)

    dma("sp", pv[:], pvec_in[:, :], writes=["pv"])
    dma("sp", xs[:], xsT_in.rearrange("(kc p) n -> p kc n", p=128), writes=["xs"])
    dma("sp", sel4[:], sel4_in[:, :], writes=["sel4"])
    dma("pool", identb[:], ident_in[:, :], writes=["identb"])
    T.op("dve", lambda: V.memset(meanm[:], 1.0 / D), writes=["meanm"])
    T.op("dve", lambda: V.memset(ones32[:], 1.0), writes=["ones32"])
    T.op("dve", lambda: V.memset(epsc[:], EPS), writes=["epsc"])

    with ExitStack() as es:
        rb = es.enter_context(nc.sbuf_tensor(un("rb"), [32, 48], F32))
        ohp = es.enter_context(nc.sbuf_tensor(un("ohp"), [32, 3, 384], F32))
        ohs = es.enter_context(nc.sbuf_tensor(un("ohs"), [32, 3, 128], F32))
        oh0 = es.enter_context(nc.sbuf_tensor(un("oh0"), [32, NS], F32))
        mpad = es.enter_context(nc.sbuf_tensor(un("mpad"), [16, 384], F32))
        selh = es.enter_context(nc.sbuf_tensor(un("selh"), [16, 16 * 128], F32))
        fpad = es.enter_context(nc.sbuf_tensor(un("fpad"), [16, 384], F32))
        rep = [es.enter_context(nc.sbuf_tensor(un(f"rep{i}"), [128, 384], F32)) for i in range(2)]
        dma("sp", rb[:], relb_in[:, :], writes=["rb"])
        dma("sp", ohp[:], ohpad_in.rearrange("g k x -> k g x"), writes=["ohp"])
        dma("sp", ohs[:], ohs_in.rearrange("g k x -> k g x"), writes=["ohs"])
        dma("sp", oh0[:], oh0_in[:, :], writes=["oh0"])
        dma("sp", mpad[:], maskpad_in[:, :], writes=["mpad"])
        dma("sp", selh[:], selh_in[:, :], writes=["selh"])
        mm(ps[0][0:NS, 0:48], [(oh0[:], rb[:])], ["oh0", "rb"], P(0))
        T.op("dve", lambda: V.tensor_copy(out=B0[:], in_=ps[0][0:NS, 0:48]), reads=P(0), writes=["B0"])
        for g in range(3):
            mm(ps[1][:, g * 16:(g + 1) * 16], [(ohs[:, g, :], rb[:, g * 16:(g + 1) * 16])], ["ohs", "rb"], P(1))
        T.op("dve", lambda: V.tensor_copy(out=SB[:], in_=ps[1][:, 0:48]), reads=P(1), writes=["SB"])
        for g in range(3):
            mm(ps[2][0:16, 0:384], [(rb[:, g * 16:(g + 1) * 16], ohp[:, g, :])], ["rb", "ohp"], P(2))
            T.op("dve", lambda: V.scalar_tensor_tensor(out=fpad[:], in0=ps[2][0:16, 0:384], scalar=8.0, in1=mpad[:],
                                                       op0=ALU.mult, op1=ALU.add), reads=P(2) + ["mpad"], writes=["fpad"])
            for h in range(16):
                i = h % 2
                mm(ps[i][:, 0:384], [(selh[:, h * 128:(h + 1) * 128], fpad[:])], ["selh", "fpad"], P(i))
                T.op("act", lambda: A.copy(out=rep[i][:], in_=ps[i][:, 0:384]), reads=P(i), writes=[f"rep{i}"])
                dma("sp", scrb[g * 16 + h], rep[i][:], reads=[f"rep{i}"], writes=["scrb"])
    T.drain("sp")
    nc.all_engine_barrier()
    try:
        chk("setup")
    except _Stop:
        return nc

    def src_x(l):
        return xT_in if l == 0 else xres

    def x_view(ap, t0, n):
        return ap.rearrange("(kc p) t -> p kc t", p=128)[:, :, t0:t0 + n]

    def rms_rstd(xt_ap, n, sq, rstd, psm, xname, tag):
        pn_ = P(0, 0, n)
        T.op("act", lambda: A.activation(out=sq, in_=xt_ap, func=AF.Square), reads=[xname], writes=["sq" + tag])
        mm(psm, [(meanm[:], sq[:, kc, :]) for kc in range(KC)], ["meanm", "sq" + tag], pn_)
        T.op("act", lambda: A.activation(out=rstd, in_=psm, func=AF.Sqrt, bias=epsc[:, 0:1]), reads=pn_ + ["epsc"], writes=["rstd" + tag])
        T.op("dve", lambda: V.reciprocal(out=rstd, in_=rstd), reads=["rstd" + tag], writes=["rstd" + tag])

    def attention_pass(l):
        a = l // 2
        src = src_x(l)
        with ExitStack() as es:
            sb = lambda name, shape, dt: es.enter_context(nc.sbuf_tensor(un(name), shape, dt))
            hT = sb("hT", [128, KC, ST], BF16)
            attnT = sb("attnT", [128, KC, ST], BF16)
            BT = sb("BT", [128, 48, 256], BF16)
            hsT = sb("hsT", [128, KC, NS], BF16)
            sqs = sb("sqs", [128, KC, NS], BF16)
            rstds = sb("rstds", [128, NS], F32)

            for g in range(3):
                skew = bass.AP(scrb_h, 127 + g * 16 * 128 * 384, [[383, 128], [128 * 384, 16], [1, 256]])
                dma("pool", BT[:, g * 16:(g + 1) * 16, :], skew, reads=["scrb"], writes=["BT"])

            rms_rstd(xs[:], NS, sqs[:], rstds[:], ps[0][:, 0:NS], "xs", "s")
            for kc in range(KC):
                T.op("dve", lambda: V.scalar_tensor_tensor(out=hsT[:, kc, :], in0=xs[:, kc, :], scalar=pv[:, PV_NM + l * 8 + kc:PV_NM + l * 8 + kc + 1],
                                                           in1=rstds[:], op0=ALU.mult, op1=ALU.mult), reads=["xs", "pv", "rstds"], writes=["hsT"])
            it = 0
            for st in range(NST):
                base = st * ST
                with ExitStack() as esA:
                    sbA = lambda name, shape, dt: esA.enter_context(nc.sbuf_tensor(un(name), shape, dt))
                    xt = sbA("xt_a", [128, KC, 512], F32)
                    sq = sbA("sq_a", [128, KC, 512], BF16)
                    rstd = sbA("rstd_a", [128, 512], F32)
                    for tt in range(4):
                        t0 = base + tt * 512
                        dma("sp", xt[:], x_view(src, t0, 512), reads=[f"xr{t0 // 256}", f"xr{t0 // 256 + 1}"], writes=["xt_a"])
                        rms_rstd(xt[:], 512, sq[:], rstd[:], ps[0][:, :], "xt_a", "a")
                        for kc in range(KC):
                            T.op("dve", lambda: V.scalar_tensor_tensor(out=hT[:, kc, tt * 512:(tt + 1) * 512], in0=xt[:, kc, :],
                                                                       scalar=pv[:, PV_NM + l * 8 + kc:PV_NM + l * 8 + kc + 1], in1=rstd[:],
                                                                       op0=ALU.mult, op1=ALU.mult),
                                 reads=["xt_a", "pv", "rstda"], writes=[f"hT{tt}"])
                    T.drain("sp")
                    nc.all_engine_barrier()
                chk(f"A{l}{st}")
                hT_all = [f"hT{tt}" for tt in range(4)]
                with ExitStack() as esB:
                    sbB = lambda name, shape, dt: esB.enter_context(nc.sbuf_tensor(un(name), shape, dt))
                    wq = [sbB(f"wq{i}", [128, KC, 128], BF16) for i in range(2)]
                    wk = [sbB(f"wk{i}", [128, KC, 128], BF16) for i in range(2)]
                    wv = [sbB(f"wv{i}", [128, KC, 128], BF16) for i in range(2)]
                    Qs = [sbB(f"Qs{i}", [128, 2, ST], BF16) for i in range(2)]
                    Ks = [sbB(f"Ks{i}", [128, 16 * 128 + ST], BF16) for i in range(2)]
                    Vx = [sbB(f"Vx{i}", [128, 32, VW], BF16) for i in range(2)]
                    pt = [sbB("pt0", [128, 17, 256], BF16), sbB("pt1", [128, 8, 256], BF16)]
                    accn = sbB("accn", [128, ST], F32)
                    accd = sbB("accd", [1, 2, ST], F32)
                    kst = [sbB("kst0", [128, 512], F32)] * 2
                    vst = [sbB("vst0", [128, 512], F32)] * 2
                    sst = [sbB(f"sst{i}", [NS, 384], F32) for i in range(2)]
                    for i in range(2):
                        T.op("pool", lambda: G.memset(Qs[i][:], 0.0), writes=[f"Qs{i}"])
                    for hp in range(8):
                        for g in range(3):
                            d = DILS[g]
                            W = WINS[g]
                            S = ST // d
                            nb = S // 128
                            par = it % 2
                            it += 1
                            slot = g * 8 + hp
                            wts = (wq[par], wk[par], wv[par])
                            wn = (f"wq{par}", f"wk{par}", f"wv{par}")
                            for s in range(3):
                                c0 = (g * 3 + s) * D + hp * 128
                                dma("pool", wts[s][:], wqkv_in[a].rearrange("(kc p) n -> p kc n", p=128)[:, :, c0:c0 + 128], writes=[wn[s]])
                            Qn, Kn, Vn = f"Qs{par}", f"Ks{par}", f"Vx{par}"
                            Ksv = Ks[par][:, 0:d * (128 + S)].rearrange("p (r s) -> p r s", r=d)
                            Qsv = [Qs[par][:, e_, :].rearrange("p (r s) -> p r s", r=d) for e_ in range(2)]
                            NH = d
                            if st == 0:
                                T.op("pool", lambda: G.memset(Ksv[:, :, 0:128], 0.0), writes=[Kn])
                                T.op("pool", lambda: G.memset(Vx[par][:, 0:NH, :], 0.0), writes=[Vn])
                            else:
                                dma("sp", Ksv[:, :, 0:128], kh_scr[slot, :, 0:d * 128].rearrange("p (r s) -> p r s", r=d), reads=[f"kh{slot}"], writes=[Kn])
                                dma("sp", Vx[par][:, 0:NH, :], vh_scr[slot, :, 0:d * VW].rearrange("p (r s) -> p r s", r=d), reads=[f"vh{slot}"], writes=[Vn])
                            T.op("pool", lambda: G.memset(Vx[par][:, NH:NH + 16, 128:VW], 1.0), writes=[Vn])
                            chk(f"p1{l}{st}{hp}{g}")
                            if st == 0:
                                sp_ = slot % 2
                                for s in range(3):
                                    mm(ps[3][0:NS, s * 128:(s + 1) * 128], [(hsT[:, kc, :], wts[s][:, kc, :]) for kc in range(KC)], ["hsT", wn[s]], P(3))
                                T.op("act", lambda: A.copy(out=sst[sp_][:], in_=ps[3][0:NS, 0:384]), reads=P(3), writes=[f"sst{sp_}"])
                                for s in range(3):
                                    c0 = (g * 3 + s) * D + hp * 128
                                    dma("sp", qkvs_scr[:, c0:c0 + 128], sst[sp_][:, s * 128:(s + 1) * 128], reads=[f"sst{sp_}"], writes=["qkvs_scr"])
                                    dma("sp", qkvs_out[a, :, c0:c0 + 128], sst[sp_][:, s * 128:(s + 1) * 128], reads=[f"sst{sp_}"])
                            chk(f"p2{l}{st}{hp}{g}")
                            for tt in range(4):
                                t0 = base + tt * 512
                                sl = 512 // d
                                pq, pk = ps[0], ps[1]
                                mm(pq[:, :], [(wq[par][:, kc, :], hT[:, kc, tt * 512:(tt + 1) * 512]) for kc in range(KC)], [wn[0], f"hT{tt}"], P(0))
                                for e_ in range(2):
                                    T.op("act", lambda: A.copy(out=Qsv[e_][64 * e_:64 * e_ + 64, :, tt * sl:(tt + 1) * sl],
                                                               in_=pq[64 * e_:64 * e_ + 64, :].rearrange("p (s r) -> p r s", r=d)),
                                         reads=P(0), writes=[Qn])
                                mm(pk[:, :], [(wk[par][:, kc, :], hT[:, kc, tt * 512:(tt + 1) * 512]) for kc in range(KC)], [wn[1], f"hT{tt}"], P(1))
                                T.op("dve", lambda: V.tensor_copy(out=Ksv[:, :, 128 + tt * sl:128 + (tt + 1) * sl], in_=pk[:, :].rearrange("p (s r) -> p r s", r=d)),
                                     reads=P(1), writes=[Kn])
                                klo = max(t0, NT - W)
                                if klo < t0 + 512:
                                    kp_ = 0
                                    T.op("act", lambda: A.copy(out=kst[kp_][:], in_=pk[:, :]), reads=P(1), writes=[f"kst{kp_}"])
                                    dma("sp", kpT[g][a, hp * 128:(hp + 1) * 128, klo - (NT - W):t0 + 512 - (NT - W)], kst[kp_][:, klo - t0:512],
                                        reads=[f"kst{kp_}"])
                            chk(f"p3{l}{st}{hp}{g}")
                            blocks = [(r, j) for r in range(d) for j in range(nb)]
                            for b4 in range(4):
                                pvb = ps[2 + (b4 % 2)]
                                pvn = P(2 + (b4 % 2))
                                for q in range(4):
                                    r, j = blocks[b4 * 4 + q]
                                    off = 128 * j * d + r
                                    mm(pvb[:, q * 128:(q + 1) * 128],
                                       [(hT[:, kc, off:off + 127 * d + 1:d], wv[par][:, kc, :]) for kc in range(KC)], [wn[2]] + hT_all, pvn)
                                T.op("act", lambda: A.copy(out=Vx[par][:, NH + b4 * 4:NH + b4 * 4 + 4, 0:128], in_=pvb[:, :].rearrange("p (q c) -> p q c", q=4)),
                                     reads=pvn, writes=[Vn])
                                keep = [q for q in range(4) if base + 128 * blocks[b4 * 4 + q][1] * d + blocks[b4 * 4 + q][0] >= NT - W]
                                if keep:
                                    vp_ = 0
                                    T.op("dve", lambda: V.tensor_copy(out=vst[vp_][:], in_=pvb[:, :]), reads=pvn, writes=[f"vst{vp_}"])
                                    for q in keep:
                                        r, j = blocks[b4 * 4 + q]
                                        row0 = base + 128 * j * d + r - (NT - W)
                                        dst = vp[g][a, row0:row0 + 127 * d + 1:d, hp * 128:(hp + 1) * 128]
                                        dma("sp", dst, vst[vp_][:, q * 128:(q + 1) * 128], reads=[f"vst{vp_}"])
                            chk(f"p4{l}{st}{hp}{g}")
                            if st + 1 < NST:
                                if True:
                                    dma("sp", kh_scr[slot, :, 0:d * 128].rearrange("p (r s) -> p r s", r=d), Ksv[:, :, S:S + 128], reads=[Kn], writes=[f"kh{slot}"])
                                if True:
                                  dma("sp", vh_scr[slot, :, 0:d * VW].rearrange("p (r s) -> p r s", r=d),
                                    Vx[par][:, NH:NH + 16, :].rearrange("p (r j) c -> p r j c", j=nb)[:, :, nb - 1, :], reads=[Vn], writes=[f"vh{slot}"])
                            chk(f"proj{l}{st}{hp}{g}")
                            for e in range(2):
                                h = 2 * hp + e
                                pe0 = 64 * e
                                if g == 0:
                                    bundles = [[(0, qb) for qb in range(4 * m, 4 * m + 4)] for m in range(4)]
                                elif g == 1:
                                    bundles = [[(r, qb) for qb in range(4)] for r in range(4)]
                                else:
                                    bundles = [[(r, 0) for r in range(4 * m, 4 * m + 4)] for m in range(4)]
                                done_streams = {}
                                sslot = 0
                                prev_b = None
                                for bi, bun in enumerate(bundles):
                                    for (r, qb) in bun:
                                        if r in done_streams:
                                            continue
                                        if g == 0:
                                            ptp, sbase = 0, 0
                                        elif g == 1:
                                            ptp, sbase = (r % 2), 0
                                            if ptp == 0:
                                                sbase = 0
                                        else:
                                            ptp, sbase = bi % 2, (r % 4) * 2
                                        if g == 1:
                                            pass
                                        done_streams[r] = (ptp, sbase)
                                        for kb in range(-1, nb):
                                            q_lo = max(kb, 0)
                                            q_hi = min(kb + 2, nb)
                                            N = (q_hi - q_lo) * 128
                                            bc0 = 128 if kb == -1 else 0
                                            ss = sslot % 4
                                            sslot += 1
                                            pso = psY[:, ss * 512:ss * 512 + N]
                                            pname = PY(ss * 512, ss * 512 + N)

                                            def sc_fn():
                                                PE.matmul(pso, lhsT=Ksv[:, r, 128 * (kb + 1):128 * (kb + 2)],
                                                          rhs=Qsv[e][:, r, 128 * q_lo:128 * q_hi], start=True, stop=False)
                                                return PE.matmul(pso, lhsT=identb[:], rhs=BT[:, g * 16 + h, bc0:bc0 + N], start=False, stop=True)
                                            T.op("pe", sc_fn, reads=[Kn, Qn, "identb", "BT"], writes=pname)
                                            T.op("act", lambda: A.activation(out=pt[ptp][:, sbase + kb + 1, 0:N], in_=pso, func=AF.Exp, scale=0.125),
                                                 reads=pname, writes=[f"pt{ptp}"])
                                    def pv_acc(bi, bun):
                                        pnum = ps[0 + 2 * (bi % 2)]
                                        pden = ps[1 + 2 * (bi % 2)]
                                        PN = P(0 + 2 * (bi % 2))
                                        PD = P(1 + 2 * (bi % 2))
                                        ptnames = set()
                                        for q, (r, qb) in enumerate(bun):
                                            ptp, sbase = done_streams[r]
                                            ptnames.add(f"pt{ptp}")
                                            blk_prev = r if qb == 0 else NH + r * nb + qb - 1
                                            blk_cur = NH + r * nb + qb
                                            c_prev = 0 if qb == 0 else 128
                                            rhs_prev = pt[ptp][:, sbase + qb, c_prev:c_prev + 128]
                                            rhs_cur = pt[ptp][:, sbase + qb + 1, 0:128]

                                            def pv_fn():
                                                PE.matmul(pnum[:, q * 128:(q + 1) * 128], lhsT=Vx[par][:, blk_prev, 0:128], rhs=rhs_prev, start=True, stop=False)
                                                PE.matmul(pnum[:, q * 128:(q + 1) * 128], lhsT=Vx[par][:, blk_cur, 0:128], rhs=rhs_cur, start=False, stop=True)
                                                PE.matmul(pden[0:1, q * 128:(q + 1) * 128], lhsT=Vx[par][:, blk_prev, 128:129], rhs=rhs_prev, start=True, stop=False)
                                                return PE.matmul(pden[0:1, q * 128:(q + 1) * 128], lhsT=Vx[par][:, blk_cur, 128:129], rhs=rhs_cur, start=False, stop=True)
                                            T.op("pe", pv_fn, reads=[Vn, f"pt{ptp}"], writes=PN + PD)
                                        if g == 0:
                                            an = accn[pe0:pe0 + 64, bi * 512:(bi + 1) * 512]
                                            ad = accd[0:1, e, bi * 512:(bi + 1) * 512]
                                            pn = pnum[pe0:pe0 + 64, :]
                                            pd = pden[0:1, :]
                                        elif g == 1:
                                            r = bun[0][0]
                                            an = accn[pe0:pe0 + 64, :].rearrange("p (qb i r) -> p r qb i", r=4, i=128)[:, r]
                                            ad = accd[0:1, e, :].rearrange("p (qb i r) -> p r qb i", r=4, i=128)[:, r]
                                            pn = pnum[pe0:pe0 + 64, :].rearrange("p (qb i) -> p qb i", i=128)
                                            pd = pden[0:1, :].rearrange("p (qb i) -> p qb i", i=128)
                                        else:
                                            r0 = bun[0][0]
                                            an = accn[pe0:pe0 + 64, :].rearrange("p (i r) -> p r i", r=16)[:, r0:r0 + 4]
                                            ad = accd[0:1, e, :].rearrange("p (i r) -> p r i", r=16)[:, r0:r0 + 4]
                                            pn = pnum[pe0:pe0 + 64, :].rearrange("p (q i) -> p q i", i=128)
                                            pd = pden[0:1, :].rearrange("p (q i) -> p q i", i=128)
                                        if g == 0:
                                            T.op("act", lambda: A.copy(out=an, in_=pn), reads=PN, writes=["accn"])
                                            T.op("dve", lambda: V.tensor_copy(out=ad, in_=pd), reads=PD, writes=["accd"])
                                        else:
                                            T.op("dve", lambda: V.tensor_tensor(out=an, in0=an, in1=pn, op=ALU.add), reads=PN + ["accn"], writes=["accn"])
                                            T.op("dve", lambda: V.tensor_tensor(out=ad, in0=ad, in1=pd, op=ALU.add), reads=PD + ["accd"], writes=["accd"])
                                    if prev_b is not None:
                                        pv_acc(*prev_b)
                                    prev_b = (bi, bun, pv_acc)[0:2]
                                if prev_b is not None:
                                    pv_acc(*prev_b)
                            chk(f"att{l}{st}{hp}{g}")
                        T.op("dve", lambda: V.reciprocal(out=accd[:], in_=accd[:]), reads=["accd"], writes=["accd"])
                        for e in range(2):
                            pe0 = 64 * e
                            for tt in range(4):
                                bk = ps[2 + tt % 2]
                                bn = P(2 + tt % 2)
                                mm(bk[:, :], [(ones32[0:1, :], accd[0:1, e, tt * 512:(tt + 1) * 512])], ["ones32", "accd"], bn)
                                T.op("dve", lambda: V.tensor_tensor(out=attnT[pe0:pe0 + 64, hp, tt * 512:(tt + 1) * 512], in0=accn[pe0:pe0 + 64, tt * 512:(tt + 1) * 512],
                                                                    in1=bk[pe0:pe0 + 64, :], op=ALU.mult), reads=bn + ["accn"], writes=["attnT"])
                    T.drain("sp")
                    nc.all_engine_barrier()
                chk(f"B{l}{st}")
                with ExitStack() as esC:
                    sbC = lambda name, shape, dt: esC.enter_context(nc.sbuf_tensor(un(name), shape, dt))
                    xt = sbC("xt_c", [128, KC, 512], F32)
                    wo = sbC("wo", [128, KC, D], BF16)
                    for c in range(KC):
                        dma("pool", wo[:, c, :], wo_in[a, c * 128:(c + 1) * 128, :], writes=["wo"])
                    for tt in range(4):
                        t0 = base + tt * 512
                        xr = [f"xr{t0 // 256}", f"xr{t0 // 256 + 1}"]
                        dma("sp", xt[:], x_view(src, t0, 512), reads=xr, writes=["xt_c"])
                        for ec in range(KC):
                            bk = ps[ec % 2]
                            bn = P(ec % 2)
                            mm(bk[:, :], [(wo[:, c, ec * 128:(ec + 1) * 128], attnT[:, c, tt * 512:(tt + 1) * 512]) for c in range(KC)], ["wo", "attnT"], bn)
                            T.op("dve", lambda: V.tensor_tensor(out=xt[:, ec, :], in0=xt[:, ec, :], in1=bk[:, :], op=ALU.add), reads=bn + ["xt_c"], writes=["xt_c"])
                        dma("sp", x_view(xres, t0, 512), xt[:], reads=["xt_c"], writes=xr)
                    T.drain("sp")
                    nc.all_engine_barrier()
        T.drain("sp")
        nc.all_engine_barrier()

    def sample_attention(l):
        a = l // 2
        with ExitStack() as es:
            sb = lambda name, shape, dt: es.enter_context(nc.sbuf_tensor(un(name), shape, dt))
            qs = sb("qs", [NS, 9 * D], F32)
            prod0 = sb("prod0", [NS, D], F32)
            sc0 = sb("sc0", [NS, 16], F32)
            pv0 = [sb(f"pv0_{g}", [NS, D + 16], F32) for g in range(3)]
            Kc = [sb(f"Kc{i}", [128, D], F32) for i in range(2)]
            Vc = [sb(f"Vc{i}", [128, D], F32) for i in range(2)]
            prod = sb("prod", [128, D], F32)
            sc = sb("sc", [128, 16], F32)
            pvc = [sb(f"pvc{i}", [128, D + 16], F32) for i in range(2)]
            rec = sb("rec", [1, 16], F32)
            arow = sb("arow", [1, D], F32)
            asT = sb("asT", [128, KC, NS], BF16)
            wo = sb("wo_s", [128, KC, D], BF16)
            for c in range(KC):
                dma("pool", wo[:, c, :], wo_in[a, c * 128:(c + 1) * 128, :], writes=["wo_s"])
            dma("sp", qs[:], qkvs_scr[:, :], reads=["qkvs_scr"], writes=["qs"])
            for g in range(3):
                qg = qs[:, (g * 3) * D:(g * 3 + 1) * D]
                kg = qs[:, (g * 3 + 1) * D:(g * 3 + 2) * D]
                vg = qs[:, (g * 3 + 2) * D:(g * 3 + 3) * D]
                T.op("dve", lambda: V.tensor_tensor(out=prod0[:], in0=qg, in1=kg, op=ALU.mult), reads=["qs"], writes=["prod0"])
                T.op("dve", lambda: V.tensor_reduce(out=sc0[:], in_=prod0[:].rearrange("p (h d) -> p h d", d=64), axis=AX.X, op=ALU.add),
                     reads=["prod0"], writes=["sc0"])
                T.op("dve", lambda: V.scalar_tensor_tensor(out=sc0[:], in0=sc0[:], scalar=0.125, in1=B0[:, g * 16:(g + 1) * 16], op0=ALU.mult, op1=ALU.add),
                     reads=["sc0", "B0"], writes=["sc0"])
                T.op("act", lambda: A.activation(out=pv0[g][:, D:D + 16], in_=sc0[:], func=AF.Exp), reads=["sc0"], writes=[f"pv0_{g}p"])
                T.op("dve", lambda: V.tensor_tensor(out=pv0[g][:, 0:D].rearrange("p (h d) -> p h d", d=64), in0=vg.rearrange("p (h d) -> p h d", d=64),
                                                    in1=pv0[g][:, D:D + 16].unsqueeze(2).broadcast_to([NS, 16, 64]), op=ALU.mult),
                     reads=["qs", f"pv0_{g}p"], writes=[f"pv0_{g}"])
            ci = 0
            for b in range(NS):
                pieces = [(0, 512), (512, 1024), (1024, 1040)]
                pouts = [ps[2][0:1, 0:512], ps[3][0:1, 0:512], psY[0:1, 0:16]]
                pnames = [P(2), P(3), PY(0, 16)]
                for g in range(3):
                    par = ci % 2
                    ci += 1
                    dma("sp", Kc[par][:], ck[g][a, b], writes=[f"Kc{par}"])
                    dma("sp", Vc[par][:], cv[g][a, b], writes=[f"Vc{par}"])
                    qg = qs[:, (g * 3) * D:(g * 3 + 1) * D]
                    mm(ps[0][:, :], [(sel4[:, b * 128:(b + 1) * 128], qg[:, 0:512])], ["sel4", "qs"], P(0))
                    mm(ps[1][:, :], [(sel4[:, b * 128:(b + 1) * 128], qg[:, 512:1024])], ["sel4", "qs"], P(1))
                    T.op("dve", lambda: V.tensor_tensor(out=prod[:, 0:512], in0=Kc[par][:, 0:512], in1=ps[0][:, :], op=ALU.mult), reads=[f"Kc{par}"] + P(0), writes=["prodA"])
                    T.op("dve", lambda: V.tensor_tensor(out=prod[:, 512:1024], in0=Kc[par][:, 512:1024], in1=ps[1][:, :], op=ALU.mult), reads=[f"Kc{par}"] + P(1), writes=["prodB"])
                    T.op("dve", lambda: V.tensor_reduce(out=sc[:], in_=prod[:].rearrange("p (h d) -> p h d", d=64), axis=AX.X, op=ALU.add),
                         reads=["prodA", "prodB"], writes=["sc"])
                    T.op("dve", lambda: V.scalar_tensor_tensor(out=sc[:], in0=sc[:], scalar=0.125, in1=SB[:, g * 16:(g + 1) * 16], op0=ALU.mult, op1=ALU.add),
                         reads=["sc", "SB"], writes=["sc"])
                    T.op("act", lambda: A.activation(out=pvc[par][:, D:D + 16], in_=sc[:], func=AF.Exp), reads=["sc"], writes=[f"pvc{par}p"])
                    T.op("dve", lambda: V.tensor_tensor(out=pvc[par][:, 0:D].rearrange("p (h d) -> p h d", d=64), in0=Vc[par][:].rearrange("p (h d) -> p h d", d=64),
                                                        in1=pvc[par][:, D:D + 16].unsqueeze(2).broadcast_to([128, 16, 64]), op=ALU.mult),
                         reads=[f"Vc{par}", f"pvc{par}p"], writes=[f"pvc{par}"])
                    for pi, (c0, c1) in enumerate(pieces):
                        def nd_fn():
                            PE.matmul(pouts[pi][:, 0:c1 - c0], lhsT=ones32[:, 0:1], rhs=pvc[par][:, c0:c1], start=(g == 0), stop=False)
                            return PE.matmul(pouts[pi][:, 0:c1 - c0], lhsT=sel4[:, b * 128:b * 128 + 1], rhs=pv0[g][:, c0:c1], start=False, stop=(g == 2))
                        T.op("pe", nd_fn, reads=["ones32", "sel4", f"pvc{par}", f"pvc{par}p", f"pv0_{g}", f"pv0_{g}p"], writes=pnames[pi])
                T.op("dve", lambda: V.reciprocal(out=rec[:], in_=psY[0:1, 0:16]), reads=PY(0, 16), writes=["rec"])
                for pi in range(2):
                    T.op("dve", lambda: V.tensor_tensor(out=arow[:, pi * 512:(pi + 1) * 512].rearrange("p (h d) -> p h d", d=64),
                                                        in0=pouts[pi].rearrange("p (h d) -> p h d", d=64),
                                                        in1=rec[:, pi * 8:(pi + 1) * 8].unsqueeze(2).broadcast_to([1, 8, 64]), op=ALU.mult),
                         reads=pnames[pi] + ["rec"], writes=["arow%d" % pi])
                for c in range(KC):
                    mm(psY[:, 512 + c:512 + c + 1], [(arow[0:1, c * 128:(c + 1) * 128], ones32[0:1, 0:1])], ["arow0", "arow1", "ones32"], PY(512, 520))
                T.op("dve", lambda: V.tensor_copy(out=asT[:, :, b], in_=psY[:, 512:512 + KC]), reads=PY(512, 520), writes=["asT"])
            for ec in range(KC):
                mm(psY[:, 1024 + ec * NS:1024 + (ec + 1) * NS], [(wo[:, c, ec * 128:(ec + 1) * 128], asT[:, c, :]) for c in range(KC)], ["wo_s", "asT"], PY(1024, 1056))
            T.op("dve", lambda: V.tensor_tensor(out=xs[:], in0=xs[:], in1=psY[:, 1024:1024 + KC * NS].rearrange("p (c n) -> p c n", n=NS), op=ALU.add),
                 reads=PY(1024, 1056) + ["xs"], writes=["xs"])
        T.drain("sp")
        nc.all_engine_barrier()

    def tile_pass(l):
        odd = (l % 2 == 1)
        pb = l // 2
        last = (l == 3)
        with ExitStack() as es:
            sb = lambda name, shape, dt: es.enter_context(nc.sbuf_tensor(un(name), shape, dt))
            win = sb("win", [128, KC, 2 * DFF], BF16)
            wout = sb("wout", [128, FC, D], BF16)
            xt = [sb(f"xt{i}", [128, KC, FT], F32) for i in range(2)]
            sq = sb("sq", [128, KC, FT], BF16)
            rstd = sb("rstd", [128, FT], F32)
            hb = sb("hb", [128, KC, FT], BF16)
            gext = [sb(f"gext{i}", [128, FT + 2], F32) for i in range(4)]
            cvt = [sb(f"cvt{i}", [128, FT], F32) for i in range(4)]
            actc = [sb(f"actc{i}", [128, FT], BF16) for i in range(4)]
            ghalo = sb("ghalo", [128, FC, 2], F32)
            pre = sb("pre", [128, 2, FC, NS], F32)
            gs = sb("gs", [128, FC, NS], F32)
            if odd:
                wpl = sb("wpl", [128, 2, 4, 256], BF16)
                hx = sb("hx", [128, KC, 16 + FT], F32)
                hxs = sb("hxs", [128, KC, NS, 16], F32)
                hso = sb("hso", [128, KC, NS], F32)
                s1 = sb("s1", [128, 16 + FT], F32)
                s2 = sb("s2", [128, 16 + FT], F32)
                zb = sb("zb", [128, KC, FT], BF16)
                pcorr = sb("pcorr", [128, 4, 16], F32)
                dma("sp", pcorr[:], pcorr_in.rearrange("p (g t) -> p g t", g=4), writes=["pcorr"])
                for g in range(4):
                    dma("pool", wpl[:, :, g, :], wpool_in[pb, g].rearrange("(kc p) n -> p kc n", p=128), writes=["wpl"])
                dma("sp", hxs[:, :, :, 0:15], spoolT_in[pb], writes=["hxs"])
                dma("sp", pools14[pb], spool_in[pb, :, 1:15, :], reads=[])
            for kc in range(KC):
                dma("pool", win[:, kc, :], win_in[l, kc * 128:(kc + 1) * 128, :], writes=["win"])
            for f in range(FC):
                dma("pool", wout[:, f, :], wout_in[l, f * 128:(f + 1) * 128, :], writes=["wout"])
            dma("sp", pre[:], sconvT_in[l], writes=["pre"])
            dma("sp", convs0[l], sconv_in[l, :, 1, :])
            T.op("pool", lambda: G.memset(ghalo[:], 0.0), writes=["ghalo"])
            if odd:
                T.op("pool", lambda: G.memset(hx[:, :, 0:16], 0.0), writes=["hx"])

            ntile = NT // FT
            fi = 0
            for ti in range(ntile + 1):
                sample = (ti == ntile)
                n = NS if sample else FT
                t0 = ti * FT
                if sample:
                    x = xs
                    xn = "xs"
                    xa = xs[:]
                else:
                    par = ti % 2
                    x = xt[par]
                    xn = f"xt{par}"
                    xa = x[:]
                    dma("sp", xa, x_view(xres, t0, FT), reads=[f"xr{ti}"], writes=[xn])
                if odd:
                    rms_rstd(xa, n, sq[:, :, 0:n], rstd[:, 0:n], ps[0][:, 0:n], xn, "t")
                    if not sample:
                        hv = lambda c, lo, hi: hx[:, c, lo:hi]
                        for c in range(KC):
                            T.op("dve", lambda: V.scalar_tensor_tensor(out=hx[:, c, 16:16 + n], in0=x[:, c, :], scalar=pv[:, PV_NM + l * 8 + c:PV_NM + l * 8 + c + 1],
                                                                       in1=rstd[:, 0:n], op0=ALU.mult, op1=ALU.mult), reads=[xn, "pv", "rstdt"], writes=["hx"])
                        if ti == ntile - 1:
                            dma("sp", poolpT[pb].rearrange("(kc p) t -> p kc t", p=128), hx[:, :, 16 + FT - 15:16 + FT], reads=["hx"], nonc=True)
                    else:
                        for c in range(KC):
                            T.op("dve", lambda: V.scalar_tensor_tensor(out=hxs[:, c, :, 15], in0=x[:, c, :], scalar=pv[:, PV_NM + l * 8 + c:PV_NM + l * 8 + c + 1],
                                                                       in1=rstd[:, 0:n], op0=ALU.mult, op1=ALU.mult), reads=[xn, "pv", "rstdt"], writes=["hxs"])
                        T.op("pool", lambda: G.tensor_copy(out=hso[:], in_=hxs[:, :, :, 15]), reads=["hxs"], writes=["hso"])
                        dma("sp", poolsT[pb].rearrange("(kc p) n -> p kc n", p=128), hso[:], reads=["hso"], nonc=True)
                    for c in range(KC):
                        grp = c // 2
                        w = (2, 4, 8, 16)[grp]
                        steps = grp + 1
                        if not sample:
                            L = 16 + FT
                            cur = lambda lo, hi: hx[:, c, lo:hi]
                            bufs = [lambda lo, hi: s1[:, lo:hi], lambda lo, hi: s2[:, lo:hi]]
                            fin = lambda vw: vw(16, L)
                            hcur = hx[:, c, 16:L]
                            zo = zb[:, c, 0:n]
                        else:
                            L = 16
                            cur = lambda lo, hi: hxs[:, c, :, lo:hi]
                            s1v = s1[:, 0:NS * 16].rearrange("p (b t) -> p b t", t=16)
                            s2v = s2[:, 0:NS * 16].rearrange("p (b t) -> p b t", t=16)
                            bufs = [lambda lo, hi: s1v[:, :, lo:hi], lambda lo, hi: s2v[:, :, lo:hi]]
                            fin = lambda vw: vw(15, 16)
                            hcur = hxs[:, c, :, 15:16]
                            zo = zb[:, c, 0:n].unsqueeze(2)
                        srcv = cur
                        srcn = "hxs" if sample else "hx"
                        sh = 1
                        for k in range(steps):
                            dst = bufs[k % 2]
                            dn = "s1" if k % 2 == 0 else "s2"
                            lo = 2 * sh - 1
                            sv, sn_ = srcv, srcn
                            T.op("dve", lambda: V.tensor_tensor(out=dst(lo, L), in0=sv(lo, L), in1=sv(lo - sh, L - sh), op=ALU.add), reads=[sn_], writes=[dn])
                            srcv, srcn = dst, dn
                            sh *= 2
                        fv = fin(srcv)
                        if (not sample) and ti == 0:
                            sv = srcv
                            T.op("dve", lambda: V.tensor_tensor(out=sv(16, 32), in0=sv(16, 32), in1=pcorr[:, grp, :], op=ALU.mult), reads=[srcn, "pcorr"], writes=[srcn])
                        T.op("dve", lambda: V.scalar_tensor_tensor(out=zo, in0=fv, scalar=1.0 / w, in1=hcur, op0=ALU.mult, op1=ALU.subtract),
                             reads=[srcn, "hxs" if sample else "hx"], writes=["zb"])
                    for grp in range(4):
                        for eh in range(2):
                            ec = grp * 2 + eh
                            bk = ps[1 + ec % 2]
                            bn = P(1 + ec % 2, 0, n)
                            mm(bk[:, 0:n], [(wpl[:, kc2, grp, eh * 128:(eh + 1) * 128], zb[:, grp * 2 + kc2, 0:n]) for kc2 in range(2)], ["wpl", "zb"], bn)
                            T.op("dve", lambda: V.scalar_tensor_tensor(out=x[:, ec, :], in0=bk[:, 0:n], scalar=pv[:, PV_PSC + pb * 8 + ec:PV_PSC + pb * 8 + ec + 1],
                                                                       in1=x[:, ec, :], op0=ALU.mult, op1=ALU.add), reads=bn + ["pv", xn], writes=[xn])
                    if not sample:
                        T.op("pool", lambda: G.tensor_copy(out=hx[:, :, 0:16], in_=hx[:, :, FT:FT + 16]), reads=["hx"], writes=["hx"])
                rms_rstd(xa, n, sq[:, :, 0:n], rstd[:, 0:n], ps[0][:, 0:n], xn, "t")
                for kc in range(KC):
                    T.op("dve", lambda: V.scalar_tensor_tensor(out=hb[:, kc, 0:n], in0=x[:, kc, :], scalar=pv[:, PV_NF + l * 8 + kc:PV_NF + l * 8 + kc + 1],
                                                               in1=rstd[:, 0:n], op0=ALU.mult, op1=ALU.mult), reads=[xn, "pv", "rstdt"], writes=["hb"])
                def emit_y(f, p2, an):
                    def y_fn():
                        ins = None
                        for ec in range(KC):
                            ins = PE.matmul(psY[:, ec * 256:ec * 256 + n], lhsT=wout[:, f, ec * 128:(ec + 1) * 128], rhs=actc[p2][:, 0:n], start=(f == 0 and ec % 2 == 0), stop=(f == FC - 1))
                        return ins
                    T.op("pe", y_fn, reads=["wout", an], writes=PY())
                pend = []
                for f in range(FC):
                    p2 = fi % 4
                    fi += 1
                    bk = ps[p2]
                    bng = P(p2)
                    bnv = P(p2)

                    def gv_fn():
                        for kc in range(KC):
                            PE.matmul(bk[:, 0:n], lhsT=win[:, kc, f * 128:(f + 1) * 128], rhs=hb[:, kc, 0:n], start=(kc == 0), stop=(kc == KC - 1))
                        ins = None
                        for kc in range(KC):
                            ins = PE.matmul(bk[:, 256:256 + n], lhsT=win[:, kc, DFF + f * 128:DFF + (f + 1) * 128], rhs=hb[:, kc, 0:n], start=(kc == 0), stop=(kc == KC - 1))
                        return ins
                    T.op("pe", gv_fn, reads=["win", "hb"], writes=bng)
                    cw = lambda j: pv[:, PV_CW + (l * 3 + j) * FC + f:PV_CW + (l * 3 + j) * FC + f + 1]
                    cb = pv[:, PV_CB + l * FC + f:PV_CB + l * FC + f + 1]
                    cvn = f"cvt{p2}"
                    T.op("act", lambda: A.activation(out=cvt[p2][:, 0:n], in_=bk[:, 0:n], func=AF.Identity, bias=cb, scale=cw(2)), reads=bng + ["pv"], writes=[cvn])
                    if not sample:
                        gn = f"gext{p2}"
                        T.op("pool", lambda: G.tensor_copy(out=gext[p2][:, 0:2], in_=ghalo[:, f, :]), reads=["ghalo"], writes=[gn + "h"])
                        T.op("act", lambda: A.copy(out=gext[p2][:, 2:2 + n], in_=bk[:, 0:n]), reads=bng, writes=[gn])
                        T.op("pool", lambda: G.tensor_copy(out=ghalo[:, f, :], in_=gext[p2][:, n:n + 2]), reads=[gn], writes=["ghalo"])
                        T.op("dve", lambda: V.scalar_tensor_tensor(out=cvt[p2][:, 0:n], in0=gext[p2][:, 1:1 + n], scalar=cw(1), in1=cvt[p2][:, 0:n], op0=ALU.mult, op1=ALU.add),
                             reads=[gn, gn + "h", "pv", cvn], writes=[cvn])
                        T.op("dve", lambda: V.scalar_tensor_tensor(out=cvt[p2][:, 0:n], in0=gext[p2][:, 0:n], scalar=cw(0), in1=cvt[p2][:, 0:n], op0=ALU.mult, op1=ALU.add),
                             reads=[gn, gn + "h", "pv", cvn], writes=[cvn])
                    else:
                        T.op("act", lambda: A.copy(out=gs[:, f, :], in_=bk[:, 0:n]), reads=bng, writes=["gs"])
                        T.op("dve", lambda: V.scalar_tensor_tensor(out=cvt[p2][:, 0:n], in0=pre[:, 1, f, :], scalar=cw(1), in1=cvt[p2][:, 0:n], op0=ALU.mult, op1=ALU.add),
                             reads=["pre", "pv", cvn], writes=[cvn])
                        T.op("dve", lambda: V.scalar_tensor_tensor(out=cvt[p2][:, 0:n], in0=pre[:, 0, f, :], scalar=cw(0), in1=cvt[p2][:, 0:n], op0=ALU.mult, op1=ALU.add),
                             reads=["pre", "pv", cvn], writes=[cvn])
                    T.op("act", lambda: A.activation(out=cvt[p2][:, 0:n], in_=cvt[p2][:, 0:n], func=AF.Silu), reads=[cvn], writes=[cvn])
                    an = f"actc{p2}"
                    T.op("dve", lambda: V.tensor_tensor(out=actc[p2][:, 0:n], in0=cvt[p2][:, 0:n], in1=bk[:, 256:256 + n], op=ALU.mult), reads=[cvn] + bnv, writes=[an])

                    pend.append((f, p2, an))
                    if len(pend) > 2:
                        emit_y(*pend.pop(0))
                while pend:
                    emit_y(*pend.pop(0))
                yv = psY[:, :].rearrange("p (c t) -> p c t", t=256)[:, :, 0:n]
                T.op("dve", lambda: V.tensor_tensor(out=xa, in0=xa, in1=yv, op=ALU.add), reads=PY() + [xn], writes=[xn])
                if not last:
                    if not sample:
                        dma("sp", x_view(xres, t0, FT), xa, reads=[xn], writes=[f"xr{ti}"])
                else:
                    rms_rstd(xa, n, sq[:, :, 0:n], rstd[:, 0:n], ps[0][:, 0:n], xn, "t")
                    for kc in range(KC):
                        T.op("dve", lambda: V.scalar_tensor_tensor(out=x[:, kc, :], in0=x[:, kc, :], scalar=pv[:, PV_FIN + kc:PV_FIN + kc + 1],
                                                                   in1=rstd[:, 0:n], op0=ALU.mult, op1=ALU.mult), reads=[xn, "pv", "rstdt"], writes=[xn])
                    if not sample:
                        dma("sp", x_view(yT, t0, FT), xa, reads=[xn])
                    else:
                        dma("sp", ysT.rearrange("(kc p) n -> p kc n", p=128), xa, reads=[xn], nonc=True)
                if (not sample) and ti == ntile - 1:
                    dma("sp", convpT[l].rearrange("(f p) j -> p f j", p=128), ghalo[:], reads=["ghalo"], nonc=True)
                if sample:
                    dma("sp", convsT[l].rearrange("(f p) n -> p f n", p=128), gs[:], reads=["gs"], nonc=True)
        T.drain("sp")
        nc.all_engine_barrier()

    try:
        for l in range(4):
            if l % 2 == 0:
                attention_pass(l)
                chk(f"attn{l}")
                sample_attention(l)
                chk(f"sattn{l}")
            tile_pass(l)
            chk(f"tile{l}")
    except _Stop:
        pass
    T.drain("sp")
    T.drain("act")
    return nc


_NC = None


def kernel(**inp):
    global _NC
    f32 = np.float32
    g = lambda k: np.asarray(inp[k], dtype=f32)
    x_prompt, x_sample = g("x_prompt"), g("x_sample")
    caches_k = [g("cache_k_w128"), g("cache_k_w512"), g("cache_k_w2048")]
    caches_v = [g("cache_v_w128"), g("cache_v_w512"), g("cache_v_w2048")]
    state_pool, state_conv = g("state_pool"), g("state_conv")
    norm_mix, norm_ffn, norm_final = g("norm_mix"), g("norm_ffn"), g("norm_final")
    pool_scale, conv_w, conv_b = g("pool_scale"), g("conv_w"), g("conv_b")

    fm = lambda v: np.ascontiguousarray(v.reshape(-1, 128).T)
    pvec = np.zeros((128, NPV), f32)
    for l in range(4):
        pvec[:, PV_NM + l * 8:PV_NM + (l + 1) * 8] = fm(norm_mix[l])
        pvec[:, PV_NF + l * 8:PV_NF + (l + 1) * 8] = fm(norm_ffn[l])
        pvec[:, PV_CB + l * FC:PV_CB + (l + 1) * FC] = fm(conv_b[l])
        for j in range(3):
            pvec[:, PV_CW + (l * 3 + j) * FC:PV_CW + (l * 3 + j + 1) * FC] = fm(conv_w[l, j])
    pvec[:, PV_FIN:PV_FIN + 8] = fm(norm_final)
    for b in range(2):
        pvec[:, PV_PSC + b * 8:PV_PSC + (b + 1) * 8] = fm(pool_scale[b])
    consts = host_consts()
    shared = dict(rel_bias=g("rel_bias"), pvec=pvec, w_qkv=g("w_qkv"), w_o=g("w_o"), w_pool=g("w_pool"),
                  w_in=g("w_in"), w_out=g("w_out"), **consts)
    in_maps = []
    for c in range(8):
        b = c % 4
        sl = slice(NS * c, NS * c + NS)
        m = dict(shared)
        m["xT"] = np.ascontiguousarray(x_prompt[b].T)
        m["xsT"] = np.ascontiguousarray(x_sample[sl, 0, :].T)
        for gi in range(3):
            dil = DILS[gi]
            m[f"ck{gi}"] = np.ascontiguousarray(caches_k[gi][:, sl, ::dil]).reshape(2, NS, 128, D)
            m[f"cv{gi}"] = np.ascontiguousarray(caches_v[gi][:, sl, ::dil]).reshape(2, NS, 128, D)
        sp = state_pool[:, sl]
        m["spool"] = np.ascontiguousarray(sp)
        m["spoolT"] = np.ascontiguousarray(sp.reshape(2, NS, 15, KC, 128).transpose(0, 4, 3, 1, 2))
        scv = state_conv[:, sl]
        m["sconv"] = np.ascontiguousarray(scv)
        m["sconvT"] = np.ascontiguousarray(scv.reshape(4, NS, 2, FC, 128).transpose(0, 4, 2, 3, 1))
        in_maps.append(m)
    if _NC is None:
        _NC = build()
    res = run_bass_kernel_spmd(_NC, in_maps, core_ids=list(range(8)))
    R = res.results
    B = 4
    y_prompt = np.stack([R[b]["yT"].T for b in range(B)]).astype(f32)
    y_sample = np.concatenate([R[c]["ysT"].T for c in range(8)], 0).reshape(32, 1, D).astype(f32)
    outs = [y_prompt, y_sample]
    for gi in range(3):
        kp = np.stack([R[b][f"kp{gi}T"].transpose(0, 2, 1) for b in range(B)], 1)
        vpp = np.stack([R[b][f"vp{gi}"] for b in range(B)], 1)
        outs.append(np.ascontiguousarray(kp).reshape(2, B, WINS[gi], 16, 64).astype(f32))
        outs.append(np.ascontiguousarray(vpp).reshape(2, B, WINS[gi], 16, 64).astype(f32))
    qk = np.concatenate([R[c]["qkvs"] for c in range(8)], 1)
    for gi in range(3):
        outs.append(np.ascontiguousarray(qk[:, :, (gi * 3 + 1) * D:(gi * 3 + 2) * D]).reshape(2, 32, 1, 16, 64).astype(f32))
        outs.append(np.ascontiguousarray(qk[:, :, (gi * 3 + 2) * D:(gi * 3 + 3) * D]).reshape(2, 32, 1, 16, 64).astype(f32))
    outs.append(np.stack([R[b]["poolpT"].transpose(0, 2, 1) for b in range(B)], 1).astype(f32))
    ps_ = np.zeros((2, 32, 15, D), f32)
    for c in range(8):
        ps_[:, NS * c:NS * c + NS, 0:14] = R[c]["pools14"]
        ps_[:, NS * c:NS * c + NS, 14] = R[c]["poolsT"].transpose(0, 2, 1)
    outs.append(ps_)
    outs.append(np.stack([R[b]["convpT"].transpose(0, 2, 1) for b in range(B)], 1).astype(f32))
    cs_ = np.zeros((4, 32, 2, DFF), f32)
    for c in range(8):
        cs_[:, NS * c:NS * c + NS, 0] = R[c]["convs0"]
        cs_[:, NS * c:NS * c + NS, 1] = R[c]["convsT"].transpose(0, 2, 1)
    outs.append(cs_)
    return tuple(outs)
```

```python
from contextlib import ExitStack
import numpy as np
import concourse.bass as bass
import concourse.mybir as mybir
from concourse.bass_utils import run_bass_kernel_spmd

F32 = mybir.dt.float32
BF16 = mybir.dt.bfloat16
ALU = mybir.AluOpType
AF = mybir.ActivationFunctionType
AX = mybir.AxisListType

D = 1024
KC = 8
NT = 4096
ST = 2048
NST = NT // ST
NS = 4
DFF = 2816
FC = 22
FT = 256
VW = 144
WINS = (128, 512, 2048)
DILS = (1, 4, 16)
EPS = 1e-6
NEG = -30000.0
NPV = 440
PV_NM, PV_NF, PV_FIN, PV_PSC, PV_CW, PV_CB = 0, 32, 64, 72, 88, 352


DEBUG_STOP = None


class _Stop(Exception):
    pass


class Trk:
    def __init__(self, nc):
        self.nc = nc
        self.E = dict(pe=nc.tensor, act=nc.scalar, dve=nc.vector, pool=nc.gpsimd, sp=nc.sync)
        self.sem = {}
        self.cnt = {}
        for n in ["pe", "act", "dve", "pool", "q_sp", "q_act", "q_pool"]:
            self.sem[n] = nc.alloc_semaphore("s_" + n)
            self.cnt[n] = 0
        self.waited = {}
        self.lastw = {}
        self.readers = {}

    def _wait(self, eng, deps):
        best = {}
        for (s, v) in deps:
            if eng == "pe" and s == "pe":
                continue
            if best.get(s, 0) < v:
                best[s] = v
        for s, v in best.items():
            if s.startswith("q_"):
                v = self.cnt[s]
            if self.waited.get((eng, s), 0) < v:
                self.E[eng].wait_ge(self.sem[s], v)
                self.waited[(eng, s)] = v

    def op(self, eng, fn, reads=(), writes=(), dma=False):
        reads = list(dict.fromkeys(reads))
        writes = list(dict.fromkeys(list(writes) + [r for r in reads if isinstance(r, str) and r[0] == "P" and r[1:].isdigit()]))
        deps = set()
        for r in reads:
            if r in self.lastw:
                deps.add(self.lastw[r])
        for w in writes:
            if w in self.lastw:
                deps.add(self.lastw[w])
            for t in self.readers.get(w, ()):
                deps.add(t)
        self._wait(eng, deps)
        ins = fn()
        if dma:
            s = "q_" + eng
            self.cnt[s] += 16
            ins.then_inc(self.sem[s], 16)
        else:
            s = eng
            self.cnt[s] += 1
            ins.then_inc(self.sem[s], 1)
        tok = (s, self.cnt[s])
        for w in writes:
            self.lastw[w] = tok
            self.readers[w] = []
        for r in reads:
            if r in writes:
                continue
            lst = self.readers.setdefault(r, [])
            lst.append(tok)
            if len(lst) > 48:
                best = {}
                for (s2, v2) in lst:
                    if best.get(s2, 0) < v2:
                        best[s2] = v2
                self.readers[r] = list(best.items())
        return ins

    def drain(self, eng="sp"):
        for s, v in self.cnt.items():
            if v > 0 and self.waited.get((eng, s), 0) < v:
                self.E[eng].wait_ge(self.sem[s], v)
                self.waited[(eng, s)] = v


def t5_buckets(dist):
    n_b, max_d = 32, 2048
    max_exact = n_b // 2
    n = np.maximum(dist, 1).astype(np.float32)
    large = max_exact + (np.log(n / max_exact) / np.log(max_d / max_exact) * (n_b - max_exact)).astype(np.int32)
    large = np.minimum(large, n_b - 1)
    return np.where(dist < max_exact, dist, large).astype(np.int32)


def host_consts():
    c = {}
    ohpad = np.zeros((3, 32, 384), np.float32)
    ohs = np.zeros((3, 32, 128), np.float32)
    for g in range(3):
        bk = t5_buckets(np.arange(129) * DILS[g])
        for rel in range(129):
            ohpad[g, bk[rel], 127 + rel] = 1.0
        for p in range(128):
            ohs[g, bk[128 - p], p] = 1.0
    maskpad = np.full((16, 384), NEG, np.float32)
    maskpad[:, 127:256] = 0.0
    selh = np.zeros((16, 16, 128), np.float32)
    for h in range(16):
        selh[h, h, :] = 1.0
    oh0 = np.zeros((32, NS), np.float32)
    oh0[0, :] = 1.0
    sel4 = np.zeros((NS, NS, 128), np.float32)
    for b in range(NS):
        sel4[b, b, :] = 1.0
    c["ohpad"] = ohpad
    c["ohs"] = ohs
    c["maskpad"] = maskpad
    c["selh"] = selh.reshape(16, 16 * 128)
    c["oh0"] = oh0
    c["sel4"] = sel4.reshape(NS, NS * 128)
    c["ident"] = np.eye(128, dtype=np.float32)
    pc = np.ones((3, 128, 16), np.float32)
    for gi, w in enumerate((2, 4, 8)):
        pass
    pcorr = np.ones((4, 128, 16), np.float32)
    for gi, w in enumerate((2, 4, 8, 16)):
        for t in range(16):
            pcorr[gi, :, t] = w / min(w, t + 1)
    c["pcorr"] = np.ascontiguousarray(pcorr.transpose(1, 0, 2)).reshape(128, 64)
    return c


def build():
    nc = bass.Bass("TRN2", target_bir_lowering=False)
    T = Trk(nc)
    _uid = [0]

    def chk(name):
        if DEBUG_STOP is not None and name == DEBUG_STOP:
            raise _Stop()

    def un(name):
        _uid[0] += 1
        return f"sb_{name}_{_uid[0]}"

    def din(name, shape):
        return nc.dram_tensor(name, list(shape), F32, kind="ExternalInput")

    def dout(name, shape):
        return nc.dram_tensor(name, list(shape), F32, kind="ExternalOutput")

    xT_in = din("xT", [D, NT]).ap()
    xsT_in = din("xsT", [D, NS]).ap()
    ck = [din(f"ck{g}", [2, NS, 128, D]).ap() for g in range(3)]
    cv = [din(f"cv{g}", [2, NS, 128, D]).ap() for g in range(3)]
    spoolT_in = din("spoolT", [2, 128, KC, NS, 15]).ap()
    spool_in = din("spool", [2, NS, 15, D]).ap()
    sconvT_in = din("sconvT", [4, 128, 2, FC, NS]).ap()
    sconv_in = din("sconv", [4, NS, 2, DFF]).ap()
    relb_in = din("rel_bias", [32, 48]).ap()
    pvec_in = din("pvec", [128, NPV]).ap()
    wqkv_in = din("w_qkv", [2, D, 9 * D]).ap()
    wo_in = din("w_o", [2, D, D]).ap()
    wpool_in = din("w_pool", [2, 4, 256, 256]).ap()
    win_in = din("w_in", [4, D, 2 * DFF]).ap()
    wout_in = din("w_out", [4, DFF, D]).ap()
    ohpad_in = din("ohpad", [3, 32, 384]).ap()
    ohs_in = din("ohs", [3, 32, 128]).ap()
    maskpad_in = din("maskpad", [16, 384]).ap()
    selh_in = din("selh", [16, 16 * 128]).ap()
    oh0_in = din("oh0", [32, NS]).ap()
    sel4_in = din("sel4", [NS, NS * 128]).ap()
    ident_in = din("ident", [128, 128]).ap()
    pcorr_in = din("pcorr", [128, 64]).ap()

    yT = dout("yT", [D, NT]).ap()
    ysT = dout("ysT", [D, NS]).ap()
    kpT = [dout(f"kp{g}T", [2, D, WINS[g]]).ap() for g in range(3)]
    vp = [dout(f"vp{g}", [2, WINS[g], D]).ap() for g in range(3)]
    qkvs_out = dout("qkvs", [2, NS, 9 * D]).ap()
    poolpT = dout("poolpT", [2, D, 15]).ap()
    poolsT = dout("poolsT", [2, D, NS]).ap()
    pools14 = dout("pools14", [2, NS, 14, D]).ap()
    convpT = dout("convpT", [4, DFF, 2]).ap()
    convsT = dout("convsT", [4, DFF, NS]).ap()
    convs0 = dout("convs0", [4, NS, DFF]).ap()

    xres = nc.dram_tensor("xres", [D, NT], F32).ap()
    scrb_h = nc.dram_tensor("scrb", [48, 128, 384], F32)
    scrb = scrb_h.ap()
    kh_scr = nc.dram_tensor("kh_scr", [24, 128, 16 * 128], BF16).ap()
    vh_scr = nc.dram_tensor("vh_scr", [24, 128, 16 * VW], BF16).ap()
    qkvs_scr = nc.dram_tensor("qkvs_scr", [NS, 9 * D], F32).ap()

    ps = [nc.alloc_psum_tensor(f"ps{i}", [128, 512], F32) for i in range(4)]
    psY = nc.alloc_psum_tensor("psY", [128, 2048], F32)

    def P(bank, lo=0, hi=512):
        return [f"P{bank}"]

    def PY(lo=0, hi=2048):
        out = []
        for bk in range(4):
            l2, h2 = max(lo, bk * 512), min(hi, (bk + 1) * 512)
            if l2 < h2:
                out += P(4 + bk, l2 - bk * 512, h2 - bk * 512)
        return out

    pv = nc.alloc_sbuf_tensor(un("pv"), [128, NPV], F32)
    meanm = nc.alloc_sbuf_tensor(un("meanm"), [128, 128], BF16)
    ones32 = nc.alloc_sbuf_tensor(un("ones32"), [128, 128], F32)
    identb = nc.alloc_sbuf_tensor(un("identb"), [128, 128], BF16)
    xs = nc.alloc_sbuf_tensor(un("xs"), [128, KC, NS], F32)
    SB = nc.alloc_sbuf_tensor(un("SB"), [128, 48], F32)
    B0 = nc.alloc_sbuf_tensor(un("B0"), [NS, 48], F32)
    sel4 = nc.alloc_sbuf_tensor(un("sel4"), [NS, NS * 128], F32)
    epsc = nc.alloc_sbuf_tensor(un("epsc"), [128, 1], F32)

    V = nc.vector
    A = nc.scalar
    G = nc.gpsimd
    PE = nc.tensor
    SP = nc.sync

    def dma(eng, out, in_, reads=(), writes=(), nonc=False):
        e = {"sp": SP, "pool": G, "act": A}[eng]
        if nonc:
            return T.op(eng, lambda: e.dma_start(out=out, in_=in_, allow_slow_non_contiguous=True), reads=reads, writes=writes, dma=True)
        return T.op(eng, lambda: e.dma_start(out=out, in_=in_), reads=reads, writes=writes, dma=True)

    def mm(out, pairs, reads, writes, first=True, last=True):
        def fn():
            ins = None
            n = len(pairs)
            for i, (l, r) in enumerate(pairs):
                ins = PE.matmul(out, lhsT=l, rhs=r, start=(first and i == 0), stop=(last and i == n - 1))
            return ins
        return T.op("pe", fn, reads=reads, writes=writes)

    dma("sp", pv[:], pvec_in[:, :], writes=["pv"])
    dma("sp", xs[:], xsT_in.rearrange("(kc p) n -> p kc n", p=128), writes=["xs"])
    dma("sp", sel4[:], sel4_in[:, :], writes=["sel4"])
    dma("pool", identb[:], ident_in[:, :], writes=["identb"])
    T.op("dve", lambda: V.memset(meanm[:], 1.0 / D), writes=["meanm"])
    T.op("dve", lambda: V.memset(ones32[:], 1.0), writes=["ones32"])
    T.op("dve", lambda: V.memset(epsc[:], EPS), writes=["epsc"])

    with ExitStack() as es:
        rb = es.enter_context(nc.sbuf_tensor(un("rb"), [32, 48], F32))
        ohp = es.enter_context(nc.sbuf_tensor(un("ohp"), [32, 3, 384], F32))
        ohs = es.enter_context(nc.sbuf_tensor(un("ohs"), [32, 3, 128], F32))
        oh0 = es.enter_context(nc.sbuf_tensor(un("oh0"), [32, NS], F32))
        mpad = es.enter_context(nc.sbuf_tensor(un("mpad"), [16, 384], F32))
        selh = es.enter_context(nc.sbuf_tensor(un("selh"), [16, 16 * 128], F32))
        fpad = es.enter_context(nc.sbuf_tensor(un("fpad"), [16, 384], F32))
        rep = [es.enter_context(nc.sbuf_tensor(un(f"rep{i}"), [128, 384], F32)) for i in range(2)]
        dma("sp", rb[:], relb_in[:, :], writes=["rb"])
        dma("sp", ohp[:], ohpad_in.rearrange("g k x -> k g x"), writes=["ohp"])
        dma("sp", ohs[:], ohs_in.rearrange("g k x -> k g x"), writes=["ohs"])
        dma("sp", oh0[:], oh0_in[:, :], writes=["oh0"])
        dma("sp", mpad[:], maskpad_in[:, :], writes=["mpad"])
        dma("sp", selh[:], selh_in[:, :], writes=["selh"])
        mm(ps[0][0:NS, 0:48], [(oh0[:], rb[:])], ["oh0", "rb"], P(0))
        T.op("dve", lambda: V.tensor_copy(out=B0[:], in_=ps[0][0:NS, 0:48]), reads=P(0), writes=["B0"])
        for g in range(3):
            mm(ps[1][:, g * 16:(g + 1) * 16], [(ohs[:, g, :], rb[:, g * 16:(g + 1) * 16])], ["ohs", "rb"], P(1))
        T.op("dve", lambda: V.tensor_copy(out=SB[:], in_=ps[1][:, 0:48]), reads=P(1), writes=["SB"])
        for g in range(3):
            mm(ps[2][0:16, 0:384], [(rb[:, g * 16:(g + 1) * 16], ohp[:, g, :])], ["rb", "ohp"], P(2))
            T.op("dve", lambda: V.scalar_tensor_tensor(out=fpad[:], in0=ps[2][0:16, 0:384], scalar=8.0, in1=mpad[:],
                                                       op0=ALU.mult, op1=ALU.add), reads=P(2) + ["mpad"], writes=["fpad"])
            for h in range(16):
                i = h % 2
                mm(ps[i][:, 0:384], [(selh[:, h * 128:(h + 1) * 128], fpad[:])], ["selh", "fpad"], P(i))
                T.op("act", lambda: A.copy(out=rep[i][:], in_=ps[i][:, 0:384]), reads=P(i), writes=[f"rep{i}"])
                dma("sp", scrb[g * 16 + h], rep[i][:], reads=[f"rep{i}"], writes=["scrb"])
    T.drain("sp")
    nc.all_engine_barrier()
    try:
        chk("setup")
    except _Stop:
        return nc

    def src_x(l):
        return xT_in if l == 0 else xres

    def x_view(ap, t0, n):
        return ap.rearrange("(kc p) t -> p kc t", p=128)[:, :, t0:t0 + n]

    def rms_rstd(xt_ap, n, sq, rstd, psm, xname, tag):
        pn_ = P(0, 0, n)
        T.op("act", lambda: A.activation(out=sq, in_=xt_ap, func=AF.Square), reads=[xname], writes=["sq" + tag])
        mm(psm, [(meanm[:], sq[:, kc, :]) for kc in range(KC)], ["meanm", "sq" + tag], pn_)
        T.op("act", lambda: A.activation(out=rstd, in_=psm, func=AF.Sqrt, bias=epsc[:, 0:1]), reads=pn_ + ["epsc"], writes=["rstd" + tag])
        T.op("dve", lambda: V.reciprocal(out=rstd, in_=rstd), reads=["rstd" + tag], writes=["rstd" + tag])

    def rms_sq(xt_ap, n, sq, xname, tag):
        T.op("act", lambda: A.activation(out=sq, in_=xt_ap, func=AF.Square), reads=[xname], writes=["sq" + tag])

    def rms_fin(n, sq, rstd, psm, tag):
        pn_ = P(0, 0, n)
        mm(psm, [(meanm[:], sq[:, kc, :]) for kc in range(KC)], ["meanm", "sq" + tag], pn_)
        T.op("act", lambda: A.activation(out=rstd, in_=psm, func=AF.Sqrt, bias=epsc[:, 0:1]), reads=pn_ + ["epsc"], writes=["rstd" + tag])
        T.op("dve", lambda: V.reciprocal(out=rstd, in_=rstd), reads=["rstd" + tag], writes=["rstd" + tag])

    def attention_pass(l):
        a = l // 2
        src = src_x(l)
        with ExitStack() as es:
            sb = lambda name, shape, dt: es.enter_context(nc.sbuf_tensor(un(name), shape, dt))
            hT = sb("hT", [128, KC, ST], BF16)
            attnT = sb("attnT", [128, KC, ST], BF16)
            BT = sb("BT", [128, 48, 256], BF16)
            hsT = sb("hsT", [128, KC, NS], BF16)
            sqs = sb("sqs", [128, KC, NS], BF16)
            rstds = sb("rstds", [128, NS], F32)

            for g in range(3):
                skew = bass.AP(scrb_h, 127 + g * 16 * 128 * 384, [[383, 128], [128 * 384, 16], [1, 256]])
                dma("pool", BT[:, g * 16:(g + 1) * 16, :], skew, reads=["scrb"], writes=["BT"])

            rms_rstd(xs[:], NS, sqs[:], rstds[:], ps[0][:, 0:NS], "xs", "s")
            for kc in range(KC):
                T.op("dve", lambda: V.scalar_tensor_tensor(out=hsT[:, kc, :], in0=xs[:, kc, :], scalar=pv[:, PV_NM + l * 8 + kc:PV_NM + l * 8 + kc + 1],
                                                           in1=rstds[:], op0=ALU.mult, op1=ALU.mult), reads=["xs", "pv", "rstds"], writes=["hsT"])
            it = 0
            for st in range(NST):
                base = st * ST
                with ExitStack() as esA:
                    sbA = lambda name, shape, dt: esA.enter_context(nc.sbuf_tensor(un(name), shape, dt))
                    xt = sbA("xt_a", [128, KC, 512], F32)
                    sq = sbA("sq_a", [128, KC, 512], BF16)
                    rstd = sbA("rstd_a", [128, 512], F32)
                    for tt in range(4):
                        t0 = base + tt * 512
                        dma("sp", xt[:], x_view(src, t0, 512), reads=[f"xr{t0 // 256}", f"xr{t0 // 256 + 1}"], writes=["xt_a"])
                        rms_rstd(xt[:], 512, sq[:], rstd[:], ps[0][:, :], "xt_a", "a")
                        for kc in range(KC):
                            T.op("dve", lambda: V.scalar_tensor_tensor(out=hT[:, kc, tt * 512:(tt + 1) * 512], in0=xt[:, kc, :],
                                                                       scalar=pv[:, PV_NM + l * 8 + kc:PV_NM + l * 8 + kc + 1], in1=rstd[:],
                                                                       op0=ALU.mult, op1=ALU.mult),
                                 reads=["xt_a", "pv", "rstda"], writes=[f"hT{tt}"])
                    T.drain("sp")
                    nc.all_engine_barrier()
                chk(f"A{l}{st}")
                hT_all = [f"hT{tt}" for tt in range(4)]
                with ExitStack() as esB:
                    sbB = lambda name, shape, dt: esB.enter_context(nc.sbuf_tensor(un(name), shape, dt))
                    wq = [sbB(f"wq{i}", [128, KC, 128], BF16) for i in range(2)]
                    wk = [sbB(f"wk{i}", [128, KC, 128], BF16) for i in range(2)]
                    wv = [sbB(f"wv{i}", [128, KC, 128], BF16) for i in range(2)]
                    Qs = [sbB(f"Qs{i}", [128, 2, ST], BF16) for i in range(2)]
                    Ks = [sbB(f"Ks{i}", [128, 16 * 128 + ST], BF16) for i in range(2)]
                    Vx = [sbB(f"Vx{i}", [128, 32, VW], BF16) for i in range(2)]
                    pt = [sbB("pt0", [128, 17, 256], BF16), sbB("pt1", [128, 8, 256], BF16)]
                    accn = sbB("accn", [128, ST], F32)
                    accd = sbB("accd", [1, 2, ST], F32)
                    kst = [sbB("kst0", [128, 512], F32)] * 2
                    vst = [sbB("vst0", [128, 512], F32)] * 2
                    sst = [sbB(f"sst{i}", [NS, 384], F32) for i in range(2)]
                    for i in range(2):
                        T.op("pool", lambda: G.memset(Qs[i][:], 0.0), writes=[f"Qs{i}"])
                    for hp in range(8):
                        for g in range(3):
                            d = DILS[g]
                            W = WINS[g]
                            S = ST // d
                            nb = S // 128
                            par = it % 2
                            it += 1
                            slot = g * 8 + hp
                            wts = (wq[par], wk[par], wv[par])
                            wn = (f"wq{par}", f"wk{par}", f"wv{par}")
                            for s in range(3):
                                c0 = (g * 3 + s) * D + hp * 128
                                dma("pool", wts[s][:], wqkv_in[a].rearrange("(kc p) n -> p kc n", p=128)[:, :, c0:c0 + 128], writes=[wn[s]])
                            Qn, Kn, Vn = f"Qs{par}", f"Ks{par}", f"Vx{par}"
                            Ksv = Ks[par][:, 0:d * (128 + S)].rearrange("p (r s) -> p r s", r=d)
                            Qsv = [Qs[par][:, e_, :].rearrange("p (r s) -> p r s", r=d) for e_ in range(2)]
                            NH = d
                            if st == 0:
                                T.op("pool", lambda: G.memset(Ksv[:, :, 0:128], 0.0), writes=[Kn])
                                T.op("pool", lambda: G.memset(Vx[par][:, 0:NH, :], 0.0), writes=[Vn])
                            else:
                                dma("sp", Ksv[:, :, 0:128], kh_scr[slot, :, 0:d * 128].rearrange("p (r s) -> p r s", r=d), reads=[f"kh{slot}"], writes=[Kn])
                                dma("sp", Vx[par][:, 0:NH, :], vh_scr[slot, :, 0:d * VW].rearrange("p (r s) -> p r s", r=d), reads=[f"vh{slot}"], writes=[Vn])
                            T.op("pool", lambda: G.memset(Vx[par][:, NH:NH + 16, 128:VW], 1.0), writes=[Vn])
                            chk(f"p1{l}{st}{hp}{g}")
                            if st == 0:
                                sp_ = slot % 2
                                for s in range(3):
                                    mm(ps[3][0:NS, s * 128:(s + 1) * 128], [(hsT[:, kc, :], wts[s][:, kc, :]) for kc in range(KC)], ["hsT", wn[s]], P(3))
                                T.op("act", lambda: A.copy(out=sst[sp_][:], in_=ps[3][0:NS, 0:384]), reads=P(3), writes=[f"sst{sp_}"])
                                for s in range(3):
                                    c0 = (g * 3 + s) * D + hp * 128
                                    dma("sp", qkvs_scr[:, c0:c0 + 128], sst[sp_][:, s * 128:(s + 1) * 128], reads=[f"sst{sp_}"], writes=["qkvs_scr"])
                                    dma("sp", qkvs_out[a, :, c0:c0 + 128], sst[sp_][:, s * 128:(s + 1) * 128], reads=[f"sst{sp_}"])
                            chk(f"p2{l}{st}{hp}{g}")
                            for tt in range(4):
                                t0 = base + tt * 512
                                sl = 512 // d
                                pq, pk = ps[2 * (tt % 2)], ps[1 + 2 * (tt % 2)]
                                PQ, PK = P(2 * (tt % 2)), P(1 + 2 * (tt % 2))
                                mm(pq[:, :], [(wq[par][:, kc, :], hT[:, kc, tt * 512:(tt + 1) * 512]) for kc in range(KC)], [wn[0], f"hT{tt}"], PQ)
                                for e_ in range(2):
                                    T.op("act", lambda: A.copy(out=Qsv[e_][64 * e_:64 * e_ + 64, :, tt * sl:(tt + 1) * sl],
                                                               in_=pq[64 * e_:64 * e_ + 64, :].rearrange("p (s r) -> p r s", r=d)),
                                         reads=PQ, writes=[Qn])
                                mm(pk[:, :], [(wk[par][:, kc, :], hT[:, kc, tt * 512:(tt + 1) * 512]) for kc in range(KC)], [wn[1], f"hT{tt}"], PK)
                                T.op("dve", lambda: V.tensor_copy(out=Ksv[:, :, 128 + tt * sl:128 + (tt + 1) * sl], in_=pk[:, :].rearrange("p (s r) -> p r s", r=d)),
                                     reads=PK, writes=[Kn])
                                klo = max(t0, NT - W)
                                if klo < t0 + 512:
                                    kp_ = 0
                                    T.op("act", lambda: A.copy(out=kst[kp_][:], in_=pk[:, :]), reads=PK, writes=[f"kst{kp_}"])
                                    dma("sp", kpT[g][a, hp * 128:(hp + 1) * 128, klo - (NT - W):t0 + 512 - (NT - W)], kst[kp_][:, klo - t0:512],
                                        reads=[f"kst{kp_}"])
                            chk(f"p3{l}{st}{hp}{g}")
                            blocks = [(r, j) for r in range(d) for j in range(nb)]
                            for b4 in range(4):
                                pvb = ps[2 + (b4 % 2)]
                                pvn = P(2 + (b4 % 2))
                                for q in range(4):
                                    r, j = blocks[b4 * 4 + q]
                                    off = 128 * j * d + r
                                    mm(pvb[:, q * 128:(q + 1) * 128],
                                       [(hT[:, kc, off:off + 127 * d + 1:d], wv[par][:, kc, :]) for kc in range(KC)], [wn[2]] + hT_all, pvn)
                                T.op("act", lambda: A.copy(out=Vx[par][:, NH + b4 * 4:NH + b4 * 4 + 4, 0:128], in_=pvb[:, :].rearrange("p (q c) -> p q c", q=4)),
                                     reads=pvn, writes=[Vn])
                                keep = [q for q in range(4) if base + 128 * blocks[b4 * 4 + q][1] * d + blocks[b4 * 4 + q][0] >= NT - W]
                                if keep:
                                    vp_ = 0
                                    T.op("dve", lambda: V.tensor_copy(out=vst[vp_][:], in_=pvb[:, :]), reads=pvn, writes=[f"vst{vp_}"])
                                    for q in keep:
                                        r, j = blocks[b4 * 4 + q]
                                        row0 = base + 128 * j * d + r - (NT - W)
                                        dst = vp[g][a, row0:row0 + 127 * d + 1:d, hp * 128:(hp + 1) * 128]
                                        dma("sp", dst, vst[vp_][:, q * 128:(q + 1) * 128], reads=[f"vst{vp_}"])
                            chk(f"p4{l}{st}{hp}{g}")
                            if st + 1 < NST:
                                if True:
                                    dma("sp", kh_scr[slot, :, 0:d * 128].rearrange("p (r s) -> p r s", r=d), Ksv[:, :, S:S + 128], reads=[Kn], writes=[f"kh{slot}"])
                                if True:
                                  dma("sp", vh_scr[slot, :, 0:d * VW].rearrange("p (r s) -> p r s", r=d),
                                    Vx[par][:, NH:NH + 16, :].rearrange("p (r j) c -> p r j c", j=nb)[:, :, nb - 1, :], reads=[Vn], writes=[f"vh{slot}"])
                            chk(f"proj{l}{st}{hp}{g}")
                            for e in range(2):
                                h = 2 * hp + e
                                pe0 = 64 * e
                                if g == 0:
                                    bundles = [[(0, qb) for qb in range(4 * m, 4 * m + 4)] for m in range(4)]
                                elif g == 1:
                                    bundles = [[(r, qb) for qb in range(4)] for r in range(4)]
                                else:
                                    bundles = [[(r, 0) for r in range(4 * m, 4 * m + 4)] for m in range(4)]
                                done_streams = {}
                                sslot = 0
                                prev_b = None
                                for bi, bun in enumerate(bundles):
                                    for (r, qb) in bun:
                                        if r in done_streams:
                                            continue
                                        if g == 0:
                                            ptp, sbase = 0, 0
                                        elif g == 1:
                                            ptp, sbase = (r % 2), 0
                                            if ptp == 0:
                                                sbase = 0
                                        else:
                                            ptp, sbase = bi % 2, (r % 4) * 2
                                        if g == 1:
                                            pass
                                        done_streams[r] = (ptp, sbase)
                                        for kb in range(-1, nb):
                                            q_lo = max(kb, 0)
                                            q_hi = min(kb + 2, nb)
                                            N = (q_hi - q_lo) * 128
                                            bc0 = 128 if kb == -1 else 0
                                            ss = sslot % 4
                                            sslot += 1
                                            pso = psY[:, ss * 512:ss * 512 + N]
                                            pname = PY(ss * 512, ss * 512 + N)

                                            def sc_fn():
                                                PE.matmul(pso, lhsT=Ksv[:, r, 128 * (kb + 1):128 * (kb + 2)],
                                                          rhs=Qsv[e][:, r, 128 * q_lo:128 * q_hi], start=True, stop=False)
                                                return PE.matmul(pso, lhsT=identb[:], rhs=BT[:, g * 16 + h, bc0:bc0 + N], start=False, stop=True)
                                            T.op("pe", sc_fn, reads=[Kn, Qn, "identb", "BT"], writes=pname)
                                            T.op("act", lambda: A.activation(out=pt[ptp][:, sbase + kb + 1, 0:N], in_=pso, func=AF.Exp, scale=0.125),
                                                 reads=pname, writes=[f"pt{ptp}"])
                                    def pv_acc(bi, bun):
                                        pnum = ps[0 + 2 * (bi % 2)]
                                        pden = ps[1 + 2 * (bi % 2)]
                                        PN = P(0 + 2 * (bi % 2))
                                        PD = P(1 + 2 * (bi % 2))
                                        ptnames = set()
                                        for q, (r, qb) in enumerate(bun):
                                            ptp, sbase = done_streams[r]
                                            ptnames.add(f"pt{ptp}")
                                            blk_prev = r if qb == 0 else NH + r * nb + qb - 1
                                            blk_cur = NH + r * nb + qb
                                            c_prev = 0 if qb == 0 else 128
                                            rhs_prev = pt[ptp][:, sbase + qb, c_prev:c_prev + 128]
                                            rhs_cur = pt[ptp][:, sbase + qb + 1, 0:128]

                                            def pv_fn():
                                                PE.matmul(pnum[:, q * 128:(q + 1) * 128], lhsT=Vx[par][:, blk_prev, 0:128], rhs=rhs_prev, start=True, stop=False)
                                                PE.matmul(pnum[:, q * 128:(q + 1) * 128], lhsT=Vx[par][:, blk_cur, 0:128], rhs=rhs_cur, start=False, stop=True)
                                                PE.matmul(pden[0:1, q * 128:(q + 1) * 128], lhsT=Vx[par][:, blk_prev, 128:129], rhs=rhs_prev, start=True, stop=False)
                                                return PE.matmul(pden[0:1, q * 128:(q + 1) * 128], lhsT=Vx[par][:, blk_cur, 128:129], rhs=rhs_cur, start=False, stop=True)
                                            T.op("pe", pv_fn, reads=[Vn, f"pt{ptp}"], writes=PN + PD)
                                        if g == 0:
                                            an = accn[pe0:pe0 + 64, bi * 512:(bi + 1) * 512]
                                            ad = accd[0:1, e, bi * 512:(bi + 1) * 512]
                                            pn = pnum[pe0:pe0 + 64, :]
                                            pd = pden[0:1, :]
                                        elif g == 1:
                                            r = bun[0][0]
                                            an = accn[pe0:pe0 + 64, :].rearrange("p (qb i r) -> p r qb i", r=4, i=128)[:, r]
                                            ad = accd[0:1, e, :].rearrange("p (qb i r) -> p r qb i", r=4, i=128)[:, r]
                                            pn = pnum[pe0:pe0 + 64, :].rearrange("p (qb i) -> p qb i", i=128)
                                            pd = pden[0:1, :].rearrange("p (qb i) -> p qb i", i=128)
                                        else:
                                            r0 = bun[0][0]
                                            an = accn[pe0:pe0 + 64, :].rearrange("p (i r) -> p r i", r=16)[:, r0:r0 + 4]
                                            ad = accd[0:1, e, :].rearrange("p (i r) -> p r i", r=16)[:, r0:r0 + 4]
                                            pn = pnum[pe0:pe0 + 64, :].rearrange("p (q i) -> p q i", i=128)
                                            pd = pden[0:1, :].rearrange("p (q i) -> p q i", i=128)
                                        if g == 0:
                                            T.op("act", lambda: A.copy(out=an, in_=pn), reads=PN, writes=["accn"])
                                            T.op("dve", lambda: V.tensor_copy(out=ad, in_=pd), reads=PD, writes=["accd"])
                                        else:
                                            T.op("dve", lambda: V.tensor_tensor(out=an, in0=an, in1=pn, op=ALU.add), reads=PN + ["accn"], writes=["accn"])
                                            T.op("dve", lambda: V.tensor_tensor(out=ad, in0=ad, in1=pd, op=ALU.add), reads=PD + ["accd"], writes=["accd"])
                                    if prev_b is not None:
                                        pv_acc(*prev_b)
                                    prev_b = (bi, bun, pv_acc)[0:2]
                                if prev_b is not None:
                                    pv_acc(*prev_b)
                            chk(f"att{l}{st}{hp}{g}")
                        T.op("dve", lambda: V.reciprocal(out=accd[:], in_=accd[:]), reads=["accd"], writes=["accd"])
                        for e in range(2):
                            pe0 = 64 * e
                            for tt in range(4):
                                bk = ps[2 + tt % 2]
                                bn = P(2 + tt % 2)
                                mm(bk[:, :], [(ones32[0:1, :], accd[0:1, e, tt * 512:(tt + 1) * 512])], ["ones32", "accd"], bn)
                                T.op("dve", lambda: V.tensor_tensor(out=attnT[pe0:pe0 + 64, hp, tt * 512:(tt + 1) * 512], in0=accn[pe0:pe0 + 64, tt * 512:(tt + 1) * 512],
                                                                    in1=bk[pe0:pe0 + 64, :], op=ALU.mult), reads=bn + ["accn"], writes=["attnT"])
                    T.drain("sp")
                    nc.all_engine_barrier()
                chk(f"B{l}{st}")
                with ExitStack() as esC:
                    sbC = lambda name, shape, dt: esC.enter_context(nc.sbuf_tensor(un(name), shape, dt))
                    xt = sbC("xt_c", [128, KC, 512], F32)
                    wo = sbC("wo", [128, KC, D], BF16)
                    for c in range(KC):
                        dma("pool", wo[:, c, :], wo_in[a, c * 128:(c + 1) * 128, :], writes=["wo"])
                    for tt in range(4):
                        t0 = base + tt * 512
                        xr = [f"xr{t0 // 256}", f"xr{t0 // 256 + 1}"]
                        dma("sp", xt[:], x_view(src, t0, 512), reads=xr, writes=["xt_c"])
                        for ec in range(KC):
                            bk = ps[ec % 2]
                            bn = P(ec % 2)
                            mm(bk[:, :], [(wo[:, c, ec * 128:(ec + 1) * 128], attnT[:, c, tt * 512:(tt + 1) * 512]) for c in range(KC)], ["wo", "attnT"], bn)
                            T.op("dve", lambda: V.tensor_tensor(out=xt[:, ec, :], in0=xt[:, ec, :], in1=bk[:, :], op=ALU.add), reads=bn + ["xt_c"], writes=["xt_c"])
                        dma("sp", x_view(xres, t0, 512), xt[:], reads=["xt_c"], writes=xr)
                    T.drain("sp")
                    nc.all_engine_barrier()
        T.drain("sp")
        nc.all_engine_barrier()

    def sample_attention(l):
        a = l // 2
        with ExitStack() as es:
            sb = lambda name, shape, dt: es.enter_context(nc.sbuf_tensor(un(name), shape, dt))
            qs = sb("qs", [NS, 9 * D], F32)
            prod0 = sb("prod0", [NS, D], F32)
            sc0 = sb("sc0", [NS, 16], F32)
            pv0 = [sb(f"pv0_{g}", [NS, D + 16], F32) for g in range(3)]
            Kc = [sb(f"Kc{i}", [128, D], F32) for i in range(2)]
            Vc = [sb(f"Vc{i}", [128, D], F32) for i in range(2)]
            prod = sb("prod", [128, D], F32)
            sc = sb("sc", [128, 16], F32)
            pvc = [sb(f"pvc{i}", [128, D + 16], F32) for i in range(2)]
            rec = sb("rec", [1, 16], F32)
            arow = sb("arow", [1, D], F32)
            asT = sb("asT", [128, KC, NS], BF16)
            wo = sb("wo_s", [128, KC, D], BF16)
            for c in range(KC):
                dma("pool", wo[:, c, :], wo_in[a, c * 128:(c + 1) * 128, :], writes=["wo_s"])
            dma("sp", qs[:], qkvs_scr[:, :], reads=["qkvs_scr"], writes=["qs"])
            for g in range(3):
                qg = qs[:, (g * 3) * D:(g * 3 + 1) * D]
                kg = qs[:, (g * 3 + 1) * D:(g * 3 + 2) * D]
                vg = qs[:, (g * 3 + 2) * D:(g * 3 + 3) * D]
                T.op("dve", lambda: V.tensor_tensor(out=prod0[:], in0=qg, in1=kg, op=ALU.mult), reads=["qs"], writes=["prod0"])
                T.op("dve", lambda: V.tensor_reduce(out=sc0[:], in_=prod0[:].rearrange("p (h d) -> p h d", d=64), axis=AX.X, op=ALU.add),
                     reads=["prod0"], writes=["sc0"])
                T.op("dve", lambda: V.scalar_tensor_tensor(out=sc0[:], in0=sc0[:], scalar=0.125, in1=B0[:, g * 16:(g + 1) * 16], op0=ALU.mult, op1=ALU.add),
                     reads=["sc0", "B0"], writes=["sc0"])
                T.op("act", lambda: A.activation(out=pv0[g][:, D:D + 16], in_=sc0[:], func=AF.Exp), reads=["sc0"], writes=[f"pv0_{g}p"])
                T.op("dve", lambda: V.tensor_tensor(out=pv0[g][:, 0:D].rearrange("p (h d) -> p h d", d=64), in0=vg.rearrange("p (h d) -> p h d", d=64),
                                                    in1=pv0[g][:, D:D + 16].unsqueeze(2).broadcast_to([NS, 16, 64]), op=ALU.mult),
                     reads=["qs", f"pv0_{g}p"], writes=[f"pv0_{g}"])
            ci = 0
            for b in range(NS):
                pieces = [(0, 512), (512, 1024), (1024, 1040)]
                pouts = [ps[2][0:1, 0:512], ps[3][0:1, 0:512], psY[0:1, 0:16]]
                pnames = [P(2), P(3), PY(0, 16)]
                for g in range(3):
                    par = ci % 2
                    ci += 1
                    dma("sp", Kc[par][:], ck[g][a, b], writes=[f"Kc{par}"])
                    dma("sp", Vc[par][:], cv[g][a, b], writes=[f"Vc{par}"])
                    qg = qs[:, (g * 3) * D:(g * 3 + 1) * D]
                    mm(ps[0][:, :], [(sel4[:, b * 128:(b + 1) * 128], qg[:, 0:512])], ["sel4", "qs"], P(0))
                    mm(ps[1][:, :], [(sel4[:, b * 128:(b + 1) * 128], qg[:, 512:1024])], ["sel4", "qs"], P(1))
                    T.op("dve", lambda: V.tensor_tensor(out=prod[:, 0:512], in0=Kc[par][:, 0:512], in1=ps[0][:, :], op=ALU.mult), reads=[f"Kc{par}"] + P(0), writes=["prodA"])
                    T.op("dve", lambda: V.tensor_tensor(out=prod[:, 512:1024], in0=Kc[par][:, 512:1024], in1=ps[1][:, :], op=ALU.mult), reads=[f"Kc{par}"] + P(1), writes=["prodB"])
                    T.op("dve", lambda: V.tensor_reduce(out=sc[:], in_=prod[:].rearrange("p (h d) -> p h d", d=64), axis=AX.X, op=ALU.add),
                         reads=["prodA", "prodB"], writes=["sc"])
                    T.op("dve", lambda: V.scalar_tensor_tensor(out=sc[:], in0=sc[:], scalar=0.125, in1=SB[:, g * 16:(g + 1) * 16], op0=ALU.mult, op1=ALU.add),
                         reads=["sc", "SB"], writes=["sc"])
                    T.op("act", lambda: A.activation(out=pvc[par][:, D:D + 16], in_=sc[:], func=AF.Exp), reads=["sc"], writes=[f"pvc{par}p"])
                    T.op("dve", lambda: V.tensor_tensor(out=pvc[par][:, 0:D].rearrange("p (h d) -> p h d", d=64), in0=Vc[par][:].rearrange("p (h d) -> p h d", d=64),
                                                        in1=pvc[par][:, D:D + 16].unsqueeze(2).broadcast_to([128, 16, 64]), op=ALU.mult),
                         reads=[f"Vc{par}", f"pvc{par}p"], writes=[f"pvc{par}"])
                    for pi, (c0, c1) in enumerate(pieces):
                        def nd_fn():
                            PE.matmul(pouts[pi][:, 0:c1 - c0], lhsT=ones32[:, 0:1], rhs=pvc[par][:, c0:c1], start=(g == 0), stop=False)
                            return PE.matmul(pouts[pi][:, 0:c1 - c0], lhsT=sel4[:, b * 128:b * 128 + 1], rhs=pv0[g][:, c0:c1], start=False, stop=(g == 2))
                        T.op("pe", nd_fn, reads=["ones32", "sel4", f"pvc{par}", f"pvc{par}p", f"pv0_{g}", f"pv0_{g}p"], writes=pnames[pi])
                T.op("dve", lambda: V.reciprocal(out=rec[:], in_=psY[0:1, 0:16]), reads=PY(0, 16), writes=["rec"])
                for pi in range(2):
                    T.op("dve", lambda: V.tensor_tensor(out=arow[:, pi * 512:(pi + 1) * 512].rearrange("p (h d) -> p h d", d=64),
                                                        in0=pouts[pi].rearrange("p (h d) -> p h d", d=64),
                                                        in1=rec[:, pi * 8:(pi + 1) * 8].unsqueeze(2).broadcast_to([1, 8, 64]), op=ALU.mult),
                         reads=pnames[pi] + ["rec"], writes=["arow%d" % pi])
                for c in range(KC):
                    mm(psY[:, 512 + c:512 + c + 1], [(arow[0:1, c * 128:(c + 1) * 128], ones32[0:1, 0:1])], ["arow0", "arow1", "ones32"], PY(512, 520))
                T.op("dve", lambda: V.tensor_copy(out=asT[:, :, b], in_=psY[:, 512:512 + KC]), reads=PY(512, 520), writes=["asT"])
            for ec in range(KC):
                mm(psY[:, 1024 + ec * NS:1024 + (ec + 1) * NS], [(wo[:, c, ec * 128:(ec + 1) * 128], asT[:, c, :]) for c in range(KC)], ["wo_s", "asT"], PY(1024, 1056))
            T.op("dve", lambda: V.tensor_tensor(out=xs[:], in0=xs[:], in1=psY[:, 1024:1024 + KC * NS].rearrange("p (c n) -> p c n", n=NS), op=ALU.add),
                 reads=PY(1024, 1056) + ["xs"], writes=["xs"])
        T.drain("sp")
        nc.all_engine_barrier()

    def tile_pass(l):
        odd = (l % 2 == 1)
        pb = l // 2
        last = (l == 3)
        with ExitStack() as es:
            sb = lambda name, shape, dt: es.enter_context(nc.sbuf_tensor(un(name), shape, dt))
            win = sb("win", [128, KC, 2 * DFF], BF16)
            wout = sb("wout", [128, FC, D], BF16)
            xt = [sb(f"xt{i}", [128, KC, FT], F32) for i in range(2)]
            sq = sb("sq", [128, KC, FT], BF16)
            rstd = sb("rstd", [128, FT], F32)
            hb2 = [sb(f"hb{i}", [128, KC, FT], BF16) for i in range(2)]
            sq2 = sb("sq2", [128, KC, FT], BF16)
            rstd2 = sb("rstd2", [128, FT], F32)
            gext = [sb(f"gext{i}", [128, FT + 2], F32) for i in range(4)]
            cvt = [sb(f"cvt{i}", [128, FT], F32) for i in range(4)]
            actc = [sb(f"actc{i}", [128, FT], BF16) for i in range(4)]
            ghalo = sb("ghalo", [128, FC, 2], F32)
            pre = sb("pre", [128, 2, FC, NS], F32)
            gs = sb("gs", [128, FC, NS], F32)
            if odd:
                wpl = sb("wpl", [128, 2, 4, 256], BF16)
                hx = sb("hx", [128, KC, 16 + FT], F32)
                hxs = sb("hxs", [128, KC, NS, 16], F32)
                hso = sb("hso", [128, KC, NS], F32)
                s1 = sb("s1", [128, 16 + FT], F32)
                s2 = sb("s2", [128, 16 + FT], F32)
                zb = sb("zb", [128, KC, FT], BF16)
                pcorr = sb("pcorr", [128, 4, 16], F32)
                dma("sp", pcorr[:], pcorr_in.rearrange("p (g t) -> p g t", g=4), writes=["pcorr"])
                for g in range(4):
                    dma("pool", wpl[:, :, g, :], wpool_in[pb, g].rearrange("(kc p) n -> p kc n", p=128), writes=["wpl"])
                dma("sp", hxs[:, :, :, 0:15], spoolT_in[pb], writes=["hxs"])
                dma("sp", pools14[pb], spool_in[pb, :, 1:15, :], reads=[])
            for kc in range(KC):
                dma("pool", win[:, kc, :], win_in[l, kc * 128:(kc + 1) * 128, :], writes=["win"])
            for f in range(FC):
                dma("pool", wout[:, f, :], wout_in[l, f * 128:(f + 1) * 128, :], writes=["wout"])
            dma("sp", pre[:], sconvT_in[l], writes=["pre"])
            dma("sp", convs0[l], sconv_in[l, :, 1, :])
            T.op("pool", lambda: G.memset(ghalo[:], 0.0), writes=["ghalo"])
            if odd:
                T.op("pool", lambda: G.memset(hx[:, :, 0:16], 0.0), writes=["hx"])

            ntile = NT // FT
            fic = [0]

            def tile_ctx(ti):
                sample = (ti == ntile)
                n = NS if sample else FT
                t0 = ti * FT
                if sample:
                    return sample, n, t0, xs, "xs", xs[:]
                par = ti % 2
                return sample, n, t0, xt[par], f"xt{par}", xt[par][:]

            def prologue(ti):
                sample, n, t0, x, xn, xa = tile_ctx(ti)
                hb = hb2[ti % 2]
                hbn = f"hb{ti % 2}"
                if not sample:
                    dma("sp", xa, x_view(xres, t0, FT), reads=[f"xr{ti}"], writes=[xn])
                if odd:
                    rms_sq(xa, n, sq[:, :, 0:n], xn, "t")
                    yield
                    rms_fin(n, sq[:, :, 0:n], rstd[:, 0:n], ps[0][:, 0:n], "t")
                    if not sample:
                        hv = lambda c, lo, hi: hx[:, c, lo:hi]
                        for c in range(KC):
                            T.op("dve", lambda: V.scalar_tensor_tensor(out=hx[:, c, 16:16 + n], in0=x[:, c, :], scalar=pv[:, PV_NM + l * 8 + c:PV_NM + l * 8 + c + 1],
                                                                       in1=rstd[:, 0:n], op0=ALU.mult, op1=ALU.mult), reads=[xn, "pv", "rstdt"], writes=["hx"])
                        if ti == ntile - 1:
                            dma("sp", poolpT[pb].rearrange("(kc p) t -> p kc t", p=128), hx[:, :, 16 + FT - 15:16 + FT], reads=["hx"], nonc=True)
                    else:
                        for c in range(KC):
                            T.op("dve", lambda: V.scalar_tensor_tensor(out=hxs[:, c, :, 15], in0=x[:, c, :], scalar=pv[:, PV_NM + l * 8 + c:PV_NM + l * 8 + c + 1],
                                                                       in1=rstd[:, 0:n], op0=ALU.mult, op1=ALU.mult), reads=[xn, "pv", "rstdt"], writes=["hxs"])
                        T.op("pool", lambda: G.tensor_copy(out=hso[:], in_=hxs[:, :, :, 15]), reads=["hxs"], writes=["hso"])
                        dma("sp", poolsT[pb].rearrange("(kc p) n -> p kc n", p=128), hso[:], reads=["hso"], nonc=True)
                    yield
                    for c in range(KC):
                        grp = c // 2
                        w = (2, 4, 8, 16)[grp]
                        steps = grp + 1
                        if not sample:
                            L = 16 + FT
                            cur = lambda lo, hi: hx[:, c, lo:hi]
                            bufs = [lambda lo, hi: s1[:, lo:hi], lambda lo, hi: s2[:, lo:hi]]
                            fin = lambda vw: vw(16, L)
                            hcur = hx[:, c, 16:L]
                            zo = zb[:, c, 0:n]
                        else:
                            L = 16
                            cur = lambda lo, hi: hxs[:, c, :, lo:hi]
                            s1v = s1[:, 0:NS * 16].rearrange("p (b t) -> p b t", t=16)
                            s2v = s2[:, 0:NS * 16].rearrange("p (b t) -> p b t", t=16)
                            bufs = [lambda lo, hi: s1v[:, :, lo:hi], lambda lo, hi: s2v[:, :, lo:hi]]
                            fin = lambda vw: vw(15, 16)
                            hcur = hxs[:, c, :, 15:16]
                            zo = zb[:, c, 0:n].unsqueeze(2)
                        srcv = cur
                        srcn = "hxs" if sample else "hx"
                        sh = 1
                        for k in range(steps):
                            dst = bufs[k % 2]
                            dn = "s1" if k % 2 == 0 else "s2"
                            lo = 2 * sh - 1
                            sv, sn_ = srcv, srcn
                            T.op("dve", lambda: V.tensor_tensor(out=dst(lo, L), in0=sv(lo, L), in1=sv(lo - sh, L - sh), op=ALU.add), reads=[sn_], writes=[dn])
                            srcv, srcn = dst, dn
                            sh *= 2
                        fv = fin(srcv)
                        if (not sample) and ti == 0:
                            sv = srcv
                            T.op("dve", lambda: V.tensor_tensor(out=sv(16, 32), in0=sv(16, 32), in1=pcorr[:, grp, :], op=ALU.mult), reads=[srcn, "pcorr"], writes=[srcn])
                        T.op("dve", lambda: V.scalar_tensor_tensor(out=zo, in0=fv, scalar=1.0 / w, in1=hcur, op0=ALU.mult, op1=ALU.subtract),
                             reads=[srcn, "hxs" if sample else "hx"], writes=["zb"])
                    yield
                    for grp in range(4):
                        for eh in range(2):
                            ec = grp * 2 + eh
                            bk = ps[1 + ec % 2]
                            bn = P(1 + ec % 2, 0, n)
                            mm(bk[:, 0:n], [(wpl[:, kc2, grp, eh * 128:(eh + 1) * 128], zb[:, grp * 2 + kc2, 0:n]) for kc2 in range(2)], ["wpl", "zb"], bn)
                            T.op("dve", lambda: V.scalar_tensor_tensor(out=x[:, ec, :], in0=bk[:, 0:n], scalar=pv[:, PV_PSC + pb * 8 + ec:PV_PSC + pb * 8 + ec + 1],
                                                                       in1=x[:, ec, :], op0=ALU.mult, op1=ALU.add), reads=bn + ["pv", xn], writes=[xn])
                    if not sample:
                        T.op("pool", lambda: G.tensor_copy(out=hx[:, :, 0:16], in_=hx[:, :, FT:FT + 16]), reads=["hx"], writes=["hx"])
                if odd:
                    yield
                rms_sq(xa, n, sq[:, :, 0:n], xn, "t")
                yield
                rms_fin(n, sq[:, :, 0:n], rstd[:, 0:n], ps[0][:, 0:n], "t")
                for kc in range(KC):
                    T.op("dve", lambda: V.scalar_tensor_tensor(out=hb[:, kc, 0:n], in0=x[:, kc, :], scalar=pv[:, PV_NF + l * 8 + kc:PV_NF + l * 8 + kc + 1],
                                                               in1=rstd[:, 0:n], op0=ALU.mult, op1=ALU.mult), reads=[xn, "pv", "rstdt"], writes=[hbn])

            def body(ti, nxt):
                sample, n, t0, x, xn, xa = tile_ctx(ti)
                hb = hb2[ti % 2]
                hbn = f"hb{ti % 2}"

                def emit_y(f, p2, an):
                    def y_fn():
                        ins = None
                        for ec in range(KC):
                            ins = PE.matmul(psY[:, ec * 256:ec * 256 + n], lhsT=wout[:, f, ec * 128:(ec + 1) * 128], rhs=actc[p2][:, 0:n], start=(f == 0 and ec % 2 == 0), stop=(f == FC - 1))
                        return ins
                    T.op("pe", y_fn, reads=["wout", an], writes=PY())
                pend = []
                for f in range(FC):
                    p2 = fic[0] % 4
                    fic[0] += 1
                    bk = ps[p2]
                    bng = P(p2)
                    bnv = P(p2)

                    def gv_fn():
                        for kc in range(KC):
                            PE.matmul(bk[:, 0:n], lhsT=win[:, kc, f * 128:(f + 1) * 128], rhs=hb[:, kc, 0:n], start=(kc == 0), stop=(kc == KC - 1))
                        ins = None
                        for kc in range(KC):
                            ins = PE.matmul(bk[:, 256:256 + n], lhsT=win[:, kc, DFF + f * 128:DFF + (f + 1) * 128], rhs=hb[:, kc, 0:n], start=(kc == 0), stop=(kc == KC - 1))
                        return ins
                    T.op("pe", gv_fn, reads=["win", hbn], writes=bng)
                    cw = lambda j: pv[:, PV_CW + (l * 3 + j) * FC + f:PV_CW + (l * 3 + j) * FC + f + 1]
                    cb = pv[:, PV_CB + l * FC + f:PV_CB + l * FC + f + 1]
                    cvn = f"cvt{p2}"
                    T.op("act", lambda: A.activation(out=cvt[p2][:, 0:n], in_=bk[:, 0:n], func=AF.Identity, bias=cb, scale=cw(2)), reads=bng + ["pv"], writes=[cvn])
                    if not sample:
                        gn = f"gext{p2}"
                        T.op("pool", lambda: G.tensor_copy(out=gext[p2][:, 0:2], in_=ghalo[:, f, :]), reads=["ghalo"], writes=[gn + "h"])
                        T.op("act", lambda: A.copy(out=gext[p2][:, 2:2 + n], in_=bk[:, 0:n]), reads=bng, writes=[gn])
                        T.op("pool", lambda: G.tensor_copy(out=ghalo[:, f, :], in_=gext[p2][:, n:n + 2]), reads=[gn], writes=["ghalo"])
                        T.op("dve", lambda: V.scalar_tensor_tensor(out=cvt[p2][:, 0:n], in0=gext[p2][:, 1:1 + n], scalar=cw(1), in1=cvt[p2][:, 0:n], op0=ALU.mult, op1=ALU.add),
                             reads=[gn, gn + "h", "pv", cvn], writes=[cvn])
                        T.op("dve", lambda: V.scalar_tensor_tensor(out=cvt[p2][:, 0:n], in0=gext[p2][:, 0:n], scalar=cw(0), in1=cvt[p2][:, 0:n], op0=ALU.mult, op1=ALU.add),
                             reads=[gn, gn + "h", "pv", cvn], writes=[cvn])
                    else:
                        T.op("act", lambda: A.copy(out=gs[:, f, :], in_=bk[:, 0:n]), reads=bng, writes=["gs"])
                        T.op("dve", lambda: V.scalar_tensor_tensor(out=cvt[p2][:, 0:n], in0=pre[:, 1, f, :], scalar=cw(1), in1=cvt[p2][:, 0:n], op0=ALU.mult, op1=ALU.add),
                             reads=["pre", "pv", cvn], writes=[cvn])
                        T.op("dve", lambda: V.scalar_tensor_tensor(out=cvt[p2][:, 0:n], in0=pre[:, 0, f, :], scalar=cw(0), in1=cvt[p2][:, 0:n], op0=ALU.mult, op1=ALU.add),
                             reads=["pre", "pv", cvn], writes=[cvn])
                    T.op("act", lambda: A.activation(out=cvt[p2][:, 0:n], in_=cvt[p2][:, 0:n], func=AF.Silu), reads=[cvn], writes=[cvn])
                    an = f"actc{p2}"
                    T.op("dve", lambda: V.tensor_tensor(out=actc[p2][:, 0:n], in0=cvt[p2][:, 0:n], in1=bk[:, 256:256 + n], op=ALU.mult), reads=[cvn] + bnv, writes=[an])

                    pend.append((f, p2, an))
                    if len(pend) > 2:
                        emit_y(*pend.pop(0))
                    if f == 1 and nxt is not None:
                        gen[0] = prologue(nxt)
                    if gen[0] is not None and f in (2, 5, 8, 11, 14, 17):
                        next(gen[0], None)
                while pend:
                    emit_y(*pend.pop(0))
                if gen[0] is not None:
                    for _ in gen[0]:
                        pass
                    gen[0] = None
                yv = psY[:, :].rearrange("p (c t) -> p c t", t=256)[:, :, 0:n]
                T.op("dve", lambda: V.tensor_tensor(out=xa, in0=xa, in1=yv, op=ALU.add), reads=PY() + [xn], writes=[xn])
                if not last:
                    if not sample:
                        dma("sp", x_view(xres, t0, FT), xa, reads=[xn], writes=[f"xr{ti}"])
                else:
                    rms_rstd(xa, n, sq2[:, :, 0:n], rstd2[:, 0:n], ps[0][:, 0:n], xn, "u")
                    for kc in range(KC):
                        T.op("dve", lambda: V.scalar_tensor_tensor(out=x[:, kc, :], in0=x[:, kc, :], scalar=pv[:, PV_FIN + kc:PV_FIN + kc + 1],
                                                                   in1=rstd2[:, 0:n], op0=ALU.mult, op1=ALU.mult), reads=[xn, "pv", "rstdu"], writes=[xn])
                    if not sample:
                        dma("sp", x_view(yT, t0, FT), xa, reads=[xn])
                    else:
                        dma("sp", ysT.rearrange("(kc p) n -> p kc n", p=128), xa, reads=[xn], nonc=True)
                if (not sample) and ti == ntile - 1:
                    dma("sp", convpT[l].rearrange("(f p) j -> p f j", p=128), ghalo[:], reads=["ghalo"], nonc=True)
                if sample:
                    dma("sp", convsT[l].rearrange("(f p) n -> p f n", p=128), gs[:], reads=["gs"], nonc=True)
            gen = [None]
            for _ in prologue(0):
                pass
            for ti in range(ntile + 1):
                body(ti, ti + 1 if ti < ntile else None)
        T.drain("sp")
        nc.all_engine_barrier()

    try:
        for l in range(4):
            if l % 2 == 0:
                attention_pass(l)
                chk(f"attn{l}")
                sample_attention(l)
                chk(f"sattn{l}")
            tile_pass(l)
            chk(f"tile{l}")
    except _Stop:
        pass
    T.drain("sp")
    T.drain("act")
    return nc


_NC = None


def kernel(**inp):
    global _NC
    f32 = np.float32
    g = lambda k: np.asarray(inp[k], dtype=f32)
    x_prompt, x_sample = g("x_prompt"), g("x_sample")
    caches_k = [g("cache_k_w128"), g("cache_k_w512"), g("cache_k_w2048")]
    caches_v = [g("cache_v_w128"), g("cache_v_w512"), g("cache_v_w2048")]
    state_pool, state_conv = g("state_pool"), g("state_conv")
    norm_mix, norm_ffn, norm_final = g("norm_mix"), g("norm_ffn"), g("norm_final")
    pool_scale, conv_w, conv_b = g("pool_scale"), g("conv_w"), g("conv_b")

    fm = lambda v: np.ascontiguousarray(v.reshape(-1, 128).T)
    pvec = np.zeros((128, NPV), f32)
    for l in range(4):
        pvec[:, PV_NM + l * 8:PV_NM + (l + 1) * 8] = fm(norm_mix[l])
        pvec[:, PV_NF + l * 8:PV_NF + (l + 1) * 8] = fm(norm_ffn[l])
        pvec[:, PV_CB + l * FC:PV_CB + (l + 1) * FC] = fm(conv_b[l])
        for j in range(3):
            pvec[:, PV_CW + (l * 3 + j) * FC:PV_CW + (l * 3 + j + 1) * FC] = fm(conv_w[l, j])
    pvec[:, PV_FIN:PV_FIN + 8] = fm(norm_final)
    for b in range(2):
        pvec[:, PV_PSC + b * 8:PV_PSC + (b + 1) * 8] = fm(pool_scale[b])
    consts = host_consts()
    shared = dict(rel_bias=g("rel_bias"), pvec=pvec, w_qkv=g("w_qkv"), w_o=g("w_o"), w_pool=g("w_pool"),
                  w_in=g("w_in"), w_out=g("w_out"), **consts)
    in_maps = []
    for c in range(8):
        b = c % 4
        sl = slice(NS * c, NS * c + NS)
        m = dict(shared)
        m["xT"] = np.ascontiguousarray(x_prompt[b].T)
        m["xsT"] = np.ascontiguousarray(x_sample[sl, 0, :].T)
        for gi in range(3):
            dil = DILS[gi]
            m[f"ck{gi}"] = np.ascontiguousarray(caches_k[gi][:, sl, ::dil]).reshape(2, NS, 128, D)
            m[f"cv{gi}"] = np.ascontiguousarray(caches_v[gi][:, sl, ::dil]).reshape(2, NS, 128, D)
        sp = state_pool[:, sl]
        m["spool"] = np.ascontiguousarray(sp)
        m["spoolT"] = np.ascontiguousarray(sp.reshape(2, NS, 15, KC, 128).transpose(0, 4, 3, 1, 2))
        scv = state_conv[:, sl]
        m["sconv"] = np.ascontiguousarray(scv)
        m["sconvT"] = np.ascontiguousarray(scv.reshape(4, NS, 2, FC, 128).transpose(0, 4, 2, 3, 1))
        in_maps.append(m)
    if _NC is None:
        _NC = build()
    res = run_bass_kernel_spmd(_NC, in_maps, core_ids=list(range(8)))
    R = res.results
    B = 4
    y_prompt = np.stack([R[b]["yT"].T for b in range(B)]).astype(f32)
    y_sample = np.concatenate([R[c]["ysT"].T for c in range(8)], 0).reshape(32, 1, D).astype(f32)
    outs = [y_prompt, y_sample]
    for gi in range(3):
        kp = np.stack([R[b][f"kp{gi}T"].transpose(0, 2, 1) for b in range(B)], 1)
        vpp = np.stack([R[b][f"vp{gi}"] for b in range(B)], 1)
        outs.append(np.ascontiguousarray(kp).reshape(2, B, WINS[gi], 16, 64).astype(f32))
        outs.append(np.ascontiguousarray(vpp).reshape(2, B, WINS[gi], 16, 64).astype(f32))
    qk = np.concatenate([R[c]["qkvs"] for c in range(8)], 1)
    for gi in range(3):
        outs.append(np.ascontiguousarray(qk[:, :, (gi * 3 + 1) * D:(gi * 3 + 2) * D]).reshape(2, 32, 1, 16, 64).astype(f32))
        outs.append(np.ascontiguousarray(qk[:, :, (gi * 3 + 2) * D:(gi * 3 + 3) * D]).reshape(2, 32, 1, 16, 64).astype(f32))
    outs.append(np.stack([R[b]["poolpT"].transpose(0, 2, 1) for b in range(B)], 1).astype(f32))
    ps_ = np.zeros((2, 32, 15, D), f32)
    for c in range(8):
        ps_[:, NS * c:NS * c + NS, 0:14] = R[c]["pools14"]
        ps_[:, NS * c:NS * c + NS, 14] = R[c]["poolsT"].transpose(0, 2, 1)
    outs.append(ps_)
    outs.append(np.stack([R[b]["convpT"].transpose(0, 2, 1) for b in range(B)], 1).astype(f32))
    cs_ = np.zeros((4, 32, 2, DFF), f32)
    for c in range(8):
        cs_[:, NS * c:NS * c + NS, 0] = R[c]["convs0"]
        cs_[:, NS * c:NS * c + NS, 1] = R[c]["convsT"].transpose(0, 2, 1)
    outs.append(cs_)
    return tuple(outs)
```
